# Optimizing a Trainium2 kernel written in Bass

```python
import math
import functools
import jax
import jax.numpy as jnp
from jax import lax
import numpy as np

D_MODEL = 1024
BATCH = 16
SEQ = 2048
DEPTH = 1
DEC_BATCH = 32
DEC_SEQ = 8
PAST_LEN = 16384
PAGE_SIZE = 128

D_FF = 2816
D_INNER = 2 * D_MODEL
M_HEADDIM = 64
M_HEADS = D_INNER // M_HEADDIM
M_GROUPS = 4
M_HPG = M_HEADS // M_GROUPS
D_STATE = 128
D_CONV = 4
CONV_DIM = D_INNER + 2 * M_GROUPS * D_STATE
SSD_CHUNK = 128
HEAD_DIM = 64
Q_HEADS = D_MODEL // HEAD_DIM
KV_HEADS = 4
Q_PER_KV = Q_HEADS // KV_HEADS
D_ATTN = Q_HEADS * HEAD_DIM
CMP_BLOCK = 32
CMP_HIDDEN = 2 * HEAD_DIM
SEL_BLOCK = 64
SEL_TOPK = 16
WINDOW = 512
Q_BLOCK = 16
NUM_BUCKETS = 32
MAX_DISTANCE = 2048
EPS = 1e-6
PROJ_SIZES = (D_INNER, CONV_DIM, M_HEADS, D_ATTN, 6 * KV_HEADS * HEAD_DIM, 3 * Q_HEADS, 2 * D_MODEL)
D_PROJ = sum(PROJ_SIZES)

kernel_name = 'hybrid_ssd_nsa_macaron_step'


def rmsnorm(x, g):
    xf = x.astype(jnp.float32)
    y = xf * lax.rsqrt(jnp.mean(xf * xf, axis=-1, keepdims=True) + EPS)
    return (y * g.astype(jnp.float32)).astype(x.dtype)


def swiglu_ffn(x, w_in, w_out):
    gu = x @ w_in
    return (jax.nn.silu(gu[..., :D_FF]) * gu[..., D_FF:]) @ w_out


def rel_bucket(dist):
    max_exact = NUM_BUCKETS // 2
    d = jnp.maximum(dist, 0)
    df = jnp.maximum(d, 1).astype(jnp.float32)
    large = max_exact + (jnp.log(df / max_exact) / math.log(MAX_DISTANCE / max_exact)
                         * (NUM_BUCKETS - max_exact)).astype(jnp.int32)
    large = jnp.minimum(large, NUM_BUCKETS - 1)
    return jnp.where(d < max_exact, d, large)


def bias_qk(dist, bias_tab):
    return jnp.transpose(bias_tab[rel_bucket(dist)], (0, 2, 3, 1))[None]


def masked_softmax(s, mask):
    s = jnp.where(mask, s.astype(jnp.float32), -jnp.inf)
    m = jnp.max(s, axis=-1, keepdims=True)
    m = jnp.where(jnp.isfinite(m), m, 0.0)
    e = jnp.exp(s - m)
    return e / jnp.maximum(jnp.sum(e, axis=-1, keepdims=True), 1e-30)


def causal_conv(xbc, conv_state, w, b):
    L = xbc.shape[1]
    xp = jnp.concatenate([conv_state.astype(xbc.dtype), xbc], axis=1)
    y = b + sum(xp[:, k:k + L] * w[k] for k in range(D_CONV))
    return jax.nn.silu(y), xp[:, L:]


def segsum(a):
    l = a.shape[-1]
    cs = jnp.cumsum(a, axis=-1)
    diff = cs[..., :, None] - cs[..., None, :]
    return jnp.where(jnp.tril(jnp.ones((l, l), dtype=bool)), diff, -jnp.inf)


def ssd(x, dt, A, Bm, Cm, h0):
    b, L, G, J, P = x.shape
    N = Bm.shape[-1]
    cl = SSD_CHUNK if L % SSD_CHUNK == 0 else L
    nc = L // cl
    f32 = jnp.float32
    xdt = (x.astype(f32) * dt[..., None]).reshape(b, nc, cl, G, J, P)
    Bc = Bm.astype(f32).reshape(b, nc, cl, G, N)
    Cc = Cm.astype(f32).reshape(b, nc, cl, G, N)
    dA = jnp.moveaxis((dt * A).reshape(b, nc, cl, G, J), 2, -1)
    a_cs = jnp.cumsum(dA, axis=-1)
    decay_in = jnp.exp(segsum(dA))
    cb = jnp.einsum('bclgn,bcsgn->bcgls', Cc, Bc)
    y_diag = jnp.einsum('bcgjls,bcsgjp->bclgjp', cb[:, :, :, None] * decay_in, xdt)
    decay_to_end = jnp.moveaxis(jnp.exp(a_cs[..., -1:] - a_cs), -1, 2)
    states = jnp.einsum('bclgn,bclgjp->bcgjpn', Bc, xdt * decay_to_end[..., None])
    chunk_decay = jnp.exp(a_cs[..., -1])

    def step(h, inp):
        s_c, d_c = inp
        return h * d_c[..., None, None] + s_c, h

    h_final, h_start = lax.scan(step, h0, (jnp.moveaxis(states, 1, 0), jnp.moveaxis(chunk_decay, 1, 0)))
    h_start = jnp.moveaxis(h_start, 0, 1)
    decay_from_start = jnp.moveaxis(jnp.exp(a_cs), -1, 2)
    y_off = jnp.einsum('bclgn,bcgjpn->bclgjp', Cc, h_start) * decay_from_start[..., None]
    return (y_diag + y_off).reshape(b, L, G, J, P), h_final


def mamba_mixer(z, xbc, dt_raw, conv_state, ssm_state, conv_w, conv_b, dt_bias, a_log, d_skip, ssm_norm):
    b, L = z.shape[:2]
    f32 = jnp.float32
    xbc_c, new_conv = causal_conv(xbc, conv_state, conv_w, conv_b)
    xs = xbc_c[..., :D_INNER].reshape(b, L, M_GROUPS, M_HPG, M_HEADDIM)
    Bm = xbc_c[..., D_INNER:D_INNER + M_GROUPS * D_STATE].reshape(b, L, M_GROUPS, D_STATE)
    Cm = xbc_c[..., D_INNER + M_GROUPS * D_STATE:].reshape(b, L, M_GROUPS, D_STATE)
    dt = jax.nn.softplus(dt_raw.astype(f32) + dt_bias.astype(f32)).reshape(b, L, M_GROUPS, M_HPG)
    A = -jnp.exp(a_log.astype(f32)).reshape(M_GROUPS, M_HPG)
    h0 = ssm_state.astype(f32).reshape(b, M_GROUPS, M_HPG, M_HEADDIM, D_STATE)
    y, h_final = ssd(xs, dt, A, Bm, Cm, h0)
    y = y + d_skip.astype(f32).reshape(M_GROUPS, M_HPG)[:, :, None] * xs.astype(f32)
    y = y.reshape(b, L, D_INNER) * jax.nn.silu(z.astype(f32))
    yg = y.reshape(b, L, M_GROUPS, D_INNER // M_GROUPS)
    yg = yg * lax.rsqrt(jnp.mean(yg * yg, axis=-1, keepdims=True) + EPS)
    y = yg.reshape(b, L, D_INNER) * ssm_norm.astype(f32)
    new_ssm = h_final.reshape(b, M_HEADS, M_HEADDIM, D_STATE).astype(ssm_state.dtype)
    return y.astype(z.dtype), new_conv, new_ssm


def compress_rows(rows, pe, w1, w2):
    b, T = rows.shape[:2]
    n = T // CMP_BLOCK
    blk = rows[:, :n * CMP_BLOCK].reshape(b, n, CMP_BLOCK, KV_HEADS, HEAD_DIM) + pe[None, None, :, None, :]
    hid = jax.nn.silu(jnp.einsum('bnjhd,jdf->bnhf', blk, w1))
    return jnp.einsum('bnhf,fd->bnhd', hid, w2)


def sel_blocks(rows):
    b, T = rows.shape[:2]
    n_s = -(-T // SEL_BLOCK)
    rows = jnp.pad(rows, ((0, 0), (0, n_s * SEL_BLOCK - T), (0, 0), (0, 0)))
    return jnp.transpose(rows.reshape(b, n_s, SEL_BLOCK, KV_HEADS, HEAD_DIM), (0, 3, 1, 2, 4))


def nsa_block(q, q_pos, gates, kc, vc, kc_end, ksel_h, vsel_h, kw, vw, kw_pos, bias_tab):
    f32 = jnp.float32
    b, t = q.shape[:2]
    dist_c = q_pos[:, None] - kc_end[None, :]
    s_c = jnp.einsum('bthgd,bnhd->bthgn', q, kc, preferred_element_type=f32) + bias_qk(dist_c, bias_tab)
    p_c = masked_softmax(s_c, (dist_c >= 0)[None, :, None, None, :])
    o_c = jnp.einsum('bthgn,bnhd->bthgd', p_c, vc.astype(f32))
    n_s = ksel_h.shape[2]
    ratio = SEL_BLOCK // CMP_BLOCK
    imp = jnp.sum(p_c, axis=3)
    imp = jnp.pad(imp, ((0, 0), (0, 0), (0, 0), (0, n_s * ratio - imp.shape[-1])))
    imp = imp.reshape(b, t, KV_HEADS, n_s, ratio).sum(-1)
    blk = jnp.arange(n_s, dtype=jnp.int32)[None, :]
    cur = (q_pos // SEL_BLOCK)[:, None]
    forced = (blk == 0) | (blk == cur) | (blk == cur - 1)
    causal = blk * SEL_BLOCK <= q_pos[:, None]
    score = jnp.where(forced[None, :, None, :], jnp.inf,
                      jnp.where(causal[None, :, None, :], imp, -jnp.inf))
    _, idx = lax.top_k(score, min(SEL_TOPK, n_s))
    bi = jnp.arange(b)[:, None, None, None]
    hi = jnp.arange(KV_HEADS)[None, None, :, None]
    ks = ksel_h[bi, hi, idx]
    vs = vsel_h[bi, hi, idx]
    key_pos = idx[..., None] * SEL_BLOCK + jnp.arange(SEL_BLOCK, dtype=jnp.int32)
    dist_s = q_pos[None, :, None, None, None] - key_pos
    bias_s = jnp.moveaxis(bias_tab[rel_bucket(dist_s), hi[..., None]], -1, 3)
    s_s = jnp.einsum('bthgd,bthksd->bthgks', q, ks, preferred_element_type=f32) + bias_s
    shp = s_s.shape
    n_k = shp[-2] * shp[-1]
    p_s = masked_softmax(s_s.reshape(b, t, KV_HEADS, Q_PER_KV, n_k),
                         (dist_s >= 0).reshape(b, t, KV_HEADS, 1, n_k)).reshape(shp)
    o_s = jnp.einsum('bthgks,bthksd->bthgd', p_s, vs.astype(f32))
    dist_w = q_pos[:, None] - kw_pos[None, :]
    mask_w = (dist_w >= 0) & (dist_w <= WINDOW) & (kw_pos >= 0)[None, :]
    s_w = jnp.einsum('bthgd,bnhd->bthgn', q, kw, preferred_element_type=f32) + bias_qk(dist_w, bias_tab)
    p_w = masked_softmax(s_w, mask_w[None, :, None, None, :])
    o_w = jnp.einsum('bthgn,bnhd->bthgd', p_w, vw.astype(f32))
    return gates[..., 0:1] * o_c + gates[..., 1:2] * o_s + gates[..., 2:3] * o_w


def nsa_prompt(q, gates, paged_rows, win_rows, compress_kv, bias_tab):
    b, L = q.shape[:2]
    kc, vc = compress_kv(paged_rows[:, :, 0], paged_rows[:, :, 1])
    kc_end = jnp.arange(kc.shape[1], dtype=jnp.int32) * CMP_BLOCK + (CMP_BLOCK - 1)
    ksel_h = sel_blocks(paged_rows[:, :, 2])
    vsel_h = sel_blocks(paged_rows[:, :, 3])
    w_pad = jnp.pad(win_rows, ((0, 0), (WINDOW, 0), (0, 0), (0, 0), (0, 0)))
    w_pos = jnp.arange(-WINDOW, L, dtype=jnp.int32)
    qb = Q_BLOCK if L % Q_BLOCK == 0 else L

    def one_block(i):
        s = i * qb
        w_i = lax.dynamic_slice_in_dim(w_pad, s, qb + WINDOW, axis=1)
        return nsa_block(lax.dynamic_slice_in_dim(q, s, qb, axis=1),
                         s + jnp.arange(qb, dtype=jnp.int32),
                         lax.dynamic_slice_in_dim(gates, s, qb, axis=1),
                         kc, vc, kc_end, ksel_h, vsel_h, w_i[:, :, 0], w_i[:, :, 1],
                         lax.dynamic_slice_in_dim(w_pos, s, qb + WINDOW, axis=0), bias_tab)

    o = lax.map(one_block, jnp.arange(L // qb, dtype=jnp.int32))
    o = jnp.moveaxis(o, 0, 1).reshape(b, L, KV_HEADS, Q_PER_KV, HEAD_DIM)
    return o, win_rows[:, L - min(WINDOW, L):]


def nsa_sample(q, gates, paged_rows, win_rows, compress_kv, bias_tab, cache_kv_l, page_table, cache_win_l):
    b, L = q.shape[:2]
    past_len = page_table.shape[1] * cache_kv_l.shape[1]

    def rows_with_past(c):
        past = cache_kv_l[page_table, :, c].reshape(b, past_len, KV_HEADS, HEAD_DIM)
        return jnp.concatenate([past, paged_rows[:, :, c].astype(past.dtype)], axis=1)

    kc, vc = compress_kv(rows_with_past(0), rows_with_past(1))
    kc_end = jnp.arange(kc.shape[1], dtype=jnp.int32) * CMP_BLOCK + (CMP_BLOCK - 1)
    ksel_h = sel_blocks(rows_with_past(2))
    vsel_h = sel_blocks(rows_with_past(3))
    win_all = jnp.concatenate([cache_win_l, win_rows.astype(cache_win_l.dtype)], axis=1)
    n_buf = cache_win_l.shape[1]
    w_pos = jnp.arange(past_len - n_buf, past_len + L, dtype=jnp.int32)
    o = nsa_block(q, past_len + jnp.arange(L, dtype=jnp.int32), gates, kc, vc, kc_end, ksel_h, vsel_h,
                  win_all[:, :, 0], win_all[:, :, 1], w_pos, bias_tab)
    n_keep = min(WINDOW, n_buf + L)
    return o, win_all[:, n_buf + L - n_keep:]


def layer_forward(x, ssm_state, conv_state, nsa_fn, bias_tab, lw):
    (ffn1_norm, ffn1_w_in, ffn1_w_out, mix_norm, w_in_proj, conv_w, conv_b, dt_bias, a_log, d_skip,
     ssm_norm, qk_norm, cmp_pe, cmp_w1, cmp_w2, w_branch_ssm, w_branch_attn, w_out,
     ffn2_norm, ffn2_w_in, ffn2_w_out) = lw
    f32 = jnp.float32
    b, L = x.shape[:2]
    x = x + 0.5 * swiglu_ffn(rmsnorm(x, ffn1_norm), ffn1_w_in, ffn1_w_out)
    h = rmsnorm(x, mix_norm)
    cuts = [int(c) for c in np.cumsum(PROJ_SIZES)[:-1]]
    z, xbc, dt_raw, q, kv, nsa_g, merge_g = jnp.split(h @ w_in_proj, cuts, axis=-1)
    y_ssm, new_conv, new_ssm = mamba_mixer(z, xbc, dt_raw, conv_state, ssm_state, conv_w, conv_b,
                                           dt_bias, a_log, d_skip, ssm_norm)
    q = rmsnorm(q.reshape(b, L, KV_HEADS, Q_PER_KV, HEAD_DIM), qk_norm[0]) * (HEAD_DIM ** -0.5)
    kv = kv.reshape(b, L, 6, KV_HEADS, HEAD_DIM)
    paged_rows = jnp.stack([kv[:, :, 0], kv[:, :, 1], rmsnorm(kv[:, :, 2], qk_norm[2]), kv[:, :, 3]], axis=2)
    win_rows = jnp.stack([rmsnorm(kv[:, :, 4], qk_norm[3]), kv[:, :, 5]], axis=2)
    gates = jax.nn.sigmoid(nsa_g.astype(f32)).reshape(b, L, KV_HEADS, Q_PER_KV, 3)

    def compress_kv(k_rows, v_rows):
        kc = rmsnorm(compress_rows(k_rows, cmp_pe[0], cmp_w1[0], cmp_w2[0]), qk_norm[1])
        vc = compress_rows(v_rows, cmp_pe[1], cmp_w1[1], cmp_w2[1])
        return kc, vc

    o, new_win = nsa_fn(q, gates, paged_rows, win_rows, compress_kv, bias_tab)
    gm = jax.nn.sigmoid(merge_g.astype(f32))
    y_a = (y_ssm @ w_branch_ssm).astype(f32)
    y_b = (o.reshape(b, L, D_ATTN).astype(x.dtype) @ w_branch_attn).astype(f32)
    mix = (gm[..., :D_MODEL] * y_a + gm[..., D_MODEL:] * y_b).astype(x.dtype) @ w_out
    x = x + mix
    x = x + 0.5 * swiglu_ffn(rmsnorm(x, ffn2_norm), ffn2_w_in, ffn2_w_out)
    return x, paged_rows, new_win, new_ssm, new_conv


def setup_inputs(seed: int = 0) -> dict:
    key = jax.random.key(seed)
    keys = jax.random.split(key, 32)
    f32 = jnp.float32

    def nrm(i, shape, scale):
        return scale * jax.random.normal(keys[i], shape, f32)

    n_pages = PAST_LEN // PAGE_SIZE
    n_used = DEC_BATCH * n_pages
    n_pool = n_used + (n_used + 3) // 4
    w_buf = min(WINDOW, PAST_LEN)
    page_table = jax.random.permutation(keys[6], n_pool)[:n_used].reshape(DEC_BATCH, n_pages).astype(jnp.int32)
    dt0 = jnp.exp(jax.random.uniform(keys[15], (DEPTH, M_HEADS), f32, math.log(1e-3), math.log(1e-1)))
    return {
        'x_prompt': nrm(0, (BATCH, SEQ, D_MODEL), 1.0),
        'x_sample': nrm(1, (DEC_BATCH, DEC_SEQ, D_MODEL), 1.0),
        'cache_kv': nrm(2, (DEPTH, n_pool, PAGE_SIZE, 4, KV_HEADS, HEAD_DIM), 1.0),
        'cache_win': nrm(3, (DEPTH, DEC_BATCH, w_buf, 2, KV_HEADS, HEAD_DIM), 1.0),
        'state_ssm': nrm(4, (DEPTH, DEC_BATCH, M_HEADS, M_HEADDIM, D_STATE), 0.5),
        'state_conv': nrm(5, (DEPTH, DEC_BATCH, D_CONV - 1, CONV_DIM), 1.0),
        'page_table': page_table,
        'rel_bias': nrm(7, (NUM_BUCKETS, Q_HEADS), 0.5),
        'ffn1_norm': 1.0 + nrm(8, (DEPTH, D_MODEL), 0.02),
        'ffn1_w_in': nrm(9, (DEPTH, D_MODEL, 2 * D_FF), D_MODEL ** -0.5),
        'ffn1_w_out': nrm(10, (DEPTH, D_FF, D_MODEL), D_FF ** -0.5),
        'mix_norm': 1.0 + nrm(11, (DEPTH, D_MODEL), 0.02),
        'w_in_proj': nrm(12, (DEPTH, D_MODEL, D_PROJ), D_MODEL ** -0.5),
        'conv_w': nrm(13, (DEPTH, D_CONV, CONV_DIM), D_CONV ** -0.5),
        'conv_b': nrm(14, (DEPTH, CONV_DIM), 0.02),
        'dt_bias': dt0 + jnp.log(-jnp.expm1(-dt0)),
        'a_log': jnp.log(jax.random.uniform(keys[16], (DEPTH, M_HEADS), f32, 1.0, 16.0)),
        'd_skip': 1.0 + nrm(17, (DEPTH, M_HEADS), 0.02),
        'ssm_norm': 1.0 + nrm(18, (DEPTH, D_INNER), 0.02),
        'qk_norm': 1.0 + nrm(19, (DEPTH, 4, HEAD_DIM), 0.02),
        'cmp_pe': nrm(20, (DEPTH, 2, CMP_BLOCK, HEAD_DIM), 0.1),
        'cmp_w1': nrm(21, (DEPTH, 2, CMP_BLOCK, HEAD_DIM, CMP_HIDDEN), (CMP_BLOCK * HEAD_DIM) ** -0.5),
        'cmp_w2': nrm(22, (DEPTH, 2, CMP_HIDDEN, HEAD_DIM), CMP_HIDDEN ** -0.5),
        'w_branch_ssm': nrm(23, (DEPTH, D_INNER, D_MODEL), D_INNER ** -0.5),
        'w_branch_attn': nrm(24, (DEPTH, D_ATTN, D_MODEL), D_ATTN ** -0.5),
        'w_out': nrm(25, (DEPTH, D_MODEL, D_MODEL), D_MODEL ** -0.5),
        'ffn2_norm': 1.0 + nrm(26, (DEPTH, D_MODEL), 0.02),
        'ffn2_w_in': nrm(27, (DEPTH, D_MODEL, 2 * D_FF), D_MODEL ** -0.5),
        'ffn2_w_out': nrm(28, (DEPTH, D_FF, D_MODEL), D_FF ** -0.5),
    }


def reference(x_prompt, x_sample, cache_kv, cache_win, state_ssm, state_conv, page_table, rel_bias,
              ffn1_norm, ffn1_w_in, ffn1_w_out, mix_norm, w_in_proj, conv_w, conv_b, dt_bias, a_log,
              d_skip, ssm_norm, qk_norm, cmp_pe, cmp_w1, cmp_w2, w_branch_ssm, w_branch_attn, w_out,
              ffn2_norm, ffn2_w_in, ffn2_w_out):
    bias_tab = rel_bias.astype(jnp.float32).reshape(NUM_BUCKETS, KV_HEADS, Q_PER_KV)
    xp = x_prompt
    xs = x_sample
    kv_p, win_p, ssm_p, conv_p = [], [], [], []
    kv_s, win_s, ssm_s, conv_s = [], [], [], []
    for l in range(DEPTH):
        lw = (ffn1_norm[l], ffn1_w_in[l], ffn1_w_out[l], mix_norm[l], w_in_proj[l], conv_w[l], conv_b[l],
              dt_bias[l], a_log[l], d_skip[l], ssm_norm[l], qk_norm[l], cmp_pe[l], cmp_w1[l], cmp_w2[l],
              w_branch_ssm[l], w_branch_attn[l], w_out[l], ffn2_norm[l], ffn2_w_in[l], ffn2_w_out[l])
        zero_ssm = jnp.zeros((xp.shape[0], M_HEADS, M_HEADDIM, D_STATE), state_ssm.dtype)
        zero_conv = jnp.zeros((xp.shape[0], D_CONV - 1, CONV_DIM), xp.dtype)
        xp, r_kv, r_win, r_ssm, r_conv = layer_forward(xp, zero_ssm, zero_conv, nsa_prompt, bias_tab, lw)
        kv_p.append(r_kv)
        win_p.append(r_win)
        ssm_p.append(r_ssm)
        conv_p.append(r_conv)
        nsa_s = functools.partial(nsa_sample, cache_kv_l=cache_kv[l], page_table=page_table,
                                  cache_win_l=cache_win[l])
        xs, r_kv, r_win, r_ssm, r_conv = layer_forward(xs, state_ssm[l], state_conv[l], nsa_s, bias_tab, lw)
        kv_s.append(r_kv)
        win_s.append(r_win)
        ssm_s.append(r_ssm)
        conv_s.append(r_conv)
    return (xp, xs, jnp.stack(kv_p), jnp.stack(win_p), jnp.stack(ssm_p), jnp.stack(conv_p),
            jnp.stack(kv_s), jnp.stack(win_s), jnp.stack(ssm_s), jnp.stack(conv_s))
```

```python
from contextlib import ExitStack

import numpy as np
import concourse.bass as bass
import concourse.mybir as mybir
from concourse.bass_utils import run_bass_kernel_spmd

F32 = mybir.dt.float32
AF = mybir.ActivationFunctionType
ALU = mybir.AluOpType
AX = mybir.AxisListType

N_CORES = 8
D = 1024
DFF = 2816
SEQ = 2048
NSEQ_PC = 2
TP = NSEQ_PC * SEQ
NS_SEQ_PC = 4
DEC = 8
TS = NS_SEQ_PC * DEC
DPROJ = 9808
EPS = 1e-6
PBLOCKS = ([(i * 512, 512, 'z') for i in range(4)]
           + [(2048 + i * 512, 512, 'xbc') for i in range(6)]
           + [(5120, 32, 'dt')]
           + [(5152 + i * 512, 512, 'q') for i in range(2)]
           + [(6176, 512, 'kv0'), (6688, 512, 'kv1'), (7200, 512, 'kv2')]
           + [(7712, 48, 'ng')]
           + [(7760 + i * 512, 512, 'mg') for i in range(4)])


class Prog:
    ENG = ('pe', 'dve', 'act', 'pool', 'sp')
    NDS = 6

    def __init__(self, nc, es):
        self.nc = nc
        self.es = es
        self.prog = {e: [] for e in self.ENG}
        self.cnt = {}
        self.sem = {}
        for e in self.ENG:
            self._mk(e)
        self.lastw = {}
        self.rd = {}
        self.seen = {e: {} for e in self.ENG}
        self.rr = {e: 0 for e in self.ENG}
        self.nbuf = 0
        self.cur = es
        self.reg = {}

    def _mk(self, key):
        self.sem[key] = self.es.enter_context(self.nc.semaphore('s_' + key))
        self.cnt[key] = 0

    def sb(self, name, shape, dtype=F32):
        self.nbuf += 1
        t = self.cur.enter_context(self.nc.sbuf_tensor(f'{name}_{self.nbuf}', list(shape), dtype))
        return t

    def ps(self, name, shape, dtype=F32):
        self.nbuf += 1
        t = self.cur.enter_context(self.nc.psum_tensor(f'{name}_{self.nbuf}', list(shape), dtype))
        return t

    def _deps(self, reads, writes):
        d = {}

        def add(k, v):
            if v > d.get(k, 0):
                d[k] = v
        for b in reads:
            if b in self.lastw:
                add(*self.lastw[b])
        for b in writes:
            if b in self.lastw:
                add(*self.lastw[b])
            for k, v in self.rd.get(b, {}).items():
                add(k, v)
        return d

    def _commit(self, eng, d, fn, key, inc, reads, writes):
        waits = []
        for k, v in d.items():
            if k == 'pe' and eng == 'pe':
                continue
            if self.seen[eng].get(k, 0) < v:
                self.seen[eng][k] = v
                waits.append((k, v))
        self.cnt[key] += inc
        val = self.cnt[key]
        self.prog[eng].append((waits, fn, key, inc))
        for b in writes:
            self.lastw[b] = (key, val)
            self.rd[b] = {}
        for b in reads:
            r = self.rd.setdefault(b, {})
            if r.get(key, 0) < val:
                r[key] = val

    def op(self, eng, fn, reads=(), writes=()):
        d = self._deps(reads, writes)
        self._commit(eng, d, fn, eng, 1, reads, writes)

    def dma(self, eng, out, in_, reads=(), writes=(), **kw):
        i = self.rr[eng]
        self.rr[eng] = (i + 1) % self.NDS
        key = f'{eng}_d{i}'
        if key not in self.sem:
            self._mk(key)
        d = self._deps(reads, writes)
        if self.cnt[key] > d.get(key, 0):
            d[key] = self.cnt[key]
        self._commit(eng, d, lambda e: e.dma_start(out=out, in_=in_, **kw), key, 16, reads, writes)

    def barrier(self):
        allw = [(k, v) for k, v in self.cnt.items() if v > 0]
        for e in self.ENG:
            waits = []
            for k, v in allw:
                if self.seen[e].get(k, 0) < v:
                    self.seen[e][k] = v
                    waits.append((k, v))
            self.prog[e].append((waits, None, None, 0))

    def finish(self):
        waits = [(k, v) for k, v in self.cnt.items() if '_d' in k and v > 0]
        self.prog['sp'].append((waits, None, None, 0))

    def emit(self):
        nc = self.nc
        with nc.Block() as block:
            def run(name, e):
                for waits, fn, key, inc in self.prog[name]:
                    for k, v in waits:
                        e.wait_ge(self.sem[k], v)
                    if fn is not None:
                        fn(e).then_inc(self.sem[key], inc)

            @block.tensor
            def _(e):
                run('pe', e)

            @block.vector
            def _(e):
                run('dve', e)

            @block.scalar
            def _(e):
                with e.register('pgreg2') as r:
                    self.reg['act'] = r
                    run('act', e)

            @block.gpsimd
            def _(e):
                run('pool', e)

            @block.sync
            def _(e):
                with e.register('pgreg') as r:
                    self.reg['sp'] = r
                    run('sp', e)


def build(n_ptiles=TP // 512, do_sample=True, o_from_host=False, dbg_o=False, branches='csw', npool=5120, sbranches='csw'):
    nc = bass.Bass("TRN2", target_bir_lowering=False)
    es = ExitStack()
    P = Prog(nc, es)

    def din(name, shape, dt=F32):
        return nc.dram_tensor(name, list(shape), dt, kind="ExternalInput").ap()

    def dout(name, shape, dt=F32):
        return nc.dram_tensor(name, list(shape), dt, kind="ExternalOutput").ap()

    xp = din('xp', [TP, D])
    xs = din('xs', [TS, D])
    ident_d = din('ident', [128, 128])
    g1_d = din('g_ffn1', [128, 8])
    gm_d = din('g_mix', [128, 8])
    qkn_d = din('qkn', [128, 4 * 64])
    w1i = din('ffn1_w_in', [D, 2 * DFF])
    w1o = din('ffn1_w_out', [DFF, D])
    wpj = din('w_in_proj', [D, DPROJ])
    cwin = din('cache_win', [NS_SEQ_PC, 512, 512])
    tri_d = din('tri', [128, 128])
    convw_d = din('conv_w', [4, 3072])
    convb_d = din('conv_b', [3072])
    dtb_d = din('dt_bias', [32])
    alog_d = din('a_log', [32])
    dsk_d = din('d_skip', [32])
    ssmn_d = din('ssm_norm', [2048])
    sssm_d = din('state_ssm', [NS_SEQ_PC, 2048, 128])
    sconv_d = din('state_conv', [NS_SEQ_PC, 3, 3072])
    tabT_d = din('tabT', [16, 32])
    cache_d = din('cache_kv', [npool * 256, 512])
    pt_d = din('page_table', [NS_SEQ_PC, 128], mybir.dt.int32)
    sel8_d = din('Sel8', [32, 8])
    e2_d = din('E2', [2, 128])
    relb_d = din('rel_bias', [32, 16])
    J_d = din('Jrev', [128, 128])
    E_d = din('Esel', [32, 16, 128])
    peT_d = din('cmp_peT', [64, 2, 32])
    cw1_d = din('cmp_w1', [2, 32, 64, 128])
    cw2_d = din('cmp_w2', [2, 128, 64])
    wbs_d = din('w_branch_ssm', [2048, D])
    wba_d = din('w_branch_attn', [D, D])
    wout_d = din('w_out', [D, D])
    g2_d = din('g_ffn2', [128, 8])
    w2i = din('ffn2_w_in', [D, 2 * DFF])
    w2o = din('ffn2_w_out', [DFF, D])

    o_kvp = dout('o_kv_p', [TP, 1024])
    o_winp = dout('o_win_p', [NSEQ_PC * 512, 512])
    o_convp = dout('o_conv_p', [NSEQ_PC * 3, 3072])
    o_kvs = dout('o_kv_s', [TS, 1024])
    o_wins = dout('o_win_s', [NS_SEQ_PC * 512, 512])
    o_convs = dout('o_conv_s', [NS_SEQ_PC * 3, 3072])
    o_ssmp = dout('o_ssm_p', [NSEQ_PC * 2048, 128])
    o_ssms = dout('o_ssm_s', [NS_SEQ_PC * 2048, 128])
    ys_scr = nc.dram_tensor('ys_scr', [TP + TS, 2048], F32, kind="ExternalOutput" if o_from_host else "Internal").ap()
    if o_from_host:
        o_scr = din('o_dbg', [TP + TS, D])
    else:
        o_scr = nc.dram_tensor('o_scr', [TP + TS, D], F32, kind="ExternalOutput" if dbg_o else "Internal").ap()
    bias_scr = nc.dram_tensor('bias_scr', [2, 16, 2560], F32, kind="Internal").ap()
    nm_scr = nc.dram_tensor('nm_scr', [NS_SEQ_PC, 4, 8, 256], F32, kind="Internal").ap()
    o_yp = dout('o_y_p', [TP, D])
    o_ys = dout('o_y_s', [TS, D])
    p_scr = nc.dram_tensor('p_scr', [TP + TS, DPROJ], F32, kind="Internal").ap()
    x1_scr = nc.dram_tensor('x1_scr', [8, 128, TP + TS], F32, kind="ExternalOutput" if o_from_host else "Internal").ap()

    ident = P.sb('ident', [128, 128])
    ones = P.sb('ones', [128, 128])
    g1 = P.sb('g1', [128, 8])
    gm = P.sb('gm', [128, 8])
    qkn = P.sb('qkn', [128, 4, 64])
    P.dma('sp', ident[:], ident_d, writes=['ident'])
    P.dma('sp', g1[:], g1_d, writes=['g1'])
    P.dma('sp', gm[:], gm_d, writes=['gm'])
    P.dma('sp', qkn[:], qkn_d.rearrange('p (a d) -> p a d', a=4), writes=['qkn'])
    P.op('dve', lambda e: e.memset(ones[:], 1.0), writes=['ones'])
    P.op('dve', lambda e: e.tensor_scalar(out=qkn[:, 0, :], in0=qkn[:, 0, :], scalar1=0.125, scalar2=None, op0=ALU.mult),
         reads=['qkn'], writes=['qkn'])

    NPS = 6
    psb = [P.ps('psb', [128, 512]) for _ in range(NPS)]
    tri = P.sb('tri', [128, 128])
    P.dma('sp', tri[:], tri_d, writes=['tri'])
    esA = ExitStack()
    P.cur = esA
    NT = 512
    xt = P.sb('xt', [128, 4, D])
    xT = P.sb('xT', [128, 8, NT])
    xnT = P.sb('xnT', [128, 8, NT])
    rstd = P.sb('rstd', [128, NT])
    hT = P.sb('hT', [128, 22, NT])
    sq = hT
    sg = [P.sb('sg', [128, NT]) for _ in range(2)]
    NWB = 3
    wblk = [P.sb('wblk', [128, 2, 8, 128]) for _ in range(NWB)]
    woblk = [P.sb('woblk', [128, 22, 128]) for _ in range(2)]
    wpblk = [P.sb('wpblk', [128, 8, 512]) for _ in range(2)]
    pst = [P.sb('pst', [128, 4, 512]) for _ in range(2)]
    hsq = P.sb('hsq', [128, 512])
    hss = P.sb('hss', [128, 16])
    hrs = P.sb('hrs', [128, 16])
    TA = dict(xT=xT, xnT=xnT, hT=hT, sg=sg, wblk=wblk, woblk=woblk, rstd=rstd)
    cnt = {'ps': 0, 'w': 0, 'wo': 0, 'wp': 0, 'pst': 0, 'sg': 0}

    def nxt(k, n):
        i = cnt[k]
        cnt[k] = (i + 1) % n
        return i

    def getps():
        i = nxt('ps', NPS)
        return psb[i], f'psb{i}'

    def rmsnorm_T(T, g, gname, TT):
        src, dst, sq, rstd = T['xT'], T['xnT'], T['hT'], T['rstd']
        for c in range(8):
            if c % 2 == 0:
                P.op('act', lambda e, c=c: e.activation(out=sq[:, c, :TT], in_=src[:, c, :TT], func=AF.Square),
                     reads=['xT'], writes=[f'hT{c}'])
            else:
                P.op('pool', lambda e, c=c: e.tensor_tensor(out=sq[:, c, :TT], in0=src[:, c, :TT],
                                                           in1=src[:, c, :TT], op=ALU.mult),
                     reads=['xT'], writes=[f'hT{c}'])
        ps, pn = getps()
        for c in range(8):
            P.op('pe', lambda e, c=c: e.matmul(ps[:, :TT], lhsT=ones[:], rhs=sq[:, c, :TT],
                                               start=(c == 0), stop=(c == 7)),
                 reads=['ones', f'hT{c}'], writes=[pn])
        P.op('act', lambda e: e.activation(out=rstd[:, :TT], in_=ps[:, :TT], func=AF.Sqrt,
                                           scale=1.0 / D, bias=EPS),
             reads=[pn], writes=['rstd'])
        P.op('dve', lambda e: e.reciprocal(out=rstd[:, :TT], in_=rstd[:, :TT]),
             reads=['rstd'], writes=['rstd'])
        for c in range(8):
            P.op('dve', lambda e, c=c: e.scalar_tensor_tensor(out=dst[:, c, :TT], in0=src[:, c, :TT],
                                                            scalar=g[:, c:c + 1], in1=rstd[:, :TT],
                                                            op0=ALU.mult, op1=ALU.mult),
                 reads=['xT', gname, 'rstd'], writes=['xnT'])

    def ffn(T, w_in, w_out, g, gname, TT):
        xT, xnT, hT, sg, wblk, woblk = T['xT'], T['xnT'], T['hT'], T['sg'], T['wblk'], T['woblk']
        rmsnorm_T(T, g, gname, TT)
        for j in range(22):
            wi = nxt('w', NWB)
            wb, wn = wblk[wi], f'wblk{wi}'
            P.dma('sp', wb[:, 0, :, :], w_in[:, j * 128:(j + 1) * 128].rearrange('(c p) n -> p c n', p=128),
                  writes=[wn + 'g'])
            P.dma('sp', wb[:, 1, :, :], w_in[:, DFF + j * 128:DFF + (j + 1) * 128].rearrange('(c p) n -> p c n', p=128),
                  writes=[wn + 'u'])
            pg, pgn = getps()
            pu, pun = getps()
            for c in range(8):
                P.op('pe', lambda e, c=c, pg=pg, wb=wb: e.matmul(pg[:, :TT], lhsT=wb[:, 0, c, :], rhs=xnT[:, c, :TT],
                                                               start=(c == 0), stop=(c == 7)),
                     reads=[wn + 'g', 'xnT'], writes=[pgn])
            for c in range(8):
                P.op('pe', lambda e, c=c, pu=pu, wb=wb: e.matmul(pu[:, :TT], lhsT=wb[:, 1, c, :], rhs=xnT[:, c, :TT],
                                                               start=(c == 0), stop=(c == 7)),
                     reads=[wn + 'u', 'xnT'], writes=[pun])
            si = nxt('sg', 2)
            P.op('act', lambda e, pg=pg, si=si: e.activation(out=sg[si][:, :TT], in_=pg[:, :TT], func=AF.Silu),
                 reads=[pgn], writes=[f'sg{si}'])
            P.op('dve', lambda e, pu=pu, si=si, j=j: e.tensor_tensor(out=hT[:, j, :TT], in0=sg[si][:, :TT],
                                                                     in1=pu[:, :TT], op=ALU.mult),
                 reads=[pun, f'sg{si}'], writes=[f'hT{j}'])
        for n in range(8):
            wi = nxt('wo', 2)
            wo, won = woblk[wi], f'woblk{wi}'
            P.dma('sp', wo[:], w_out[:, n * 128:(n + 1) * 128].rearrange('(j p) n -> p j n', p=128), writes=[won])
            py, pyn = getps()
            for j in range(22):
                P.op('pe', lambda e, j=j, py=py, wo=wo: e.matmul(py[:, :TT], lhsT=wo[:, j, :], rhs=hT[:, j, :TT],
                                                               start=(j == 0), stop=(j == 21)),
                     reads=[won, f'hT{j}'], writes=[pyn])
            P.op('dve', lambda e, n=n, py=py: e.scalar_tensor_tensor(out=xT[:, n, :TT], in0=py[:, :TT], scalar=0.5,
                                                                     in1=xT[:, n, :TT], op0=ALU.mult, op1=ALU.add),
                 reads=[pyn, 'xT'], writes=['xT'])

    def headnorm(t, tname, s_list, rows, ncol_heads, gidx, scale):
        nh = ncol_heads
        for s in s_list:
            v = t[:rows, s, 0:nh * 64]
            v3 = v.rearrange('p (h d) -> p h d', h=nh)
            P.op('pool', lambda e, v=v: e.tensor_tensor(out=hsq[:rows, 0:nh * 64], in0=v, in1=v, op=ALU.mult),
                 reads=[tname], writes=['hsq'])
            P.op('dve', lambda e: e.tensor_reduce(out=hss[:rows, 0:nh],
                                                  in_=hsq[:rows, 0:nh * 64].rearrange('p (h d) -> p h d', h=nh),
                                                  axis=AX.X, op=ALU.add),
                 reads=['hsq'], writes=['hss'])
            P.op('act', lambda e: e.activation(out=hrs[:rows, 0:nh], in_=hss[:rows, 0:nh], func=AF.Sqrt,
                                               scale=1.0 / 64, bias=EPS),
                 reads=['hss'], writes=['hrs'])
            P.op('dve', lambda e: e.reciprocal(out=hrs[:rows, 0:nh], in_=hrs[:rows, 0:nh]),
                 reads=['hrs'], writes=['hrs'])
            P.op('dve', lambda e, v3=v3: e.tensor_tensor(out=v3, in0=v3,
                                                         in1=hrs[:rows, 0:nh].unsqueeze(2).to_broadcast([rows, nh, 64]),
                                                         op=ALU.mult),
                 reads=[tname, 'hrs'], writes=[tname])
            P.op('pool', lambda e, v3=v3: e.tensor_tensor(
                out=v3, in0=v3, in1=qkn[:rows, gidx:gidx + 1, :].to_broadcast([rows, nh, 64]), op=ALU.mult),
                reads=[tname, 'qkn'], writes=[tname])

    def token_tile(x_rows, ns, rows, tok0, kind, seq_info):
        TT = ns * rows
        P.dma('sp', xt[:rows, :ns, :], x_rows.rearrange('(s p) d -> p s d', p=rows), writes=['xt'])
        for c in range(8):
            ps, pn = getps()
            for s in range(ns):
                P.op('pe', lambda e, c=c, s=s, ps=ps: e.transpose(out=ps[:, s * rows:(s + 1) * rows],
                                                                  in_=xt[:rows, s, c * 128:(c + 1) * 128],
                                                                  identity=ident[:rows, :rows]),
                     reads=['xt', 'ident'], writes=[pn])
            if c % 2 == 0:
                P.op('dve', lambda e, c=c, ps=ps: e.tensor_copy(out=xT[:, c, :TT], in_=ps[:, :TT]),
                     reads=[pn], writes=['xT'])
            else:
                P.op('act', lambda e, c=c, ps=ps: e.copy(out=xT[:, c, :TT], in_=ps[:, :TT]),
                     reads=[pn], writes=['xT'])
        ffn(TA, w1i, w1o, g1, 'g1', TT)
        P.dma('pool', x1_scr[:, :, tok0:tok0 + TT].rearrange('c p t -> p c t'), xT[:, :, :TT], reads=['xT'],
              writes=['x1_scr'])
        rmsnorm_T(TA, gm, 'gm', TT)
        for (c0, w, tag) in PBLOCKS:
            wi = nxt('wp', 2)
            wp, wpn = wpblk[wi], f'wpblk{wi}'
            P.dma('sp', wp[:, :, :w], wpj[:, c0:c0 + w].rearrange('(c p) n -> p c n', p=128), writes=[wpn])
            pi = nxt('pst', 2)
            st, stn = pst[pi], f'pst{pi}'
            for s in range(ns):
                pp, ppn = getps()
                for c in range(8):
                    P.op('pe', lambda e, c=c, s=s, pp=pp, wp=wp: e.matmul(pp[:rows, :w],
                                                                         lhsT=xnT[:, c, s * rows:(s + 1) * rows],
                                                                         rhs=wp[:, c, :w], start=(c == 0), stop=(c == 7)),
                         reads=[wpn, 'xnT'], writes=[ppn])
                if s % 2 == 0:
                    P.op('act', lambda e, s=s, pp=pp, st=st: e.copy(out=st[:rows, s, :w], in_=pp[:rows, :w]),
                         reads=[ppn], writes=[stn])
                else:
                    P.op('dve', lambda e, s=s, pp=pp, st=st: e.tensor_copy(out=st[:rows, s, :w], in_=pp[:rows, :w]),
                         reads=[ppn], writes=[stn])
            if tag == 'q':
                headnorm(st, stn, range(ns), rows, 8, 0, 0.125)
            elif tag == 'kv1':
                headnorm(st, stn, range(ns), rows, 4, 2, 1.0)
            elif tag == 'kv2':
                headnorm(st, stn, range(ns), rows, 4, 3, 1.0)
            P.dma('pool', p_scr[tok0:tok0 + TT, c0:c0 + w].rearrange('(s p) n -> p s n', p=rows),
                  st[:rows, :ns, :w], reads=[stn], writes=['p_scr'])
            if kind == 'prompt':
                seq, t0 = seq_info
                if tag in ('kv0', 'kv1'):
                    co = 0 if tag == 'kv0' else 512
                    r0 = seq * SEQ + t0
                    P.dma('pool', o_kvp[r0:r0 + TT, co:co + 512].rearrange('(s p) n -> p s n', p=rows),
                          st[:rows, :ns, :512], reads=[stn], writes=['o_kvp'])
                if tag == 'kv2' and t0 >= SEQ - 512:
                    r0 = seq * 512 + (t0 - (SEQ - 512))
                    P.dma('pool', o_winp[r0:r0 + TT, :].rearrange('(s p) n -> p s n', p=rows),
                          st[:rows, :ns, :512], reads=[stn], writes=['o_winp'])
                if tag == 'xbc' and t0 + TT == SEQ:
                    cc = c0 - 2048
                    P.dma('pool', o_convp[seq * 3:seq * 3 + 3, cc:cc + 512], st[rows - 3:rows, ns - 1, :512],
                          reads=[stn], writes=['o_convp'])
            else:
                if tag in ('kv0', 'kv1'):
                    co = 0 if tag == 'kv0' else 512
                    P.dma('pool', o_kvs[:, co:co + 512], st[:rows, 0, :512], reads=[stn], writes=['o_kvs'])
                if tag == 'kv2':
                    for b in range(NS_SEQ_PC):
                        P.dma('pool', o_wins[b * 512 + 504:b * 512 + 512, :], st[b * 8:b * 8 + 8, 0, :512],
                              reads=[stn], writes=['o_wins'])
                if tag == 'xbc':
                    cc = c0 - 2048
                    for b in range(NS_SEQ_PC):
                        P.dma('pool', o_convs[b * 3:b * 3 + 3, cc:cc + 512], st[b * 8 + 5:b * 8 + 8, 0, :512],
                              reads=[stn], writes=['o_convs'])

    if do_sample:
        for b in range(NS_SEQ_PC):
            P.dma('pool', o_wins[b * 512:b * 512 + 504, :], cwin[b, 8:512, :], writes=['o_wins'])
        token_tile(xs, 1, TS, TP, 'sample', None)
    for it in range(n_ptiles):
        tok = it * NT
        token_tile(xp[tok:tok + NT, :], 4, 128, tok, 'prompt', (tok // SEQ, tok % SEQ))

    P.barrier()
    esA.close()
    esB = ExitStack()
    P.cur = esB
    cw = P.sb('cw', [128, 4, 3072])
    cbias = P.sb('cbias', [128, 3072])
    dtb = P.sb('dtb', [128, 32])
    Aneg = P.sb('Aneg', [128, 32])
    dsk = P.sb('dsk', [128, 32])
    ssmn = P.sb('ssmn', [128, 2048])
    P.dma('sp', cw[:], convw_d.partition_broadcast(128), writes=['cw'])
    P.dma('sp', cbias[:], convb_d.partition_broadcast(128), writes=['cbias'])
    P.dma('sp', dtb[:], dtb_d.partition_broadcast(128), writes=['dtb'])
    P.dma('sp', Aneg[:], alog_d.partition_broadcast(128), writes=['Aneg'])
    P.dma('sp', dsk[:], dsk_d.partition_broadcast(128), writes=['dsk'])
    P.dma('sp', ssmn[:], ssmn_d.partition_broadcast(128), writes=['ssmn'])
    P.op('act', lambda e: e.activation(out=Aneg[:], in_=Aneg[:], func=AF.Exp), reads=['Aneg'], writes=['Aneg'])
    P.op('dve', lambda e: e.tensor_scalar(out=Aneg[:], in0=Aneg[:], scalar1=-1.0, scalar2=None, op0=ALU.mult),
         reads=['Aneg'], writes=['Aneg'])

    xsh = [P.sb('xsh', [128, 3072]) for _ in range(4)]
    xc = P.sb('xc', [128, 3072])
    zt = P.sb('zt', [128, 2048])
    yt = P.sb('yt', [128, 2048])
    xdt = P.sb('xdt', [128, 2048])
    xdd = P.sb('xdd', [128, 2048])
    hS = P.sb('hTs', [128, 2048])
    BT = P.sb('BT', [128, 4, 128])
    CT = P.sb('CT', [128, 4, 128])
    dtr = P.sb('dtr', [128, 32])
    dtt = P.sb('dtt', [128, 32])
    dA = P.sb('dA', [128, 32])
    acs = P.sb('acs', [128, 32])
    ea = P.sb('ea', [128, 32])
    dte = P.sb('dte', [128, 32])
    cd = P.sb('cd', [128, 32])
    cbTm = P.sb('cbTm', [128, 128])
    rj = [P.sb('rj', [128, 128]) for _ in range(2)]
    arg = [P.sb('arg', [128, 128]) for _ in range(2)]
    MT = [P.sb('MT', [128, 128]) for _ in range(2)]
    t1 = P.sb('t1', [128, 512])
    gss = P.sb('gss', [128, 4])
    grs = P.sb('grs', [128, 4])
    sst = P.sb('sst', [128, 16, 128])
    cntB = {'h': 0}
    pyo_t = P.ps('pyo', [128, 512])
    pyd_t = P.ps('pyd', [128, 512])

    def ssd_chunk(tok, rows, first, sconv, z_rows_tok):
        R = rows
        for k in range(4):
            nm = f'xsh{k}'
            if k == 0:
                P.dma('sp', xsh[0][:R, :], p_scr[tok:tok + R, 2048:5120], reads=['p_scr'], writes=[nm])
            elif first:
                if sconv is None:
                    P.op('pool', lambda e, k=k: e.memset(xsh[k][0:k, :], 0.0), writes=[nm])
                    P.dma('sp', xsh[k][k:R, :], p_scr[tok:tok + R - k, 2048:5120], reads=['p_scr'], writes=[nm + 'b'])
                else:
                    P.dma('sp', xsh[k][0:k, :], sconv[3 - k:3, :], writes=[nm])
                    P.dma('sp', xsh[k][k:R, :], p_scr[tok:tok + R - k, 2048:5120], reads=['p_scr'], writes=[nm + 'b'])
            else:
                P.dma('sp', xsh[k][:R, :], p_scr[tok - k:tok - k + R, 2048:5120], reads=['p_scr'], writes=[nm])
        P.dma('sp', dtr[:R, :], p_scr[tok:tok + R, 5120:5152], reads=['p_scr'], writes=['dtr'])
        P.dma('sp', zt[:R, :], p_scr[tok:tok + R, 0:2048], reads=['p_scr'], writes=['zt'])
        for k in range(4):
            P.op('pool', lambda e, k=k: e.tensor_tensor(out=xsh[k][:R, :], in0=xsh[k][:R, :], in1=cw[:R, 3 - k, :],
                                                        op=ALU.mult),
                 reads=[f'xsh{k}', f'xsh{k}b', 'cw'], writes=[f'xsh{k}'])
        P.op('dve', lambda e: e.tensor_tensor(out=xc[:R, :], in0=xsh[0][:R, :], in1=xsh[1][:R, :], op=ALU.add),
             reads=['xsh0', 'xsh1'], writes=['xc'])
        P.op('dve', lambda e: e.tensor_tensor(out=xc[:R, :], in0=xc[:R, :], in1=xsh[2][:R, :], op=ALU.add),
             reads=['xc', 'xsh2'], writes=['xc'])
        P.op('dve', lambda e: e.tensor_tensor(out=xc[:R, :], in0=xc[:R, :], in1=xsh[3][:R, :], op=ALU.add),
             reads=['xc', 'xsh3'], writes=['xc'])
        P.op('dve', lambda e: e.tensor_tensor(out=xc[:R, :], in0=xc[:R, :], in1=cbias[:R, :], op=ALU.add),
             reads=['xc', 'cbias'], writes=['xc'])
        P.op('act', lambda e: e.activation(out=xc[:R, :], in_=xc[:R, :], func=AF.Silu), reads=['xc'], writes=['xc'])
        P.op('dve', lambda e: e.tensor_tensor(out=dtt[:R, :], in0=dtr[:R, :], in1=dtb[:R, :], op=ALU.add),
             reads=['dtr', 'dtb'], writes=['dtt'])
        P.op('act', lambda e: e.activation(out=dtt[:R, :], in_=dtt[:R, :], func=AF.Exp), reads=['dtt'], writes=['dtt'])
        P.op('act', lambda e: e.activation(out=dtt[:R, :], in_=dtt[:R, :], func=AF.Ln, bias=1.0, scale=1.0),
             reads=['dtt'], writes=['dtt'])
        P.op('dve', lambda e: e.tensor_tensor(out=dA[:R, :], in0=dtt[:R, :], in1=Aneg[:R, :], op=ALU.mult),
             reads=['dtt', 'Aneg'], writes=['dA'])
        pa, pan = getps()
        P.op('pe', lambda e: e.matmul(pa[:R, 0:32], lhsT=tri[:R, :R], rhs=dA[:R, :], start=True, stop=True),
             reads=['tri', 'dA'], writes=[pan])
        pt, ptn = getps()
        P.op('pe', lambda e: e.matmul(pt[:, 0:32], lhsT=ones[:R, :], rhs=dA[:R, :], start=True, stop=True),
             reads=['ones', 'dA'], writes=[ptn])
        P.op('dve', lambda e: e.tensor_copy(out=acs[:R, :], in_=pa[:R, 0:32]), reads=[pan], writes=['acs'])
        P.op('act', lambda e: e.activation(out=ea[:R, :], in_=pa[:R, 0:32], func=AF.Exp), reads=[pan], writes=['ea'])
        P.op('dve', lambda e: e.tensor_tensor(out=dte[:R, :], in0=pt[:R, 0:32], in1=acs[:R, :], op=ALU.subtract),
             reads=[ptn, 'acs'], writes=['dte'])
        P.op('act', lambda e: e.activation(out=dte[:R, :], in_=dte[:R, :], func=AF.Exp), reads=['dte'], writes=['dte'])
        P.op('act', lambda e: e.activation(out=cd[:, :], in_=pt[:, 0:32], func=AF.Exp), reads=[ptn], writes=['cd'])
        P.op('dve', lambda e: e.tensor_tensor(out=xdt[:R, :].rearrange('p (j d) -> p j d', j=32),
                                              in0=xc[:R, 0:2048].rearrange('p (j d) -> p j d', j=32),
                                              in1=dtt[:R, :].unsqueeze(2).to_broadcast([R, 32, 64]), op=ALU.mult),
             reads=['xc', 'dtt'], writes=['xdt'])
        P.op('pool', lambda e: e.tensor_tensor(out=xdd[:R, :].rearrange('p (j d) -> p j d', j=32),
                                               in0=xdt[:R, :].rearrange('p (j d) -> p j d', j=32),
                                               in1=dte[:R, :].unsqueeze(2).to_broadcast([R, 32, 64]), op=ALU.mult),
             reads=['xdt', 'dte'], writes=['xdd'])
        pb_, pbn = getps()
        pc_, pcn = getps()
        for g in range(4):
            P.op('pe', lambda e, g=g: e.transpose(out=pb_[:, g * R:(g + 1) * R],
                                                  in_=xc[:R, 2048 + g * 128:2048 + (g + 1) * 128],
                                                  identity=ident[:R, :R]),
                 reads=['xc', 'ident'], writes=[pbn])
            P.op('pe', lambda e, g=g: e.transpose(out=pc_[:, g * R:(g + 1) * R],
                                                  in_=xc[:R, 2560 + g * 128:2560 + (g + 1) * 128],
                                                  identity=ident[:R, :R]),
                 reads=['xc', 'ident'], writes=[pcn])
        P.op('dve', lambda e: e.tensor_copy(out=BT[:, :, :R], in_=pb_[:, 0:4 * R].rearrange('p (g r) -> p g r', g=4)),
             reads=[pbn], writes=['BT'])
        P.op('act', lambda e: e.copy(out=CT[:, :, :R], in_=pc_[:, 0:4 * R].rearrange('p (g r) -> p g r', g=4)),
             reads=[pcn], writes=['CT'])
        for g in range(4):
            pcb, pcbn = getps()
            P.op('pe', lambda e, g=g, pcb=pcb: e.matmul(pcb[:R, :R], lhsT=BT[:, g, :R], rhs=CT[:, g, :R],
                                                       start=True, stop=True),
                 reads=['BT', 'CT'], writes=[pcbn])
            P.op('dve', lambda e, pcb=pcb: e.tensor_tensor(out=cbTm[:R, :R], in0=pcb[:R, :R], in1=tri[:R, :R],
                                                           op=ALU.mult),
                 reads=[pcbn, 'tri'], writes=['cbTm'])
            pyo, pyon = pyo_t, 'pyo_t'
            P.op('pe', lambda e, g=g, pyo=pyo: e.matmul(pyo[:R, :], lhsT=CT[:, g, :R], rhs=hS[:, g * 512:(g + 1) * 512],
                                                       start=True, stop=True),
                 reads=['CT', f'hS{g}'], writes=[pyon])
            pyd, pydn = pyd_t, 'pyd_t'
            for jj in range(8):
                j = g * 8 + jj
                hi = cntB['h']
                cntB['h'] = (hi + 1) % 2
                P.op('pool', lambda e, j=j, hi=hi: e.tensor_scalar(out=rj[hi][:R, :R], in0=tri[:R, :R],
                                                                   scalar1=dA[:R, j:j + 1], scalar2=None, op0=ALU.mult),
                     reads=['tri', 'dA'], writes=[f'rj{hi}'])
                pbc, pbcn = getps()
                P.op('pe', lambda e, hi=hi, pbc=pbc: e.matmul(pbc[:R, :R], lhsT=ones[:R, :R], rhs=rj[hi][:R, :R],
                                                             start=True, stop=True),
                     reads=['ones', f'rj{hi}'], writes=[pbcn])
                P.op('dve', lambda e, j=j, hi=hi, pbc=pbc: e.tensor_scalar(out=arg[hi][:R, :R], in0=pbc[:R, :R],
                                                                           scalar1=acs[:R, j:j + 1], scalar2=0.0,
                                                                           op0=ALU.subtract, op1=ALU.min),
                     reads=[pbcn, 'acs'], writes=[f'arg{hi}'])
                P.op('act', lambda e, hi=hi: e.activation(out=arg[hi][:R, :R], in_=arg[hi][:R, :R], func=AF.Exp),
                     reads=[f'arg{hi}'], writes=[f'arg{hi}'])
                P.op('pool', lambda e, hi=hi: e.tensor_tensor(out=MT[hi][:R, :R], in0=arg[hi][:R, :R],
                                                             in1=cbTm[:R, :R], op=ALU.mult),
                     reads=[f'arg{hi}', 'cbTm'], writes=[f'MT{hi}'])
                P.op('pe', lambda e, j=j, jj=jj, hi=hi, pyd=pyd: e.matmul(pyd[:R, jj * 64:(jj + 1) * 64],
                                                                         lhsT=MT[hi][:R, :R],
                                                                         rhs=xdt[:R, j * 64:(j + 1) * 64],
                                                                         start=True, stop=True),
                     reads=[f'MT{hi}', 'xdt'], writes=[pydn])
            P.op('dve', lambda e, g=g, pyo=pyo: e.tensor_tensor(
                out=t1[:R, :].rearrange('p (j d) -> p j d', j=8),
                in0=pyo[:R, :].rearrange('p (j d) -> p j d', j=8),
                in1=ea[:R, g * 8:(g + 1) * 8].unsqueeze(2).to_broadcast([R, 8, 64]), op=ALU.mult),
                reads=[pyon, 'ea'], writes=['t1'])
            P.op('dve', lambda e, g=g, pyd=pyd: e.tensor_tensor(out=yt[:R, g * 512:(g + 1) * 512], in0=t1[:R, :],
                                                               in1=pyd[:R, :], op=ALU.add),
                 reads=['t1', pydn], writes=[f'yt{g}'])
            pst_, pstn = getps()
            P.op('pe', lambda e, g=g, pst_=pst_: e.matmul(pst_[:, :], lhsT=xc[:R, 2048 + g * 128:2048 + (g + 1) * 128],
                                                         rhs=xdd[:R, g * 512:(g + 1) * 512], start=True, stop=True),
                 reads=['xc', 'xdd'], writes=[pstn])
            P.op('dve', lambda e, g=g: e.tensor_tensor(
                out=hS[:, g * 512:(g + 1) * 512].rearrange('p (j d) -> p j d', j=8),
                in0=hS[:, g * 512:(g + 1) * 512].rearrange('p (j d) -> p j d', j=8),
                in1=cd[:, g * 8:(g + 1) * 8].unsqueeze(2).to_broadcast([128, 8, 64]), op=ALU.mult),
                reads=[f'hS{g}', 'cd'], writes=[f'hS{g}'])
            P.op('dve', lambda e, g=g, pst_=pst_: e.tensor_tensor(out=hS[:, g * 512:(g + 1) * 512],
                                                                 in0=hS[:, g * 512:(g + 1) * 512], in1=pst_[:, :],
                                                                 op=ALU.add),
                 reads=[f'hS{g}', pstn], writes=[f'hS{g}'])
        ytn = [f'yt{g}' for g in range(4)]
        P.op('pool', lambda e: e.tensor_tensor(out=xdd[:R, :].rearrange('p (j d) -> p j d', j=32),
                                               in0=xc[:R, 0:2048].rearrange('p (j d) -> p j d', j=32),
                                               in1=dsk[:R, :].unsqueeze(2).to_broadcast([R, 32, 64]), op=ALU.mult),
             reads=['xc', 'dsk', 'xdd'], writes=['xdd'])
        P.op('pool', lambda e: e.tensor_tensor(out=yt[:R, :], in0=yt[:R, :], in1=xdd[:R, :], op=ALU.add),
             reads=ytn + ['xdd'], writes=ytn)
        P.op('act', lambda e: e.activation(out=zt[:R, :], in_=zt[:R, :], func=AF.Silu), reads=['zt'], writes=['zt'])
        P.op('dve', lambda e: e.tensor_tensor(out=yt[:R, :], in0=yt[:R, :], in1=zt[:R, :], op=ALU.mult),
             reads=ytn + ['zt'], writes=ytn)
        P.op('pool', lambda e: e.tensor_tensor(out=xdd[:R, :], in0=yt[:R, :], in1=yt[:R, :], op=ALU.mult),
             reads=ytn + ['xdd'], writes=['xdd'])
        P.op('dve', lambda e: e.tensor_reduce(out=gss[:R, :], in_=xdd[:R, :].rearrange('p (g d) -> p g d', g=4),
                                              axis=AX.X, op=ALU.add),
             reads=['xdd'], writes=['gss'])
        P.op('act', lambda e: e.activation(out=grs[:R, :], in_=gss[:R, :], func=AF.Sqrt, scale=1.0 / 512, bias=EPS),
             reads=['gss'], writes=['grs'])
        P.op('dve', lambda e: e.reciprocal(out=grs[:R, :], in_=grs[:R, :]), reads=['grs'], writes=['grs'])
        P.op('dve', lambda e: e.tensor_tensor(out=yt[:R, :].rearrange('p (g d) -> p g d', g=4),
                                              in0=yt[:R, :].rearrange('p (g d) -> p g d', g=4),
                                              in1=grs[:R, :].unsqueeze(2).to_broadcast([R, 4, 512]), op=ALU.mult),
             reads=ytn + ['grs'], writes=ytn)
        P.op('pool', lambda e: e.tensor_tensor(out=yt[:R, :], in0=yt[:R, :], in1=ssmn[:R, :], op=ALU.mult),
             reads=ytn + ['ssmn'], writes=ytn)
        P.dma('pool', ys_scr[tok:tok + R, :], yt[:R, :], reads=ytn, writes=['ys_scr'])

    def ssm_out(dst_rows):
        for a4 in range(4):
            ps, pn = getps()
            for i in range(4):
                a = a4 * 4 + i
                P.op('pe', lambda e, a=a, i=i, ps=ps: e.transpose(out=ps[:, i * 128:(i + 1) * 128],
                                                                  in_=hS[:, a * 128:(a + 1) * 128], identity=ident[:, :]),
                     reads=[f'hS{a // 4}', 'ident'], writes=[pn])
            P.op('act', lambda e, a4=a4, ps=ps: e.copy(out=sst[:, a4 * 4:(a4 + 1) * 4, :],
                                                      in_=ps[:, :].rearrange('p (i n) -> p i n', i=4)),
                 reads=[pn], writes=['sst'])
        P.dma('pool', dst_rows.rearrange('(a p) n -> p a n', p=128), sst[:, :, :], reads=['sst'], writes=['o_ssm'])

    hTn = [f'hS{g}' for g in range(4)]
    if do_sample:
        for b in range(NS_SEQ_PC):
            P.dma('sp', sst[:, :, :], sssm_d[b].rearrange('(a p) n -> p a n', p=128), writes=['sst'])
            for a4 in range(4):
                ps, pn = getps()
                for i in range(4):
                    a = a4 * 4 + i
                    P.op('pe', lambda e, a=a, i=i, ps=ps: e.transpose(out=ps[:, i * 128:(i + 1) * 128], in_=sst[:, a, :],
                                                                      identity=ident[:, :]),
                         reads=['sst', 'ident'], writes=[pn])
                P.op('dve', lambda e, a4=a4, ps=ps: e.tensor_copy(out=hS[:, a4 * 512:(a4 + 1) * 512], in_=ps[:, :]),
                     reads=[pn], writes=[f'hS{a4}'])
            ssd_chunk(TP + b * DEC, DEC, True, sconv_d[b], None)
            ssm_out(o_ssms[b * 2048:(b + 1) * 2048, :])
    for sq_ in range(n_ptiles * 512 // SEQ):
        P.op('dve', lambda e: e.memset(hS[:, :], 0.0), reads=hTn, writes=hTn)
        for c in range(SEQ // 128):
            ssd_chunk(sq_ * SEQ + c * 128, 128, c == 0, None, None)
        ssm_out(o_ssmp[sq_ * 2048:(sq_ + 1) * 2048, :])
    P.barrier()
    esB.close()


    BIG = 30000.0
    PADC = 384
    WH = 2432

    def bucket_thresholds():
        d = np.arange(0, 40000)
        df = np.maximum(d, 1).astype(np.float32)
        large = 16 + (np.log(df / np.float32(16)) / np.float32(np.log(2048 / 16)) * np.float32(16)).astype(np.int32)
        large = np.minimum(large, 31)
        bk = np.where(d < 16, d, large)
        return [int(np.argmax(bk >= b)) for b in range(32)]

    TB = bucket_thresholds()

    def phase_C():
        esC = ExitStack()
        P.cur = esC
        I32 = mybir.dt.int32
        pacc = [P.ps('pacc', [128, 512]) for _ in range(2)]
        Jr = P.sb('Jr', [128, 128])
        Es = P.sb('Es', [32, 16, 128])
        P.dma('sp', Jr[:], J_d, writes=['Jr'])
        P.dma('sp', Es[:], E_d, writes=['Es'])
        Gc = P.sb('Gc', [128, 16, 64])
        dblki = P.sb('dblki', [128, 32], I32)
        dblk = P.sb('dblk', [128, 32])
        P.op('pool', lambda e: e.iota(out=dblki[:, :], pattern=[[-64, 32]], base=0, channel_multiplier=1), writes=['dblki'])
        P.op('dve', lambda e: e.tensor_copy(out=dblk[:, :], in_=dblki[:, :]), reads=['dblki'], writes=['dblk'])
        esC0 = ExitStack()
        P.cur = esC0
        tabT = P.sb('tabT', [16, 32])
        delT = P.sb('delT', [16, 32])
        P.dma('sp', tabT[:], tabT_d, writes=['tabT'])
        P.op('dve', lambda e: e.tensor_tensor(out=delT[:, 1:32], in0=tabT[:, 1:32], in1=tabT[:, 0:31], op=ALU.subtract),
             reads=['tabT'], writes=['delT'])
        ddi = P.sb('ddi', [16, 2560], I32)
        dd = P.sb('dd', [16, 2560])
        vacc = P.sb('vacc', [16, 2560])
        vtmp = P.sb('vtmp', [16, 2560])
        vm = P.sb('vm', [16, 2560])
        vout = P.sb('vout', [16, 2560])
        P.op('pool', lambda e: e.iota(out=ddi[:, :], pattern=[[1, 2560]], base=-511, channel_multiplier=0), writes=['ddi'])
        P.op('dve', lambda e: e.tensor_copy(out=dd[:, :], in_=ddi[:, :]), reads=['ddi'], writes=['dd'])
        P.op('dve', lambda e: e.tensor_scalar(out=vacc[:, :], in0=dd[:, :], scalar1=0.0, scalar2=tabT[:, 0:1],
                                              op0=ALU.mult, op1=ALU.add), reads=['dd', 'tabT'], writes=['vacc'])
        for b in range(1, 32):
            P.op('dve', lambda e, b=b: e.tensor_scalar(out=vtmp[:, :], in0=dd[:, :], scalar1=float(TB[b]),
                                                       scalar2=delT[:, b:b + 1], op0=ALU.is_ge, op1=ALU.mult),
                 reads=['dd', 'delT'], writes=['vtmp'])
            P.op('pool', lambda e: e.tensor_tensor(out=vacc[:, :], in0=vacc[:, :], in1=vtmp[:, :], op=ALU.add),
                 reads=['vacc', 'vtmp'], writes=['vacc'])
        for kind in range(2):
            P.op('dve', lambda e: e.tensor_scalar(out=vm[:, :], in0=dd[:, :], scalar1=0.0, scalar2=None, op0=ALU.is_ge),
                 reads=['dd'], writes=['vm'])
            if kind == 1:
                P.op('dve', lambda e: e.tensor_scalar(out=vtmp[:, :], in0=dd[:, :], scalar1=512.0, scalar2=None,
                                                      op0=ALU.is_le), reads=['dd'], writes=['vtmp'])
                P.op('dve', lambda e: e.tensor_tensor(out=vm[:, :], in0=vm[:, :], in1=vtmp[:, :], op=ALU.mult),
                     reads=['vm', 'vtmp'], writes=['vm'])
            P.op('dve', lambda e: e.tensor_tensor(out=vout[:, :], in0=vacc[:, :], in1=vm[:, :], op=ALU.mult),
                 reads=['vacc', 'vm'], writes=['vout'])
            P.op('dve', lambda e: e.tensor_scalar(out=vm[:, :], in0=vm[:, :], scalar1=-1.0, scalar2=BIG,
                                                  op0=ALU.add, op1=ALU.mult), reads=['vm'], writes=['vm'])
            P.op('dve', lambda e: e.tensor_tensor(out=vout[:, :], in0=vout[:, :], in1=vm[:, :], op=ALU.add),
                 reads=['vout', 'vm'], writes=['vout'])
            P.dma('sp', bias_scr[kind, :, :], vout[:, :], reads=['vout'], writes=['bias_scr'])
        tabB = P.sb('tabB', [128, 32, 16])
        delB = P.sb('delB', [128, 32, 16])
        P.dma('sp', tabB[:], relb_d.partition_broadcast(128), writes=['tabB'])
        P.op('dve', lambda e: e.tensor_tensor(out=delB[:, 1:32, :], in0=tabB[:, 1:32, :], in1=tabB[:, 0:31, :],
                                              op=ALU.subtract), reads=['tabB'], writes=['delB'])
        dci = P.sb('dci', [128, 64], I32)
        dc = P.sb('dc', [128, 64])
        cm = P.sb('cm', [128, 64])
        cmn = P.sb('cmn', [128, 64])
        ctmp = P.sb('ctmp', [128, 64])
        P.op('pool', lambda e: e.iota(out=dci[:, :], pattern=[[-32, 64]], base=1889, channel_multiplier=1), writes=['dci'])
        P.op('dve', lambda e: e.tensor_copy(out=dc[:, :], in_=dci[:, :]), reads=['dci'], writes=['dc'])
        P.op('dve', lambda e: e.tensor_scalar(out=cm[:, :], in0=dc[:, :], scalar1=0.0, scalar2=None, op0=ALU.is_ge),
             reads=['dc'], writes=['cm'])
        P.op('dve', lambda e: e.tensor_scalar(out=cmn[:, :], in0=cm[:, :], scalar1=-1.0, scalar2=BIG, op0=ALU.add,
                                              op1=ALU.mult), reads=['cm'], writes=['cmn'])
        for hq in range(16):
            P.op('dve', lambda e, hq=hq: e.tensor_scalar(out=Gc[:, hq, :], in0=dc[:, :], scalar1=0.0,
                                                         scalar2=tabB[:, 0, hq:hq + 1], op0=ALU.mult, op1=ALU.add),
                 reads=['dc', 'tabB'], writes=[f'Gc{hq}'])
            for b in range(1, 32):
                P.op('pool', lambda e, hq=hq, b=b: e.tensor_scalar(out=ctmp[:, :], in0=dc[:, :], scalar1=float(TB[b]),
                                                                   scalar2=delB[:, b, hq:hq + 1], op0=ALU.is_ge,
                                                                   op1=ALU.mult),
                     reads=['dc', 'delB'], writes=['ctmp'])
                P.op('pool', lambda e, hq=hq: e.tensor_tensor(out=Gc[:, hq, :], in0=Gc[:, hq, :], in1=ctmp[:, :],
                                                             op=ALU.add), reads=['ctmp', f'Gc{hq}'], writes=[f'Gc{hq}'])
            P.op('dve', lambda e, hq=hq: e.tensor_tensor(out=Gc[:, hq, :], in0=Gc[:, hq, :], in1=cm[:, :], op=ALU.mult),
                 reads=['cm', f'Gc{hq}'], writes=[f'Gc{hq}'])
            P.op('dve', lambda e, hq=hq: e.tensor_tensor(out=Gc[:, hq, :], in0=Gc[:, hq, :], in1=cmn[:, :], op=ALU.add),
                 reads=['cmn', f'Gc{hq}'], writes=[f'Gc{hq}'])
        P.barrier()
        esC0.close()
        P.cur = esC
        w1c = P.sb('w1c', [64, 2, 32, 128])
        w2c = P.sb('w2c', [128, 2, 64])
        peT = P.sb('peT', [64, 2, 32])
        cbc = P.sb('cbc', [128, 2])
        for kv in range(2):
            P.dma('sp', w1c[:, kv, :, :], cw1_d[kv].rearrange('j d f -> d j f'), writes=['w1c'])
            P.dma('sp', w2c[:, kv, :], cw2_d[kv], writes=['w2c'])
        P.dma('sp', peT[:], peT_d, writes=['peT'])
        for kv in range(2):
            ps, pn = getps()
            for j in range(32):
                P.op('pe', lambda e, kv=kv, j=j, ps=ps: e.matmul(ps[:, 0:1], lhsT=w1c[:, kv, j, :], rhs=peT[:, kv, j:j + 1],
                                                                start=(j == 0), stop=(j == 31)),
                     reads=['w1c', 'peT'], writes=[pn])
            P.op('dve', lambda e, kv=kv, ps=ps: e.tensor_copy(out=cbc[:, kv:kv + 1], in_=ps[:, 0:1]), reads=[pn],
                 writes=['cbc'])
        H4 = [P.sb('H4', [128, WH]) for _ in range(4)]
        kTs = P.sb('kTs', [64, 2048])
        kTw = P.sb('kTw', [64, 2048])
        qT4 = [P.sb('qT4', [64, 2048]) for _ in range(4)]
        Vs = P.sb('Vs', [128, 16, 65])
        Vw = P.sb('Vw', [128, 16, 65])
        oacc = P.sb('oacc', [128, 16, 256])
        nmT = P.sb('nmT', [32, 2048])
        stk = [P.sb('stk', [128, 16, 64]) for _ in range(3)]
        gts = P.sb('gts', [128, 16, 48])
        hidT = P.sb('hidT', [128, 64])
        kcr = P.sb('kcr', [64, 64])
        vcb = P.sb('vcb', [64, 64])
        kcT = P.sb('kcT', [64, 64])
        ksq = P.sb('ksq', [64, 64])
        kss = P.sb('kss', [64, 2])
        Sc = [P.sb('Sc', [128, 64]) for _ in range(2)]
        ssum = [P.sb('ssum', [128, 1]) for _ in range(2)]
        pTs = [P.sb('pTs', [64, 128]) for _ in range(2)]
        impa = P.sb('impa', [128, 64])
        scr = P.sb('scr', [128, 32])
        scw = P.sb('scw', [128, 32])
        m8 = P.sb('m8', [128, 16])
        selm = P.sb('selm', [128, 32])
        PT = [P.sb('PT', [128, 512]) for _ in range(3)]
        rden = [P.sb('rden', [128, 4]) for _ in range(2)]
        cC = {'stk': 0, 'Sc': 0, 'PT': 0, 'acc': 0}
        P.op('dve', lambda e: e.memset(Vs[:, :, 64:65], 1.0), writes=['Vs'])
        P.op('dve', lambda e: e.memset(Vw[:, :, 64:65], 1.0), writes=['Vw'])
        qkn1 = qkn

        def load_tok(col0, ncols, dst, dstname, tokbase):
            P.dma('sp', dst[:, :, :ncols], p_scr[tokbase:tokbase + SEQ, col0:col0 + ncols].rearrange('(k p) n -> p k n', p=128),
                  reads=['p_scr'], writes=[dstname])

        def transposeT(src, srcname, dstT, dstname, rev, cols=64, c0=0):
            mat, mname = (Jr, 'Jr') if rev else (ident, 'ident')
            for k4 in range(4):
                ps, pn = getps()
                for i in range(4):
                    k = k4 * 4 + i
                    P.op('pe', lambda e, k=k, i=i, ps=ps: e.transpose(out=ps[:cols, i * 128:(i + 1) * 128],
                                                                      in_=src[:, k, c0:c0 + cols], identity=mat[:, :]),
                         reads=[srcname, mname], writes=[pn])
                if k4 % 2 == 0:
                    P.op('dve', lambda e, k4=k4, ps=ps: e.tensor_copy(out=dstT[:cols, k4 * 512:(k4 + 1) * 512],
                                                                      in_=ps[:cols, :]), reads=[pn], writes=[dstname])
                else:
                    P.op('act', lambda e, k4=k4, ps=ps: e.copy(out=dstT[:cols, k4 * 512:(k4 + 1) * 512], in_=ps[:cols, :]),
                         reads=[pn], writes=[dstname])

        def build_V(src, srcname, V, vname):
            for k8 in range(2):
                ps, pn = getps()
                for i in range(8):
                    k = k8 * 8 + i
                    P.op('pe', lambda e, k=k, i=i, ps=ps: e.matmul(ps[:, i * 64:(i + 1) * 64], lhsT=Jr[:, :], rhs=src[:, k, 0:64],
                                                                  start=True, stop=True),
                         reads=[srcname, 'Jr'], writes=[pn])
                P.op('dve', lambda e, k8=k8, ps=ps: e.tensor_copy(out=V[:, k8 * 8:(k8 + 1) * 8, 0:64],
                                                                  in_=ps[:, :].rearrange('p (k d) -> p k d', k=8)),
                     reads=[pn], writes=[vname])

        def compress(kv, srcT, srcname, dst, dstname):
            ps, pn = getps()
            v3 = srcT[:, :].rearrange('d (n j) -> d j n', j=32)
            for j in range(32):
                P.op('pe', lambda e, j=j, ps=ps: e.matmul(ps[:, 0:64], lhsT=w1c[:, kv, j, :], rhs=v3[:, j, :],
                                                         start=(j == 0), stop=(j == 31)),
                     reads=['w1c', srcname], writes=[pn])
            P.op('act', lambda e, ps=ps: e.activation(out=hidT[:, :], in_=ps[:, 0:64], func=AF.Silu, bias=cbc[:, kv:kv + 1]),
                 reads=[pn, 'cbc'], writes=['hidT'])
            ps2, pn2 = getps()
            P.op('pe', lambda e, ps2=ps2: e.matmul(ps2[:64, 0:64], lhsT=hidT[:, :], rhs=w2c[:, kv, :], start=True, stop=True),
                 reads=['hidT', 'w2c'], writes=[pn2])
            P.op('dve', lambda e, ps2=ps2: e.tensor_copy(out=dst[:, :], in_=ps2[:64, 0:64]), reads=[pn2], writes=[dstname])

        def attn_T(h, kind, kT, kTname, V, vname, gcol):
            for g in range(4):
                P.dma('sp', H4[g][:, :], bass.AP(tensor=bias_scr.tensor, offset=bias_scr[kind, h * 4 + g, :].offset,
                                                 ap=[[1, 128], [1, WH]]), reads=['bias_scr'], writes=[f'H4{g}'])
            for qc in range(4):
                kt_lo = 0 if kind == 0 else max(0, 4 * qc - 4)
                kt_hi = 4 * qc + 3
                for g in range(4):
                    ai = cC['acc']
                    cC['acc'] = (ai + 1) % 2
                    acc, accn = pacc[ai], f'pacc{ai}'
                    for kt in range(kt_lo, kt_hi + 1):
                        c0 = qc * 512 - kt * 128 + PADC
                        ps, pn = getps()
                        P.op('pe', lambda e, kt=kt, g=g, qc=qc, ps=ps: e.matmul(
                            ps[:, :], lhsT=kT[:, kt * 128:(kt + 1) * 128], rhs=qT4[g][:, qc * 512:(qc + 1) * 512],
                            start=True, stop=False), reads=[kTname, f'qT{g}'], writes=[pn])
                        P.op('pe', lambda e, g=g, c0=c0, ps=ps: e.matmul(
                            ps[:, :], lhsT=ident[:, :], rhs=H4[g][:, c0:c0 + 512], start=False, stop=(kind == 1)),
                            reads=['ident', f'H4{g}'], writes=[pn])
                        if kind == 0:
                            P.op('pe', lambda e, kt=kt, qc=qc, ps=ps: e.matmul(
                                ps[:, :], lhsT=Es[:, kt, :], rhs=nmT[:, qc * 512:(qc + 1) * 512], start=False, stop=True),
                                reads=['Es', 'nmT'], writes=[pn])
                        pi = cC['PT']
                        cC['PT'] = (pi + 1) % 3
                        P.op('act', lambda e, pi=pi, ps=ps: e.activation(out=PT[pi][:, :], in_=ps[:, :], func=AF.Exp),
                             reads=[pn], writes=[f'PT{pi}'])
                        for sub in range(4):
                            P.op('pe', lambda e, sub=sub, pi=pi, kt=kt, acc=acc, kt_lo=kt_lo, kt_hi=kt_hi: e.matmul(
                                acc[:, sub * 65:(sub + 1) * 65], lhsT=PT[pi][:, sub * 128:(sub + 1) * 128], rhs=V[:, kt, :],
                                start=(kt == kt_lo and sub == 0), stop=(kt == kt_hi), skip_group_check=True),
                                reads=[f'PT{pi}', vname], writes=[accn])
                    ri = ai
                    a3 = acc[:, 0:260].rearrange('p (s c) -> p s c', s=4)
                    P.op('dve', lambda e, a3=a3, ri=ri: e.reciprocal(out=rden[ri][:, :].unsqueeze(2), in_=a3[:, :, 64:65]),
                         reads=[accn], writes=[f'rden{ri}'])
                    P.op('dve', lambda e, ri=ri, qc=qc, g=g: e.tensor_tensor(
                        out=rden[ri][:, :], in0=rden[ri][:, :],
                        in1=gts[:, qc * 4:(qc + 1) * 4, (h * 4 + g) * 3 + gcol], op=ALU.mult),
                        reads=[f'rden{ri}', 'gts'], writes=[f'rden{ri}'])
                    for sub in range(4):
                        qt = qc * 4 + sub
                        P.op('dve', lambda e, sub=sub, qt=qt, g=g, ri=ri, acc=acc: e.scalar_tensor_tensor(
                            out=oacc[:, qt, g * 64:(g + 1) * 64], in0=acc[:, sub * 65:sub * 65 + 64],
                            scalar=rden[ri][:, sub:sub + 1], in1=oacc[:, qt, g * 64:(g + 1) * 64],
                            op0=ALU.mult, op1=ALU.add), reads=[accn, f'rden{ri}', 'oacc'], writes=['oacc'])

        for h in range(4):
            for sq_ in range(n_ptiles * 512 // SEQ):
                tb = sq_ * SEQ
                load_tok(7712, 48, gts, 'gts', tb)
                P.op('act', lambda e: e.activation(out=gts[:, :, :], in_=gts[:, :, :], func=AF.Sigmoid), reads=['gts'],
                     writes=['gts'])
                for g in range(4):
                    load_tok(5152 + h * 256 + g * 64, 64, stk[2], 'stk2', tb)
                    transposeT(stk[2], 'stk2', qT4[g], f'qT{g}', False)
                load_tok(6176 + h * 64, 64, stk[0], 'stk0', tb)
                load_tok(6176 + 256 + h * 64, 64, stk[1], 'stk1', tb)
                transposeT(stk[0], 'stk0', kTs, 'kTs', False)
                transposeT(stk[1], 'stk1', kTw, 'kTw', False)
                compress(0, kTs, 'kTs', kcr, 'kcr')
                compress(1, kTw, 'kTw', vcb, 'vcb')
                P.op('pool', lambda e: e.tensor_tensor(out=ksq[:, :], in0=kcr[:, :], in1=kcr[:, :], op=ALU.mult),
                     reads=['kcr'], writes=['ksq'])
                P.op('dve', lambda e: e.tensor_reduce(out=kss[:, 0:1], in_=ksq[:, :], axis=AX.X, op=ALU.add),
                     reads=['ksq'], writes=['kss'])
                P.op('act', lambda e: e.activation(out=kss[:, 1:2], in_=kss[:, 0:1], func=AF.Sqrt, scale=1.0 / 64, bias=EPS),
                     reads=['kss'], writes=['kss'])
                P.op('dve', lambda e: e.reciprocal(out=kss[:, 1:2], in_=kss[:, 1:2]), reads=['kss'], writes=['kss'])
                P.op('dve', lambda e: e.scalar_tensor_tensor(out=kcr[:, :], in0=kcr[:, :], scalar=kss[:, 1:2],
                                                             in1=qkn1[:64, 1, :], op0=ALU.mult, op1=ALU.mult),
                     reads=['kcr', 'kss', 'qkn'], writes=['kcr'])
                ps, pn = getps()
                P.op('pe', lambda e, ps=ps: e.transpose(out=ps[:64, 0:64], in_=kcr[:, :], identity=ident[:64, :64]),
                     reads=['kcr', 'ident'], writes=[pn])
                P.op('dve', lambda e, ps=ps: e.tensor_copy(out=kcT[:, :], in_=ps[:64, 0:64]), reads=[pn], writes=['kcT'])
                load_tok(6688 + h * 64, 64, stk[0], 'stk0', tb)
                load_tok(6688 + 256 + h * 64, 64, stk[1], 'stk1', tb)
                transposeT(stk[0], 'stk0', kTs, 'kTs', True)
                build_V(stk[1], 'stk1', Vs, 'Vs')
                load_tok(7200 + h * 64, 64, stk[0], 'stk0', tb)
                load_tok(7200 + 256 + h * 64, 64, stk[1], 'stk1', tb)
                transposeT(stk[0], 'stk0', kTw, 'kTw', True)
                build_V(stk[1], 'stk1', Vw, 'Vw')
                for qt in range(16):
                    ncol = 4 * qt + 4
                    P.op('pool', lambda e: e.memset(impa[:, :], 0.0), reads=['impa'], writes=['impa'])
                    for g in range(4):
                        hq = h * 4 + g
                        si = cC['Sc']
                        cC['Sc'] = (si + 1) % 2
                        ps, pn = getps()
                        P.op('pe', lambda e, g=g, qt=qt, ncol=ncol, ps=ps: e.matmul(
                            ps[:, 0:ncol], lhsT=qT4[g][:, qt * 128:(qt + 1) * 128], rhs=kcT[:, 0:ncol], start=True, stop=True),
                            reads=[f'qT{g}', 'kcT'], writes=[pn])
                        P.op('dve', lambda e, si=si, hq=hq, qt=qt, ncol=ncol, ps=ps: e.tensor_tensor(
                            out=Sc[si][:, 0:ncol], in0=ps[:, 0:ncol], in1=Gc[:, hq, 60 - 4 * qt:64], op=ALU.add),
                            reads=[pn, f'Gc{hq}'], writes=[f'Sc{si}'])
                        P.op('act', lambda e, si=si, ncol=ncol: e.activation(out=Sc[si][:, 0:ncol], in_=Sc[si][:, 0:ncol],
                                                                             func=AF.Exp, accum_out=ssum[si][:, 0:1]),
                             reads=[f'Sc{si}'], writes=[f'Sc{si}', f'ssum{si}'])
                        P.op('dve', lambda e, si=si: e.tensor_scalar(out=ssum[si][:, :], in0=ssum[si][:, :], scalar1=1e-30,
                                                                     scalar2=None, op0=ALU.max),
                             reads=[f'ssum{si}'], writes=[f'ssum{si}'])
                        P.op('dve', lambda e, si=si: e.reciprocal(out=ssum[si][:, :], in_=ssum[si][:, :]),
                             reads=[f'ssum{si}'], writes=[f'ssum{si}'])
                        P.op('dve', lambda e, si=si, ncol=ncol: e.tensor_scalar(out=Sc[si][:, 0:ncol], in0=Sc[si][:, 0:ncol],
                                                                                scalar1=ssum[si][:, 0:1], scalar2=None,
                                                                                op0=ALU.mult),
                             reads=[f'Sc{si}', f'ssum{si}'], writes=[f'Sc{si}'])
                        P.op('pool', lambda e, si=si, ncol=ncol: e.tensor_tensor(out=impa[:, 0:ncol], in0=impa[:, 0:ncol],
                                                                                 in1=Sc[si][:, 0:ncol], op=ALU.add),
                             reads=['impa', f'Sc{si}'], writes=['impa'])
                        ps2, pn2 = getps()
                        P.op('pe', lambda e, si=si, ncol=ncol, ps2=ps2: e.transpose(out=ps2[:ncol, 0:128], in_=Sc[si][:, 0:ncol],
                                                                                   identity=ident[:, :]),
                             reads=[f'Sc{si}', 'ident'], writes=[pn2])
                        P.op('act', lambda e, si=si, ncol=ncol, ps2=ps2: e.copy(out=pTs[si][:ncol, :], in_=ps2[:ncol, 0:128]),
                             reads=[pn2], writes=[f'pTs{si}'])
                        ps3, pn3 = getps()
                        P.op('pe', lambda e, si=si, ncol=ncol, ps3=ps3: e.matmul(ps3[:, 0:64], lhsT=pTs[si][:ncol, :],
                                                                                rhs=vcb[:ncol, :], start=True, stop=True),
                             reads=[f'pTs{si}', 'vcb'], writes=[pn3])
                        P.op('dve', lambda e, g=g, qt=qt, hq=hq, ps3=ps3: e.tensor_scalar(
                            out=oacc[:, qt, g * 64:(g + 1) * 64], in0=ps3[:, 0:64], scalar1=gts[:, qt, hq * 3:hq * 3 + 1],
                            scalar2=(None if 'c' in branches else 0.0), op0=ALU.mult,
                            **({} if 'c' in branches else {'op1': ALU.mult})), reads=[pn3, 'gts'], writes=['oacc'])
                    iv = impa[:, :].rearrange('p (b two) -> p b two', two=2)
                    P.op('dve', lambda e, iv=iv: e.tensor_tensor(out=scr[:, :].unsqueeze(2), in0=iv[:, :, 0:1], in1=iv[:, :, 1:2],
                                                                 op=ALU.add), reads=['impa'], writes=['scr'])
                    P.op('dve', lambda e, qt=qt: e.tensor_scalar(out=scw[:, :], in0=dblk[:, :], scalar1=float(-qt * 128),
                                                                 scalar2=None, op0=ALU.is_ge), reads=['dblk'], writes=['scw'])
                    P.op('dve', lambda e: e.scalar_tensor_tensor(out=scr[:, :], in0=scr[:, :], scalar=1.0, in1=scw[:, :],
                                                                 op0=ALU.add, op1=ALU.mult), reads=['scr', 'scw'],
                         writes=['scr'])
                    P.op('dve', lambda e: e.tensor_scalar(out=scr[:, :], in0=scr[:, :], scalar1=-1.0, scalar2=None,
                                                          op0=ALU.add), reads=['scr'], writes=['scr'])
                    lo = max(2 * qt - 1, 0)
                    P.op('dve', lambda e, qt=qt, lo=lo: e.memset(scr[0:64, lo:2 * qt], 1e9) if 2 * qt - 1 >= 0
                         else e.memset(scr[0:64, 0:1], 1e9), reads=['scr'], writes=['scr'])
                    P.op('dve', lambda e, qt=qt: e.memset(scr[0:64, 2 * qt:2 * qt + 1], 2e9), reads=['scr'], writes=['scr'])
                    P.op('dve', lambda e, qt=qt: e.memset(scr[64:128, 2 * qt:2 * qt + 1], 1e9), reads=['scr'], writes=['scr'])
                    P.op('dve', lambda e, qt=qt: e.memset(scr[64:128, 2 * qt + 1:2 * qt + 2], 2e9), reads=['scr'],
                         writes=['scr'])
                    P.op('dve', lambda e: e.memset(scr[:, 0:1], 3e9), reads=['scr'], writes=['scr'])
                    P.op('dve', lambda e: e.max(out=m8[:, 0:8], in_=scr[:, :]), reads=['scr'], writes=['m8'])
                    P.op('dve', lambda e: e.match_replace(out=scw[:, :], in_to_replace=m8[:, 0:8], in_values=scr[:, :],
                                                          imm_value=-2.0), reads=['scr', 'm8'], writes=['scw'])
                    P.op('dve', lambda e: e.max(out=m8[:, 8:16], in_=scw[:, :]), reads=['scw'], writes=['m8'])
                    P.op('dve', lambda e: e.tensor_scalar(out=selm[:, :], in0=scr[:, :], scalar1=m8[:, 15:16], scalar2=None,
                                                          op0=ALU.is_ge), reads=['scr', 'm8'], writes=['selm'])
                    ps4, pn4 = getps()
                    P.op('pe', lambda e, ps4=ps4: e.transpose(out=ps4[:32, 0:128], in_=selm[:, :], identity=ident[:, :]),
                         reads=['selm', 'ident'], writes=[pn4])
                    P.op('dve', lambda e, qt=qt, ps4=ps4: e.tensor_scalar(out=nmT[:, qt * 128:(qt + 1) * 128],
                                                                         in0=ps4[:32, 0:128], scalar1=-1.0, scalar2=BIG,
                                                                         op0=ALU.add, op1=ALU.mult),
                         reads=[pn4], writes=['nmT'])
                if 's' in branches:
                    attn_T(h, 0, kTs, 'kTs', Vs, 'Vs', 1)
                if 'w' in branches:
                    attn_T(h, 1, kTw, 'kTw', Vw, 'Vw', 2)
                P.dma('pool', o_scr[tb:tb + SEQ, h * 256:(h + 1) * 256].rearrange('(k p) n -> p k n', p=128),
                      oacc[:, :, :], reads=['oacc'], writes=['o_scr'])
        P.barrier()
        esC.close()

    if not o_from_host:
        phase_C()

    def phase_C2():
        esS = ExitStack()
        P.cur = esS
        I32 = mybir.dt.int32
        pacc = [P.ps('paccS', [128, 512]) for _ in range(2)]
        E2 = P.sb('E2', [2, 128])
        Sel8 = P.sb('Sel8', [32, 8])
        P.dma('sp', E2[:], e2_d, writes=['E2'])
        P.dma('sp', Sel8[:], sel8_d, writes=['Sel8'])
        tabB = P.sb('tabBs', [128, 32, 16])
        delB = P.sb('delBs', [128, 32, 16])
        P.dma('sp', tabB[:], relb_d.partition_broadcast(128), writes=['tabB'])
        P.op('dve', lambda e: e.tensor_tensor(out=delB[:, 1:32, :], in0=tabB[:, 1:32, :], in1=tabB[:, 0:31, :],
                                              op=ALU.subtract), reads=['tabB'], writes=['delB'])
        tab32 = P.sb('tab32', [32, 4, 32])
        del32 = P.sb('del32', [32, 4, 32])
        for h in range(4):
            for g in range(4):
                P.dma('sp', tab32[g * 8:(g + 1) * 8, h, :], tabT_d[h * 4 + g].partition_broadcast(8), writes=['tab32'])
        P.op('dve', lambda e: e.tensor_tensor(out=del32[:, :, 1:32], in0=tab32[:, :, 1:32], in1=tab32[:, :, 0:31],
                                              op=ALU.subtract), reads=['tab32'], writes=['del32'])
        w1c = P.sb('w1cS', [64, 2, 32, 128])
        w2c = P.sb('w2cS', [128, 2, 64])
        peT = P.sb('peTS', [64, 2, 32])
        cbc = P.sb('cbcS', [128, 2])
        for kv in range(2):
            P.dma('sp', w1c[:, kv, :, :], cw1_d[kv].rearrange('j d f -> d j f'), writes=['w1c'])
            P.dma('sp', w2c[:, kv, :], cw2_d[kv], writes=['w2c'])
        P.dma('sp', peT[:], peT_d, writes=['peT'])
        for kv in range(2):
            ps, pn = getps()
            for j in range(32):
                P.op('pe', lambda e, kv=kv, j=j, ps=ps: e.matmul(ps[:, 0:1], lhsT=w1c[:, kv, j, :], rhs=peT[:, kv, j:j + 1],
                                                                start=(j == 0), stop=(j == 31)),
                     reads=['w1c', 'peT'], writes=[pn])
            P.op('dve', lambda e, kv=kv, ps=ps: e.tensor_copy(out=cbc[:, kv:kv + 1], in_=ps[:, 0:1]), reads=[pn],
                 writes=['cbc'])
        Gs32 = P.sb('Gs32', [32, 4, 512])
        FV = max(0, min(127, (16257 - TB[31]) // 128 + 1))
        NPV = 128 - FV
        Bs = P.sb('Bs', [128, 4, NPV + 1, 32])
        Bw = P.sb('Bw', [128, 4, 4, 32])
        Bn = P.sb('Bn', [8, 4, 32])
        Bnw = P.sb('Bnw', [8, 4, 32])
        es0 = ExitStack()
        P.cur = es0
        d8i = P.sb('d8i', [8, 512], I32)
        d8 = P.sb('d8', [8, 512])
        dq = P.sb('dq', [32, 512])
        tq = P.sb('tq', [32, 512])
        dsi = P.sb('dsi', [128, NPV * 8], I32)
        dsf = P.sb('dsf', [128, NPV * 8])
        tsf = P.sb('tsf', [128, NPV * 8])
        dwi = P.sb('dwi', [128, 32], I32)
        dwf = P.sb('dwf', [128, 32])
        twf = P.sb('twf', [128, 32])
        mwf = P.sb('mwf', [128, 32])
        nwf = P.sb('nwf', [128, 32])
        dni = P.sb('dni', [8, 8], I32)
        dnf = P.sb('dnf', [8, 8])
        tnf = P.sb('tnf', [8, 8])
        mnf = P.sb('mnf', [8, 8])
        nnf = P.sb('nnf', [8, 8])
        P.op('pool', lambda e: e.iota(out=d8i[:, :], pattern=[[-32, 512]], base=16353, channel_multiplier=1), writes=['d8i'])
        P.op('dve', lambda e: e.tensor_copy(out=d8[:, :], in_=d8i[:, :]), reads=['d8i'], writes=['d8'])
        for g in range(4):
            P.dma('sp', dq[g * 8:(g + 1) * 8, :], d8[:, :], reads=['d8'], writes=['dq'])
        P.op('pool', lambda e: e.iota(out=dsi[:, :], pattern=[[-128, NPV], [1, 8]], base=16384 - 128 * FV,
                                      channel_multiplier=-1), writes=['dsi'])
        P.op('dve', lambda e: e.tensor_copy(out=dsf[:, :], in_=dsi[:, :]), reads=['dsi'], writes=['dsf'])
        P.op('pool', lambda e: e.iota(out=dwi[:, :], pattern=[[-128, 4], [1, 8]], base=512, channel_multiplier=-1),
             writes=['dwi'])
        P.op('dve', lambda e: e.tensor_copy(out=dwf[:, :], in_=dwi[:, :]), reads=['dwi'], writes=['dwf'])
        P.op('dve', lambda e: e.tensor_scalar(out=mwf[:, :], in0=dwf[:, :], scalar1=512.0, scalar2=None, op0=ALU.is_le),
             reads=['dwf'], writes=['mwf'])
        P.op('dve', lambda e: e.tensor_scalar(out=nwf[:, :], in0=mwf[:, :], scalar1=-1.0, scalar2=BIG, op0=ALU.add,
                                              op1=ALU.mult), reads=['mwf'], writes=['nwf'])
        P.op('pool', lambda e: e.iota(out=dni[:, :], pattern=[[1, 8]], base=0, channel_multiplier=-1), writes=['dni'])
        P.op('dve', lambda e: e.tensor_copy(out=dnf[:, :], in_=dni[:, :]), reads=['dni'], writes=['dnf'])
        P.op('dve', lambda e: e.tensor_scalar(out=mnf[:, :], in0=dnf[:, :], scalar1=0.0, scalar2=None, op0=ALU.is_ge),
             reads=['dnf'], writes=['mnf'])
        P.op('dve', lambda e: e.tensor_scalar(out=nnf[:, :], in0=mnf[:, :], scalar1=-1.0, scalar2=BIG, op0=ALU.add,
                                              op1=ALU.mult), reads=['mnf'], writes=['nnf'])

        def gen(dst, dname, Dap, dn, tmp, tn, t0, dl, sn, e_tmp='pool', e_add='dve'):
            P.op('dve', lambda e: e.tensor_scalar(out=dst, in0=Dap, scalar1=0.0, scalar2=t0, op0=ALU.mult, op1=ALU.add),
                 reads=[dn, sn], writes=[dname])
            for b in range(1, 32):
                P.op(e_tmp, lambda e, b=b: e.tensor_scalar(out=tmp, in0=Dap, scalar1=float(TB[b]), scalar2=dl(b),
                                                           op0=ALU.is_ge, op1=ALU.mult), reads=[dn, sn], writes=[tn])
                P.op(e_add, lambda e: e.tensor_tensor(out=dst, in0=dst, in1=tmp, op=ALU.add), reads=[tn, dname],
                     writes=[dname])

        for h in range(4):
            gen(Gs32[:, h, :], 'Gs32', dq[:, :], 'dq', tq[:, :], 'tq', tab32[:, h, 0:1],
                lambda b, h=h: del32[:, h, b:b + 1], 'del32', 'pool', 'pool')
            for g in range(4):
                hq = h * 4 + g
                P.op('dve', lambda e, h=h, g=g, hq=hq: e.tensor_copy(out=Bs[:, h, 0, g * 8:(g + 1) * 8],
                                                                     in_=tabB[:, 31, hq:hq + 1].to_broadcast([128, 8])),
                     reads=['tabB'], writes=['Bs0'])
                gen(Bs[:, h, 1:, g * 8:(g + 1) * 8], 'Bs', dsf[:, :].rearrange('p (a r) -> p a r', r=8), 'dsf',
                    tsf[:, :].rearrange('p (a r) -> p a r', r=8), 'tsf', tabB[:, 0, hq:hq + 1],
                    lambda b, hq=hq: delB[:, b, hq:hq + 1], 'delB')
                bwv = Bw[:, h, :, g * 8:(g + 1) * 8]
                gen(bwv, 'Bw', dwf[:, :].rearrange('p (a r) -> p a r', r=8), 'dwf',
                    twf[:, :].rearrange('p (a r) -> p a r', r=8), 'twf', tabB[:, 0, hq:hq + 1],
                    lambda b, hq=hq: delB[:, b, hq:hq + 1], 'delB', 'dve', 'dve')
                P.op('dve', lambda e, bwv=bwv: e.tensor_tensor(out=bwv, in0=bwv, in1=mwf[:, :].rearrange('p (a r) -> p a r', r=8),
                                                             op=ALU.mult), reads=['Bw', 'mwf'], writes=['Bw'])
                P.op('dve', lambda e, bwv=bwv: e.tensor_tensor(out=bwv, in0=bwv, in1=nwf[:, :].rearrange('p (a r) -> p a r', r=8),
                                                             op=ALU.add), reads=['Bw', 'nwf'], writes=['Bw'])
                bnv = Bn[:, h, g * 8:(g + 1) * 8]
                gen(bnv, 'Bn', dnf[:, :], 'dnf', tnf[:, :], 'tnf', tabB[:8, 0, hq:hq + 1],
                    lambda b, hq=hq: delB[:8, b, hq:hq + 1], 'delB', 'dve', 'dve')
                P.op('dve', lambda e, bnv=bnv: e.tensor_tensor(out=bnv, in0=bnv, in1=mnf[:, :], op=ALU.mult),
                     reads=['Bn', 'mnf'], writes=['Bn'])
                P.op('dve', lambda e, bnv=bnv: e.tensor_tensor(out=bnv, in0=bnv, in1=nnf[:, :], op=ALU.add),
                     reads=['Bn', 'nnf'], writes=['Bn'])
        P.barrier()
        es0.close()
        P.cur = esS
        kbuf = P.sb('kbuf', [64, 2, 4, 1024])
        hidT = P.sb('hidTS', [128, 2, 4, 512])
        X = [P.sb('Xpg', [128, 512]) for _ in range(3)]
        kcT = P.sb('kcTS', [64, 4, 512])
        vcs = P.sb('vcsS', [128, 4, 4, 64])
        kcn = P.sb('kcnS', [128, 64])
        ksq = P.sb('ksqS', [128, 64])
        kss = P.sb('kssS', [128, 2])
        qrow = P.sb('qrow', [8, 1024])
        qTall = P.sb('qTall', [64, 4, 32])
        gT32 = P.sb('gT32', [32, 4, 4, 3])
        knrow = P.sb('knrow', [8, 2, 512])
        kTn = P.sb('kTn', [64, 2, 4, 8])
        Vn = P.sb('Vn', [8, 2, 4, 65])
        Scs = P.sb('ScsS', [32, 512])
        ssum = P.sb('ssumS', [32, 1])
        pT = P.sb('pTS', [128, 4, 32])
        scr = P.sb('scrS', [8, 256])
        scw = P.sb('scwS', [8, 256])
        m8 = P.sb('m8S', [8, 16])
        selm = P.sb('selmS', [8, 256])
        NM2 = P.sb('NM2', [2, 4, 128, 8])
        ksT = [P.sb('ksTS', [64, 4, 128]) for _ in range(2)]
        PT = [P.sb('PTS', [128, 128]) for _ in range(2)]
        Vaug = [P.sb('VaugS', [128, 4, 65]) for _ in range(2)]
        PTn = P.sb('PTn', [8, 128])
        ofin = P.sb('ofin', [32, 4, 64])
        rden = P.sb('rdenS', [32, 4])
        Wt = P.sb('Wt', [128, 4, 512])
        cS = {'X': 0, 'k': 0}
        for i in range(2):
            P.op('dve', lambda e, i=i: e.memset(Vaug[i][:, :, 64:65], 1.0), writes=[f'Vaug{i}'])
        P.op('dve', lambda e: e.memset(Vn[:, :, :, 64:65], 1.0), writes=['Vn'])

        ptb = P.sb('ptb', [128, 128], I32)
        ptf = P.sb('ptf', [128, 128])
        pgi = P.sb('pgi', [128, 1], I32)
        pgf = P.sb('pgf', [128, 2])
        idxh = P.sb('idxh', [128, 2, 128], I32)
        P.op('pool', lambda e: e.iota(out=pgi[:, :], pattern=[[0, 1]], base=0, channel_multiplier=2), writes=['pgi'])
        P.op('dve', lambda e: e.tensor_copy(out=pgf[:, 0:1], in_=pgi[:, :]), reads=['pgi'], writes=['pgf'])
        P.op('dve', lambda e: e.tensor_scalar(out=pgf[:, 1:2], in0=pgf[:, 0:1], scalar1=1.0, scalar2=None, op0=ALU.add),
             reads=['pgf'], writes=['pgf'])

        def page_index(b):
            P.dma('sp', ptb[:, :], pt_d[b].partition_broadcast(128), writes=['ptb'])
            P.op('dve', lambda e: e.tensor_copy(out=ptf[:, :], in_=ptb[:, :]), reads=['ptb'], writes=['ptf'])
            for half in range(2):
                P.op('dve', lambda e, half=half: e.tensor_scalar(out=idxh[:, half, :], in0=ptf[:, :], scalar1=256.0,
                                                                 scalar2=pgf[:, half:half + 1], op0=ALU.mult, op1=ALU.add),
                     reads=['ptf', 'pgf'], writes=['idxh'])

        def page_dma(dst, dstname, b, pg, c0):
            half = c0 // 512

            def fn(e):
                return e.indirect_dma_start(out=dst, out_offset=None, in_=cache_d[:, :],
                                            in_offset=bass.IndirectOffsetOnAxis(ap=idxh[:, half, pg:pg + 1], axis=0))
            d = P._deps(['idxh'], [dstname])
            i = P.rr['pool']
            P.rr['pool'] = (i + 1) % P.NDS
            key = f'pool_d{i}'
            if key not in P.sem:
                P._mk(key)
            if P.cnt[key] > d.get(key, 0):
                d[key] = P.cnt[key]
            P._commit('pool', d, fn, key, 16, ['idxh'], [dstname])

        def attn_pass(b, nkt, load_fn, bias_fn, mask, kn_idx, acc, accn, gcol):
            for kt in range(nkt):
                xt_, xn_, kc0, vc0 = load_fn(kt)
                ki = cS['k']
                cS['k'] = (ki + 1) % 2
                ps, pn = getps()
                for h in range(4):
                    P.op('pe', lambda e, h=h, ps=ps, xt_=xt_, kc0=kc0: e.transpose(
                        out=ps[:64, h * 128:(h + 1) * 128], in_=xt_[:, kc0 + h * 64:kc0 + (h + 1) * 64], identity=ident[:, :]),
                        reads=[xn_, 'ident'], writes=[pn])
                P.op('dve', lambda e, ki=ki, ps=ps: e.tensor_copy(out=ksT[ki][:, :, :],
                                                                  in_=ps[:64, :].rearrange('p (h k) -> p h k', h=4)),
                     reads=[pn], writes=[f'ksT{ki}'])
                P.op('pool', lambda e, ki=ki, xt_=xt_, vc0=vc0: e.tensor_copy(
                    out=Vaug[ki][:, :, 0:64], in_=xt_[:, vc0:vc0 + 256].rearrange('p (h d) -> p h d', h=4)),
                    reads=[xn_], writes=[f'Vaug{ki}'])
                ps2, pn2 = getps()
                for h in range(4):
                    P.op('pe', lambda e, h=h, ki=ki, ps2=ps2: e.matmul(ps2[:, h * 32:(h + 1) * 32], lhsT=ksT[ki][:, h, :],
                                                                      rhs=qTall[:, h, :], start=(h == 0), stop=False,
                                                                      skip_group_check=True),
                         reads=[f'ksT{ki}', 'qTall'], writes=[pn2])
                bap, bn_ = bias_fn(kt)
                P.op('pe', lambda e, ps2=ps2, bap=bap: e.matmul(ps2[:, 0:128], lhsT=ident[:, :], rhs=bap, start=False,
                                                               stop=(not mask), skip_group_check=True),
                     reads=['ident', bn_], writes=[pn2])
                if mask:
                    P.op('pe', lambda e, ps2=ps2, kt=kt: e.matmul(
                        ps2[:, 0:128], lhsT=E2[:, :], rhs=NM2[:, :, kt, :].unsqueeze(2).to_broadcast([2, 4, 4, 8]),
                        start=False, stop=True, skip_group_check=True), reads=['E2', 'NM2'], writes=[pn2])
                P.op('act', lambda e, ki=ki, ps2=ps2: e.activation(out=PT[ki][:, :], in_=ps2[:, 0:128], func=AF.Exp),
                     reads=[pn2], writes=[f'PT{ki}'])
                for h in range(4):
                    P.op('pe', lambda e, h=h, ki=ki, kt=kt: e.matmul(acc[:32, h * 65:(h + 1) * 65],
                                                                    lhsT=PT[ki][:, h * 32:(h + 1) * 32], rhs=Vaug[ki][:, h, :],
                                                                    start=(kt == 0 and h == 0), stop=False,
                                                                    skip_group_check=True),
                         reads=[f'PT{ki}', f'Vaug{ki}'], writes=[accn])
            ps3, pn3 = getps()
            for h in range(4):
                P.op('pe', lambda e, h=h, ps3=ps3: e.matmul(ps3[:8, h * 32:(h + 1) * 32], lhsT=kTn[:, kn_idx, h, :],
                                                           rhs=qTall[:, h, :], start=(h == 0), stop=False,
                                                           skip_group_check=True),
                     reads=['kTn', 'qTall'], writes=[pn3])
            P.op('pe', lambda e, ps3=ps3: e.matmul(ps3[:8, 0:128], lhsT=ident[:8, :8], rhs=Bn[:, :, :], start=False,
                                                   stop=True, skip_group_check=True),
                 reads=['ident', 'Bn'], writes=[pn3])
            P.op('act', lambda e, ps3=ps3: e.activation(out=PTn[:, :], in_=ps3[:8, 0:128], func=AF.Exp), reads=[pn3],
                 writes=['PTn'])
            for h in range(4):
                P.op('pe', lambda e, h=h: e.matmul(acc[:32, h * 65:(h + 1) * 65], lhsT=PTn[:, h * 32:(h + 1) * 32],
                                                   rhs=Vn[:, kn_idx, h, :], start=False, stop=(h == 3),
                                                   skip_group_check=True),
                     reads=['PTn', 'Vn'], writes=[accn])
            a3 = acc[:32, 0:260].rearrange('p (h c) -> p h c', h=4)
            P.op('dve', lambda e: e.reciprocal(out=rden[:, :].unsqueeze(2), in_=a3[:, :, 64:65]), reads=[accn],
                 writes=['rden'])
            P.op('dve', lambda e: e.tensor_tensor(out=rden[:, :], in0=rden[:, :], in1=gT32[:, :, 0, gcol], op=ALU.mult),
                 reads=['rden', 'gT32'], writes=['rden'])
            for h in range(4):
                P.op('dve', lambda e, h=h: e.scalar_tensor_tensor(out=ofin[:, h, :], in0=acc[:32, h * 65:h * 65 + 64],
                                                                  scalar=rden[:, h:h + 1], in1=ofin[:, h, :],
                                                                  op0=ALU.mult, op1=ALU.add),
                     reads=[accn, 'rden', 'ofin'], writes=['ofin'])

        for b in range(NS_SEQ_PC):
            tok0 = TP + b * DEC
            page_index(b)
            P.dma('sp', qrow[:, :], p_scr[tok0:tok0 + 8, 5152:6176], reads=['p_scr'], writes=['qrow'])
            ps, pn = getps()
            for hq in range(16):
                P.op('pe', lambda e, hq=hq, ps=ps: e.transpose(out=ps[:64, hq * 8:(hq + 1) * 8],
                                                              in_=qrow[:, hq * 64:(hq + 1) * 64], identity=ident[:8, :8]),
                     reads=['qrow', 'ident'], writes=[pn])
            P.op('dve', lambda e, ps=ps: e.tensor_copy(out=qTall[:, :, :], in_=ps[:64, 0:128].rearrange('p (h q) -> p h q', h=4)),
                 reads=[pn], writes=['qTall'])
            gsrc = p_scr[tok0:tok0 + 8, 7712:7760].rearrange('r (h g c) -> r h g c', h=4, g=4)
            for g in range(4):
                P.dma('sp', gT32[g * 8:(g + 1) * 8, :, 0, :], gsrc[:, :, g, :], reads=['p_scr'], writes=['gT32'])
            P.op('act', lambda e: e.activation(out=gT32[:, :, 0, :], in_=gT32[:, :, 0, :], func=AF.Sigmoid), reads=['gT32'],
                 writes=['gT32'])
            P.dma('sp', knrow[:, 0, :], p_scr[tok0:tok0 + 8, 6688:7200], reads=['p_scr'], writes=['knrow'])
            P.dma('sp', knrow[:, 1, :], p_scr[tok0:tok0 + 8, 7200:7712], reads=['p_scr'], writes=['knrow'])
            ps, pn = getps()
            for kn in range(2):
                for h in range(4):
                    P.op('pe', lambda e, kn=kn, h=h, ps=ps: e.transpose(out=ps[:64, (kn * 4 + h) * 8:(kn * 4 + h + 1) * 8],
                                                                       in_=knrow[:, kn, h * 64:(h + 1) * 64],
                                                                       identity=ident[:8, :8]),
                         reads=['knrow', 'ident'], writes=[pn])
            P.op('dve', lambda e, ps=ps: e.tensor_copy(out=kTn[:, :, :, :],
                                                       in_=ps[:64, 0:64].rearrange('p (k h r) -> p k h r', k=2, h=4)),
                 reads=[pn], writes=['kTn'])
            P.op('pool', lambda e: e.tensor_copy(out=Vn[:, :, :, 0:64],
                                                 in_=knrow[:, :, 256:512].rearrange('p k (h d) -> p k h d', h=4)),
                 reads=['knrow'], writes=['Vn'])
            for bt in range(16):
                for sl in range(8):
                    pg = bt * 8 + sl
                    xi = cS['X']
                    cS['X'] = (xi + 1) % 3
                    page_dma(X[xi][:, :], f'X{xi}', b, pg, 0)
                    for kv in range(2):
                        ps, pn = getps()
                        for h in range(4):
                            P.op('pe', lambda e, kv=kv, h=h, xi=xi, ps=ps: e.transpose(
                                out=ps[:64, h * 128:(h + 1) * 128], in_=X[xi][:, kv * 256 + h * 64:kv * 256 + (h + 1) * 64],
                                identity=ident[:, :]), reads=[f'X{xi}', 'ident'], writes=[pn])
                        eng = 'dve' if kv == 0 else 'act'
                        if kv == 0:
                            P.op('dve', lambda e, kv=kv, sl=sl, ps=ps: e.tensor_copy(
                                out=kbuf[:, kv, :, sl * 128:(sl + 1) * 128], in_=ps[:64, :].rearrange('p (h k) -> p h k', h=4)),
                                reads=[pn], writes=[f'kbuf{kv}'])
                        else:
                            P.op('act', lambda e, kv=kv, sl=sl, ps=ps: e.copy(
                                out=kbuf[:, kv, :, sl * 128:(sl + 1) * 128], in_=ps[:64, :].rearrange('p (h k) -> p h k', h=4)),
                                reads=[pn], writes=[f'kbuf{kv}'])
                for kv in range(2):
                    for h in range(4):
                        ps, pn = getps()
                        v3 = kbuf[:, kv, h, :].rearrange('d (n j) -> d j n', j=32)
                        for j in range(32):
                            P.op('pe', lambda e, kv=kv, j=j, ps=ps, v3=v3: e.matmul(ps[:, 0:32], lhsT=w1c[:, kv, j, :],
                                                                                   rhs=v3[:, j, :], start=(j == 0),
                                                                                   stop=(j == 31)),
                                 reads=['w1c', f'kbuf{kv}'], writes=[pn])
                        P.op('act', lambda e, kv=kv, h=h, bt=bt, ps=ps: e.activation(
                            out=hidT[:, kv, h, bt * 32:(bt + 1) * 32], in_=ps[:, 0:32], func=AF.Silu, bias=cbc[:, kv:kv + 1]),
                            reads=[pn, 'cbc'], writes=['hidT'])
            P.op('dve', lambda e: e.memset(ofin[:, :, :], 0.0), reads=['ofin'], writes=['ofin'])
            for h in range(4):
                for ch in range(4):
                    ps, pn = getps()
                    P.op('pe', lambda e, h=h, ch=ch, ps=ps: e.matmul(ps[:, 0:64], lhsT=hidT[:, 0, h, ch * 128:(ch + 1) * 128],
                                                                    rhs=w2c[:, 0, :], start=True, stop=True),
                         reads=['hidT', 'w2c'], writes=[pn])
                    P.op('dve', lambda e, ps=ps: e.tensor_copy(out=kcn[:, :], in_=ps[:, 0:64]), reads=[pn], writes=['kcn'])
                    P.op('pool', lambda e: e.tensor_tensor(out=ksq[:, :], in0=kcn[:, :], in1=kcn[:, :], op=ALU.mult),
                         reads=['kcn'], writes=['ksq'])
                    P.op('dve', lambda e: e.tensor_reduce(out=kss[:, 0:1], in_=ksq[:, :], axis=AX.X, op=ALU.add),
                         reads=['ksq'], writes=['kss'])
                    P.op('act', lambda e: e.activation(out=kss[:, 1:2], in_=kss[:, 0:1], func=AF.Sqrt, scale=1.0 / 64,
                                                       bias=EPS), reads=['kss'], writes=['kss'])
                    P.op('dve', lambda e: e.reciprocal(out=kss[:, 1:2], in_=kss[:, 1:2]), reads=['kss'], writes=['kss'])
                    P.op('dve', lambda e: e.scalar_tensor_tensor(out=kcn[:, :], in0=kcn[:, :], scalar=kss[:, 1:2],
                                                                 in1=qkn[:, 1, :], op0=ALU.mult, op1=ALU.mult),
                         reads=['kcn', 'kss', 'qkn'], writes=['kcn'])
                    ps2, pn2 = getps()
                    P.op('pe', lambda e, ps2=ps2: e.transpose(out=ps2[:64, 0:128], in_=kcn[:, :], identity=ident[:, :]),
                         reads=['kcn', 'ident'], writes=[pn2])
                    P.op('dve', lambda e, h=h, ch=ch, ps2=ps2: e.tensor_copy(out=kcT[:, h, ch * 128:(ch + 1) * 128],
                                                                            in_=ps2[:64, 0:128]), reads=[pn2],
                         writes=['kcT'])
                    ps3, pn3 = getps()
                    P.op('pe', lambda e, h=h, ch=ch, ps3=ps3: e.matmul(ps3[:, 0:64], lhsT=hidT[:, 1, h, ch * 128:(ch + 1) * 128],
                                                                      rhs=w2c[:, 1, :], start=True, stop=True),
                         reads=['hidT', 'w2c'], writes=[pn3])
                    P.op('act', lambda e, h=h, ch=ch, ps3=ps3: e.copy(out=vcs[:, h, ch, :], in_=ps3[:, 0:64]), reads=[pn3],
                         writes=['vcs'])
                ps, pn = getps()
                P.op('pe', lambda e, h=h, ps=ps: e.matmul(ps[:32, :], lhsT=qTall[:, h, :], rhs=kcT[:, h, :], start=True,
                                                         stop=True), reads=['qTall', 'kcT'], writes=[pn])
                P.op('dve', lambda e, h=h, ps=ps: e.tensor_tensor(out=Scs[:, :], in0=ps[:32, :], in1=Gs32[:, h, :], op=ALU.add),
                     reads=[pn, 'Gs32'], writes=['Scs'])
                P.op('act', lambda e: e.activation(out=Scs[:, :], in_=Scs[:, :], func=AF.Exp, accum_out=ssum[:, 0:1]),
                     reads=['Scs'], writes=['Scs', 'ssum'])
                P.op('dve', lambda e: e.reciprocal(out=ssum[:, :], in_=ssum[:, :]), reads=['ssum'], writes=['ssum'])
                P.op('dve', lambda e: e.tensor_scalar(out=Scs[:, :], in0=Scs[:, :], scalar1=ssum[:, 0:1], scalar2=None,
                                                      op0=ALU.mult), reads=['Scs', 'ssum'], writes=['Scs'])
                ps, pn = getps()
                P.op('pe', lambda e, ps=ps: e.matmul(ps[:8, :], lhsT=Sel8[:, :], rhs=Scs[:, :], start=True, stop=True),
                     reads=['Sel8', 'Scs'], writes=[pn])
                P.op('dve', lambda e, ps=ps: e.tensor_copy(out=scw[:, :], in_=ps[:8, 0:256]), reads=[pn], writes=['scw'])
                iv = ps[:8, :].rearrange('p (b two) -> p b two', two=2)
                P.op('dve', lambda e, iv=iv: e.tensor_copy(out=scw[:, :].unsqueeze(2), in_=iv[:, :, 0:1]), reads=[pn],
                     writes=['scw'])
                P.op('dve', lambda e, iv=iv: e.tensor_tensor(out=scr[:, :].unsqueeze(2), in0=scw[:, :].unsqueeze(2),
                                                             in1=iv[:, :, 1:2], op=ALU.add), reads=[pn, 'scw'],
                     writes=['scr'])
                P.op('dve', lambda e: e.memset(scr[:, 255:256], 2e9), reads=['scr'], writes=['scr'])
                P.op('dve', lambda e: e.memset(scr[:, 0:1], 3e9), reads=['scr'], writes=['scr'])
                P.op('dve', lambda e: e.max(out=m8[:, 0:8], in_=scr[:, :]), reads=['scr'], writes=['m8'])
                P.op('dve', lambda e: e.match_replace(out=scw[:, :], in_to_replace=m8[:, 0:8], in_values=scr[:, :],
                                                      imm_value=-2.0), reads=['scr', 'm8'], writes=['scw'])
                P.op('dve', lambda e: e.max(out=m8[:, 8:16], in_=scw[:, :]), reads=['scw'], writes=['m8'])
                P.op('dve', lambda e: e.tensor_scalar(out=selm[:, :], in0=scr[:, :], scalar1=m8[:, 14:15], scalar2=None,
                                                      op0=ALU.is_ge), reads=['scr', 'm8'], writes=['selm'])
                P.op('dve', lambda e: e.tensor_scalar(out=selm[:, :], in0=selm[:, :], scalar1=-1.0, scalar2=BIG, op0=ALU.add,
                                                      op1=ALU.mult), reads=['selm'], writes=['selm'])
                P.dma('sp', nm_scr[b, h, :, :], selm[:, :], reads=['selm'], writes=['nm_scr'])
                for r_ in range(8):
                    P.dma('sp', NM2[:, h, :, r_], nm_scr[b, h, r_, :].rearrange('(pg par) -> par pg', par=2),
                          reads=['nm_scr'], writes=['NM2'], allow_slow_non_contiguous=True)
                ps, pn = getps()
                for ch in range(4):
                    P.op('pe', lambda e, ch=ch, ps=ps: e.transpose(out=ps[:, ch * 32:(ch + 1) * 32],
                                                                  in_=Scs[:, ch * 128:(ch + 1) * 128], identity=ident[:32, :32]),
                         reads=['Scs', 'ident'], writes=[pn])
                P.op('act', lambda e, ps=ps: e.copy(out=pT[:, :, :], in_=ps[:, 0:128].rearrange('p (c q) -> p c q', c=4)),
                     reads=[pn], writes=['pT'])
                ps2, pn2 = getps()
                for ch in range(4):
                    P.op('pe', lambda e, h=h, ch=ch, ps2=ps2: e.matmul(ps2[:32, 0:64], lhsT=pT[:, ch, :], rhs=vcs[:, h, ch, :],
                                                                      start=(ch == 0), stop=(ch == 3)),
                         reads=['pT', 'vcs'], writes=[pn2])
                if 'c' in sbranches:
                    P.op('dve', lambda e, h=h, ps2=ps2: e.tensor_scalar(out=ofin[:, h, :], in0=ps2[:32, 0:64],
                                                                       scalar1=gT32[:, h, 0, 0:1], scalar2=None, op0=ALU.mult),
                         reads=[pn2, 'gT32'], writes=['ofin'])
            if 's' in sbranches:
                def load_sel(kt, b=b):
                    xi = cS['X']
                    cS['X'] = (xi + 1) % 3
                    page_dma(X[xi][:, :], f'X{xi}', b, kt, 512)
                    return X[xi], f'X{xi}', 0, 256
                attn_pass(b, 128, load_sel, lambda kt: (Bs[:, :, (0 if kt < FV else kt - FV + 1), :], 'Bs'), True, 0,
                          pacc[0], 'paccS0', 1)
            if 'w' in sbranches:
                P.dma('sp', Wt[:, :, :], cwin[b].rearrange('(k p) c -> p k c', p=128), writes=['Wt'])
                attn_pass(b, 4, lambda kt: (Wt[:, kt, :], 'Wt', 0, 256), lambda kt: (Bw[:, :, kt, :], 'Bw'), False, 1,
                          pacc[1], 'paccS1', 2)
            osrc = o_scr[tok0:tok0 + 8, :].rearrange('r (h g d) -> r h g d', h=4, g=4)
            for g in range(4):
                P.dma('pool', osrc[:, :, g, :], ofin[g * 8:(g + 1) * 8, :, :], reads=['ofin'], writes=['o_scr'])
        P.barrier()
        esS.close()

    if not o_from_host and do_sample:
        phase_C2()

    def phase_D():
        esD = ExitStack()
        P.cur = esD
        g2 = P.sb('g2', [128, 8])
        P.dma('sp', g2[:], g2_d, writes=['g2'])
        big = P.sb('bigD', [128, 22, 512])
        oT = P.sb('oTD', [128, 8, 512])
        mixT = P.sb('mixTD', [128, 8, 512])
        xT_ = P.sb('xTD', [128, 8, 512])
        xnT_ = P.sb('xnTD', [128, 8, 512])
        rstd_ = P.sb('rstdD', [128, 512])
        sg_ = [P.sb('sgD', [128, 512]) for _ in range(2)]
        wblk_ = [P.sb('wblkD', [128, 2, 8, 128]) for _ in range(NWB)]
        woblk_ = [P.sb('woblkD', [128, 22, 128]) for _ in range(2)]
        stg = [P.sb('stgD', [128, 4, 128]) for _ in range(3)]
        wsq = [P.sb('wsqD', [128, 16, 128]) for _ in range(2)]
        gA = P.sb('gAD', [128, 512])
        gB = P.sb('gBD', [128, 512])
        ya = P.sb('yaD', [128, 512])
        yout = P.sb('youtD', [128, 4, D])
        TD = dict(xT=xT_, xnT=xnT_, hT=big, sg=sg_, wblk=wblk_, woblk=woblk_, rstd=rstd_)
        cD = {'stg': 0, 'wsq': 0}

        def load_T(src_cols, rows, ns, dst, dstname, act_func=None):
            TT = ns * rows
            i = cD['stg']
            cD['stg'] = (i + 1) % 3
            st, stn = stg[i], f'stgD{i}'
            P.dma('sp', st[:rows, :ns, :], src_cols.rearrange('(s p) n -> p s n', p=rows), writes=[stn])
            ps, pn = getps()
            for s_ in range(ns):
                P.op('pe', lambda e, s_=s_, ps=ps, st=st: e.transpose(out=ps[:, s_ * rows:(s_ + 1) * rows],
                                                                      in_=st[:rows, s_, :], identity=ident[:rows, :rows]),
                     reads=[stn, 'ident'], writes=[pn])
            if act_func is None:
                P.op('dve', lambda e, ps=ps: e.tensor_copy(out=dst[:, :TT], in_=ps[:, :TT]), reads=[pn], writes=[dstname])
            else:
                P.op('act', lambda e, ps=ps: e.activation(out=dst[:, :TT], in_=ps[:, :TT], func=act_func),
                     reads=[pn], writes=[dstname])

        def tile_D(tok0, rows, ns, y_dst):
            TT = ns * rows
            for k in range(16):
                load_T(ys_scr[tok0:tok0 + TT, k * 128:(k + 1) * 128], rows, ns, big[:, k, :], f'hT{k}')
            for k in range(8):
                load_T(o_scr[tok0:tok0 + TT, k * 128:(k + 1) * 128], rows, ns, oT[:, k, :], f'oT{k}')
            P.dma('sp', xT_[:, :, :TT], x1_scr[:, :, tok0:tok0 + TT].rearrange('c p t -> p c t'), reads=['x1_scr'],
                  writes=['xT'])
            for n in range(8):
                i = cD['wsq']
                cD['wsq'] = (i + 1) % 2
                w_, wn_ = wsq[i], f'wsqD{i}'
                P.dma('sp', w_[:, :, :], wbs_d[:, n * 128:(n + 1) * 128].rearrange('(k p) n -> p k n', p=128),
                      writes=[wn_])
                pa_, pan_ = getps()
                for k in range(16):
                    P.op('pe', lambda e, k=k, pa_=pa_, w_=w_: e.matmul(pa_[:, :TT], lhsT=w_[:, k, :], rhs=big[:, k, :TT],
                                                                     start=(k == 0), stop=(k == 15)),
                         reads=[wn_, f'hT{k}'], writes=[pan_])
                load_T(p_scr[tok0:tok0 + TT, 7760 + n * 128:7760 + (n + 1) * 128], rows, ns, gA, 'gA', AF.Sigmoid)
                load_T(p_scr[tok0:tok0 + TT, 8784 + n * 128:8784 + (n + 1) * 128], rows, ns, gB, 'gB', AF.Sigmoid)
                P.op('dve', lambda e, pa_=pa_: e.tensor_tensor(out=ya[:, :TT], in0=pa_[:, :TT], in1=gA[:, :TT], op=ALU.mult),
                     reads=[pan_, 'gA'], writes=['ya'])
                i = cD['wsq']
                cD['wsq'] = (i + 1) % 2
                w2_, wn2_ = wsq[i], f'wsqD{i}'
                P.dma('sp', w2_[:, 0:8, :], wba_d[:, n * 128:(n + 1) * 128].rearrange('(k p) n -> p k n', p=128),
                      writes=[wn2_])
                pb2, pbn2 = getps()
                for k in range(8):
                    P.op('pe', lambda e, k=k, pb2=pb2, w2_=w2_: e.matmul(pb2[:, :TT], lhsT=w2_[:, k, :], rhs=oT[:, k, :TT],
                                                                       start=(k == 0), stop=(k == 7)),
                         reads=[wn2_, f'oT{k}'], writes=[pbn2])
                P.op('dve', lambda e, pb2=pb2: e.tensor_tensor(out=gB[:, :TT], in0=pb2[:, :TT], in1=gB[:, :TT], op=ALU.mult),
                     reads=[pbn2, 'gB'], writes=['gB'])
                P.op('pool', lambda e, n=n: e.tensor_tensor(out=mixT[:, n, :TT], in0=ya[:, :TT], in1=gB[:, :TT], op=ALU.add),
                     reads=['ya', 'gB'], writes=[f'mixT{n}'])
            for n in range(8):
                i = cD['wsq']
                cD['wsq'] = (i + 1) % 2
                w_, wn_ = wsq[i], f'wsqD{i}'
                P.dma('sp', w_[:, 0:8, :], wout_d[:, n * 128:(n + 1) * 128].rearrange('(k p) n -> p k n', p=128),
                      writes=[wn_])
                pm, pmn = getps()
                for k in range(8):
                    P.op('pe', lambda e, k=k, pm=pm, w_=w_: e.matmul(pm[:, :TT], lhsT=w_[:, k, :], rhs=mixT[:, k, :TT],
                                                                   start=(k == 0), stop=(k == 7)),
                         reads=[wn_, f'mixT{k}'], writes=[pmn])
                P.op('dve', lambda e, n=n, pm=pm: e.tensor_tensor(out=xT_[:, n, :TT], in0=xT_[:, n, :TT], in1=pm[:, :TT],
                                                                 op=ALU.add),
                     reads=[pmn, 'xT'], writes=['xT'])
            ffn(TD, w2i, w2o, g2, 'g2', TT)
            for s_ in range(ns):
                for hlf in range(2):
                    ps, pn = getps()
                    for c4 in range(4):
                        c = hlf * 4 + c4
                        P.op('pe', lambda e, c=c, c4=c4, s_=s_, ps=ps: e.transpose(
                            out=ps[:rows, c4 * 128:(c4 + 1) * 128], in_=xT_[:, c, s_ * rows:(s_ + 1) * rows],
                            identity=ident[:, :]),
                            reads=['xT', 'ident'], writes=[pn])
                    if hlf == 0:
                        P.op('act', lambda e, s_=s_, ps=ps: e.copy(out=yout[:rows, s_, 0:512], in_=ps[:rows, :]),
                             reads=[pn], writes=['yout'])
                    else:
                        P.op('dve', lambda e, s_=s_, ps=ps: e.tensor_copy(out=yout[:rows, s_, 512:1024], in_=ps[:rows, :]),
                             reads=[pn], writes=['yout'])
            P.dma('pool', y_dst.rearrange('(s p) d -> p s d', p=rows), yout[:rows, :ns, :], reads=['yout'], writes=['o_y'])

        if do_sample:
            tile_D(TP, TS, 1, o_ys)
        for it in range(n_ptiles):
            tile_D(it * 512, 128, 4, o_yp[it * 512:(it + 1) * 512, :])
        P.barrier()
        esD.close()

    phase_D()

    P.finish()
    P.emit()
    es.close()
    return nc


def _gl(g):
    return np.ascontiguousarray(np.asarray(g, np.float32).reshape(8, 128).T)


def _e2():
    E = np.zeros((2, 128), np.float32)
    E[0, 0:64] = 1.0
    E[1, 64:128] = 1.0
    return E


def _esel():
    E = np.zeros((32, 16, 128), np.float32)
    for kt in range(16):
        E[2 * kt, kt, 64:128] = 1.0
        E[2 * kt + 1, kt, 0:64] = 1.0
    return E


def make_in_maps(inp, cores):
    ident = np.eye(128, dtype=np.float32)
    qkn = np.ascontiguousarray(np.broadcast_to(inp['qk_norm'][0].reshape(1, 256), (128, 256))).astype(np.float32)
    maps = []
    for j in cores:
        maps.append({
            'xp': np.ascontiguousarray(inp['x_prompt'][2 * j:2 * j + 2].reshape(TP, D)),
            'xs': np.ascontiguousarray(inp['x_sample'][4 * j:4 * j + 4].reshape(TS, D)),
            'ident': ident,
            'g_ffn1': _gl(inp['ffn1_norm'][0]),
            'g_mix': _gl(inp['mix_norm'][0]),
            'qkn': qkn,
            'ffn1_w_in': np.ascontiguousarray(inp['ffn1_w_in'][0]),
            'ffn1_w_out': np.ascontiguousarray(inp['ffn1_w_out'][0]),
            'w_in_proj': np.ascontiguousarray(inp['w_in_proj'][0]),
            'cache_win': np.ascontiguousarray(inp['cache_win'][0, 4 * j:4 * j + 4].reshape(4, 512, 512)),
            'tri': np.triu(np.ones((128, 128), np.float32)),
            'conv_w': np.ascontiguousarray(inp['conv_w'][0]),
            'conv_b': np.ascontiguousarray(inp['conv_b'][0]),
            'dt_bias': np.ascontiguousarray(inp['dt_bias'][0]),
            'a_log': np.ascontiguousarray(inp['a_log'][0]),
            'd_skip': np.ascontiguousarray(inp['d_skip'][0]),
            'ssm_norm': np.ascontiguousarray(inp['ssm_norm'][0]),
            'state_ssm': np.ascontiguousarray(inp['state_ssm'][0, 4 * j:4 * j + 4].reshape(4, 2048, 128)),
            'state_conv': np.ascontiguousarray(inp['state_conv'][0, 4 * j:4 * j + 4]),
            'w_branch_ssm': np.ascontiguousarray(inp['w_branch_ssm'][0]),
            'tabT': np.ascontiguousarray(inp['rel_bias'].T),
            'cache_kv': inp['cache_kv'].reshape(-1, 512),
            'page_table': np.ascontiguousarray(inp['page_table'][4 * j:4 * j + 4]).astype(np.int32),
            'Sel8': np.ascontiguousarray(np.tile(np.eye(8, dtype=np.float32), (4, 1))),
            'E2': _e2(),
            'rel_bias': np.ascontiguousarray(inp['rel_bias']),
            'Jrev': np.ascontiguousarray(np.eye(128, dtype=np.float32)[::-1]),
            'Esel': _esel(),
            'cmp_peT': np.ascontiguousarray(np.transpose(inp['cmp_pe'][0], (2, 0, 1))),
            'cmp_w1': np.ascontiguousarray(inp['cmp_w1'][0]),
            'cmp_w2': np.ascontiguousarray(inp['cmp_w2'][0]),
            'w_branch_attn': np.ascontiguousarray(inp['w_branch_attn'][0]),
            'w_out': np.ascontiguousarray(inp['w_out'][0]),
            'g_ffn2': _gl(inp['ffn2_norm'][0]),
            'ffn2_w_in': np.ascontiguousarray(inp['ffn2_w_in'][0]),
            'ffn2_w_out': np.ascontiguousarray(inp['ffn2_w_out'][0]),
        })
    return maps


def assemble(results, n):
    f32 = np.float32
    y_p = np.zeros((16, SEQ, D), f32)
    y_s = np.zeros((32, DEC, D), f32)
    kv_p = np.zeros((1, 16, SEQ, 4, 4, 64), f32)
    win_p = np.zeros((1, 16, 512, 2, 4, 64), f32)
    ssm_p = np.zeros((1, 16, 32, 64, 128), f32)
    conv_p = np.zeros((1, 16, 3, 3072), f32)
    kv_s = np.zeros((1, 32, DEC, 4, 4, 64), f32)
    win_s = np.zeros((1, 32, 512, 2, 4, 64), f32)
    ssm_s = np.zeros((1, 32, 32, 64, 128), f32)
    conv_s = np.zeros((1, 32, 3, 3072), f32)
    for j in range(n):
        r = results[j]
        kv_p[0, 2 * j:2 * j + 2] = r['o_kv_p'].reshape(2, SEQ, 4, 4, 64)
        win_p[0, 2 * j:2 * j + 2] = r['o_win_p'].reshape(2, 512, 2, 4, 64)
        conv_p[0, 2 * j:2 * j + 2] = r['o_conv_p'].reshape(2, 3, 3072)
        kv_s[0, 4 * j:4 * j + 4] = r['o_kv_s'].reshape(4, DEC, 4, 4, 64)
        win_s[0, 4 * j:4 * j + 4] = r['o_win_s'].reshape(4, 512, 2, 4, 64)
        conv_s[0, 4 * j:4 * j + 4] = r['o_conv_s'].reshape(4, 3, 3072)
        ssm_p[0, 2 * j:2 * j + 2] = r['o_ssm_p'].reshape(2, 32, 64, 128)
        y_p[2 * j:2 * j + 2] = r['o_y_p'].reshape(2, SEQ, D)
        y_s[4 * j:4 * j + 4] = r['o_y_s'].reshape(4, DEC, D)
        ssm_s[0, 4 * j:4 * j + 4] = r['o_ssm_s'].reshape(4, 32, 64, 128)
    return (y_p, y_s, kv_p, win_p, ssm_p, conv_p, kv_s, win_s, ssm_s, conv_s)


def kernel(**inp):
    inp = {k: np.asarray(v) for k, v in inp.items()}
    nc = build()
    maps = make_in_maps(inp, range(N_CORES))
    res = run_bass_kernel_spmd(nc, maps, core_ids=list(range(N_CORES)))
    return assemble(res.results, N_CORES)
```

```python
from contextlib import ExitStack

import numpy as np
import concourse.bass as bass
import concourse.mybir as mybir
from concourse.bass_utils import run_bass_kernel_spmd

F32 = mybir.dt.float32
BF16 = mybir.dt.bfloat16
AF = mybir.ActivationFunctionType
ALU = mybir.AluOpType
AX = mybir.AxisListType

N_CORES = 8
D = 1024
DFF = 2816
SEQ = 2048
NSEQ_PC = 2
TP = NSEQ_PC * SEQ
NS_SEQ_PC = 4
DEC = 8
TS = NS_SEQ_PC * DEC
DPROJ = 9808
EPS = 1e-6
PBLOCKS = ([(i * 512, 512, 'z') for i in range(4)]
           + [(2048 + i * 512, 512, 'xbc') for i in range(6)]
           + [(5120, 32, 'dt')]
           + [(5152 + i * 512, 512, 'q') for i in range(2)]
           + [(6176, 512, 'kv0'), (6688, 512, 'kv1'), (7200, 512, 'kv2')]
           + [(7712, 48, 'ng')]
           + [(7760 + i * 512, 512, 'mg') for i in range(4)])


class Prog:
    ENG = ('pe', 'dve', 'act', 'pool', 'sp')
    NDS = 6

    def __init__(self, nc, es):
        self.nc = nc
        self.es = es
        self.prog = {e: [] for e in self.ENG}
        self.cnt = {}
        self.sem = {}
        for e in self.ENG:
            self._mk(e)
        self.lastw = {}
        self.rd = {}
        self.seen = {e: {} for e in self.ENG}
        self.rr = {e: 0 for e in self.ENG}
        self.nbuf = 0
        self.cur = es
        self.reg = {}

    def _mk(self, key):
        self.sem[key] = self.es.enter_context(self.nc.semaphore('s_' + key))
        self.cnt[key] = 0

    def sb(self, name, shape, dtype=F32):
        self.nbuf += 1
        t = self.cur.enter_context(self.nc.sbuf_tensor(f'{name}_{self.nbuf}', list(shape), dtype))
        return t

    def ps(self, name, shape, dtype=F32):
        self.nbuf += 1
        t = self.cur.enter_context(self.nc.psum_tensor(f'{name}_{self.nbuf}', list(shape), dtype))
        return t

    def _deps(self, reads, writes):
        d = {}

        def add(k, v):
            if v > d.get(k, 0):
                d[k] = v
        for b in reads:
            if b in self.lastw:
                add(*self.lastw[b])
        for b in writes:
            if b in self.lastw:
                add(*self.lastw[b])
            for k, v in self.rd.get(b, {}).items():
                add(k, v)
        return d

    def _commit(self, eng, d, fn, key, inc, reads, writes):
        waits = []
        for k, v in d.items():
            if k == 'pe' and eng == 'pe':
                continue
            if self.seen[eng].get(k, 0) < v:
                self.seen[eng][k] = v
                waits.append((k, v))
        self.cnt[key] += inc
        val = self.cnt[key]
        self.prog[eng].append((waits, fn, key, inc))
        for b in writes:
            self.lastw[b] = (key, val)
            self.rd[b] = {}
        for b in reads:
            r = self.rd.setdefault(b, {})
            if r.get(key, 0) < val:
                r[key] = val

    def op(self, eng, fn, reads=(), writes=()):
        d = self._deps(reads, writes)
        self._commit(eng, d, fn, eng, 1, reads, writes)

    def dma(self, eng, out, in_, reads=(), writes=(), **kw):
        i = self.rr[eng]
        self.rr[eng] = (i + 1) % self.NDS
        key = f'{eng}_d{i}'
        if key not in self.sem:
            self._mk(key)
        d = self._deps(reads, writes)
        if self.cnt[key] > d.get(key, 0):
            d[key] = self.cnt[key]
        self._commit(eng, d, lambda e: e.dma_start(out=out, in_=in_, **kw), key, 16, reads, writes)

    def barrier(self):
        allw = [(k, v) for k, v in self.cnt.items() if v > 0]
        for e in self.ENG:
            waits = []
            for k, v in allw:
                if self.seen[e].get(k, 0) < v:
                    self.seen[e][k] = v
                    waits.append((k, v))
            self.prog[e].append((waits, None, None, 0))

    def finish(self):
        waits = [(k, v) for k, v in self.cnt.items() if '_d' in k and v > 0]
        self.prog['sp'].append((waits, None, None, 0))

    def emit(self):
        nc = self.nc
        with nc.Block() as block:
            def run(name, e):
                for waits, fn, key, inc in self.prog[name]:
                    for k, v in waits:
                        e.wait_ge(self.sem[k], v)
                    if fn is not None:
                        fn(e).then_inc(self.sem[key], inc)

            @block.tensor
            def _(e):
                run('pe', e)

            @block.vector
            def _(e):
                run('dve', e)

            @block.scalar
            def _(e):
                with e.register('pgreg2') as r:
                    self.reg['act'] = r
                    run('act', e)

            @block.gpsimd
            def _(e):
                run('pool', e)

            @block.sync
            def _(e):
                with e.register('pgreg') as r:
                    self.reg['sp'] = r
                    run('sp', e)


def build(n_ptiles=TP // 512, do_sample=True, o_from_host=False, dbg_o=False, branches='csw', npool=5120, sbranches='csw'):
    nc = bass.Bass("TRN2", target_bir_lowering=False)
    es = ExitStack()
    P = Prog(nc, es)

    def din(name, shape, dt=F32):
        return nc.dram_tensor(name, list(shape), dt, kind="ExternalInput").ap()

    def dout(name, shape, dt=F32):
        return nc.dram_tensor(name, list(shape), dt, kind="ExternalOutput").ap()

    xp = din('xp', [TP, D])
    xs = din('xs', [TS, D])
    ident_d = din('ident', [128, 128])
    g1_d = din('g_ffn1', [128, 8])
    gm_d = din('g_mix', [128, 8])
    qkn_d = din('qkn', [128, 4 * 64])
    w1i = din('ffn1_w_in', [D, 2 * DFF])
    w1o = din('ffn1_w_out', [DFF, D])
    wpj = din('w_in_proj', [D, DPROJ])
    cwin = din('cache_win', [NS_SEQ_PC, 512, 512])
    tri_d = din('tri', [128, 128])
    convw_d = din('conv_w', [4, 3072])
    convb_d = din('conv_b', [3072])
    dtb_d = din('dt_bias', [32])
    alog_d = din('a_log', [32])
    dsk_d = din('d_skip', [32])
    ssmn_d = din('ssm_norm', [2048])
    sssm_d = din('state_ssm', [NS_SEQ_PC, 2048, 128])
    sconv_d = din('state_conv', [NS_SEQ_PC, 3, 3072])
    tabT_d = din('tabT', [16, 32])
    cache_d = din('cache_kv', [npool * 256, 512])
    pt_d = din('page_table', [NS_SEQ_PC, 128], mybir.dt.int32)
    sel8_d = din('Sel8', [32, 8])
    e2_d = din('E2', [2, 128])
    relb_d = din('rel_bias', [32, 16])
    J_d = din('Jrev', [128, 128])
    E_d = din('Esel', [32, 16, 128])
    peT_d = din('cmp_peT', [64, 2, 32])
    cw1_d = din('cmp_w1', [2, 32, 64, 128])
    cw2_d = din('cmp_w2', [2, 128, 64])
    wbs_d = din('w_branch_ssm', [2048, D])
    wba_d = din('w_branch_attn', [D, D])
    wout_d = din('w_out', [D, D])
    g2_d = din('g_ffn2', [128, 8])
    w2i = din('ffn2_w_in', [D, 2 * DFF])
    w2o = din('ffn2_w_out', [DFF, D])

    o_kvp = dout('o_kv_p', [TP, 1024])
    o_winp = dout('o_win_p', [NSEQ_PC * 512, 512])
    o_convp = dout('o_conv_p', [NSEQ_PC * 3, 3072])
    o_kvs = dout('o_kv_s', [TS, 1024])
    o_wins = dout('o_win_s', [NS_SEQ_PC * 512, 512])
    o_convs = dout('o_conv_s', [NS_SEQ_PC * 3, 3072])
    o_ssmp = dout('o_ssm_p', [NSEQ_PC * 2048, 128])
    o_ssms = dout('o_ssm_s', [NS_SEQ_PC * 2048, 128])
    ys_scr = nc.dram_tensor('ys_scr', [TP + TS, 2048], F32, kind="ExternalOutput" if o_from_host else "Internal").ap()
    if o_from_host:
        o_scr = din('o_dbg', [TP + TS, D])
    else:
        o_scr = nc.dram_tensor('o_scr', [TP + TS, D], F32, kind="ExternalOutput" if dbg_o else "Internal").ap()
    bias_scr = nc.dram_tensor('bias_scr', [2, 16, 2560], F32, kind="Internal").ap()
    nm_scr = nc.dram_tensor('nm_scr', [NS_SEQ_PC, 4, 8, 256], F32, kind="Internal").ap()
    o_yp = dout('o_y_p', [TP, D])
    o_ys = dout('o_y_s', [TS, D])
    p_scr = nc.dram_tensor('p_scr', [TP + TS, DPROJ], F32, kind="Internal").ap()
    x1_scr = nc.dram_tensor('x1_scr', [8, 128, TP + TS], F32, kind="ExternalOutput" if o_from_host else "Internal").ap()

    ident = P.sb('ident', [128, 128])
    ones = P.sb('ones', [128, 128])
    g1 = P.sb('g1', [128, 8])
    gm = P.sb('gm', [128, 8])
    qkn = P.sb('qkn', [128, 4, 64])
    P.dma('sp', ident[:], ident_d, writes=['ident'])
    P.dma('sp', g1[:], g1_d, writes=['g1'])
    P.dma('sp', gm[:], gm_d, writes=['gm'])
    P.dma('sp', qkn[:], qkn_d.rearrange('p (a d) -> p a d', a=4), writes=['qkn'])
    P.op('dve', lambda e: e.memset(ones[:], 1.0), writes=['ones'])
    P.op('dve', lambda e: e.tensor_scalar(out=qkn[:, 0, :], in0=qkn[:, 0, :], scalar1=0.125, scalar2=None, op0=ALU.mult),
         reads=['qkn'], writes=['qkn'])

    NPS = 6
    psb = [P.ps('psb', [128, 512]) for _ in range(NPS)]
    tri = P.sb('tri', [128, 128])
    P.dma('sp', tri[:], tri_d, writes=['tri'])
    esA = ExitStack()
    P.cur = esA
    NT = 512
    xt = P.sb('xt', [128, 4, D])
    xT = P.sb('xT', [128, 8, NT])
    xnT = P.sb('xnT', [128, 8, NT], BF16)
    rstd = P.sb('rstd', [128, NT])
    hT = P.sb('hT', [128, 22, NT], BF16)
    sq = xt[:, :, :].rearrange('p s (c t) -> p (s c) t', t=NT)
    sg = [P.sb('sg', [128, NT]) for _ in range(2)]
    NWB = 3
    wblk = [P.sb('wblk', [128, 2, 8, 128]) for _ in range(NWB)]
    wblkb = [P.sb('wblkb', [128, 2, 8, 128], BF16) for _ in range(NWB)]
    woblk = [P.sb('woblk', [128, 22, 128]) for _ in range(2)]
    woblkb = [P.sb('woblkb', [128, 22, 128], BF16) for _ in range(2)]
    wpblk = [P.sb('wpblk', [128, 8, 512]) for _ in range(2)]
    wpblkb = [P.sb('wpblkb', [128, 8, 512], BF16) for _ in range(2)]
    pst = [P.sb('pst', [128, 4, 512]) for _ in range(2)]
    hsq = P.sb('hsq', [128, 512])
    hss = P.sb('hss', [128, 16])
    hrs = P.sb('hrs', [128, 16])
    TA = dict(xT=xT, xnT=xnT, hT=hT, sg=sg, wblk=wblk, woblk=woblk, rstd=rstd, sq=sq, sqn='xt', wblkb=wblkb,
              woblkb=woblkb)
    cnt = {'ps': 0, 'w': 0, 'wo': 0, 'wp': 0, 'pst': 0, 'sg': 0}

    def nxt(k, n):
        i = cnt[k]
        cnt[k] = (i + 1) % n
        return i

    def getps():
        i = nxt('ps', NPS)
        return psb[i], f'psb{i}'

    def rmsnorm_T(T, g, gname, TT):
        src, dst, sq, rstd = T['xT'], T['xnT'], T['sq'], T['rstd']
        sqn = T['sqn']
        for c in range(8):
            if c % 2 == 0:
                P.op('act', lambda e, c=c: e.activation(out=sq[:, c, :TT], in_=src[:, c, :TT], func=AF.Square),
                     reads=['xT'], writes=[sqn])
            else:
                P.op('pool', lambda e, c=c: e.tensor_tensor(out=sq[:, c, :TT], in0=src[:, c, :TT],
                                                           in1=src[:, c, :TT], op=ALU.mult),
                     reads=['xT'], writes=[sqn])
        ps, pn = getps()
        for c in range(8):
            P.op('pe', lambda e, c=c: e.matmul(ps[:, :TT], lhsT=ones[:], rhs=sq[:, c, :TT],
                                               start=(c == 0), stop=(c == 7)),
                 reads=['ones', sqn], writes=[pn])
        P.op('act', lambda e: e.activation(out=rstd[:, :TT], in_=ps[:, :TT], func=AF.Sqrt,
                                           scale=1.0 / D, bias=EPS),
             reads=[pn], writes=['rstd'])
        P.op('dve', lambda e: e.reciprocal(out=rstd[:, :TT], in_=rstd[:, :TT]),
             reads=['rstd'], writes=['rstd'])
        for c in range(8):
            P.op('dve', lambda e, c=c: e.scalar_tensor_tensor(out=dst[:, c, :TT], in0=src[:, c, :TT],
                                                            scalar=g[:, c:c + 1], in1=rstd[:, :TT],
                                                            op0=ALU.mult, op1=ALU.mult),
                 reads=['xT', gname, 'rstd'], writes=['xnT'])

    def ffn(T, w_in, w_out, g, gname, TT):
        xT, xnT, hT, sg, wblk, woblk = T['xT'], T['xnT'], T['hT'], T['sg'], T['wblk'], T['woblk']
        wblkb, woblkb = T['wblkb'], T['woblkb']
        rmsnorm_T(T, g, gname, TT)
        for j in range(22):
            wi = nxt('w', NWB)
            wb, wn = wblk[wi], f'wblk{wi}'
            P.dma('sp', wb[:, 0, :, :], w_in[:, j * 128:(j + 1) * 128].rearrange('(c p) n -> p c n', p=128),
                  writes=[wn + 'g'])
            P.dma('sp', wb[:, 1, :, :], w_in[:, DFF + j * 128:DFF + (j + 1) * 128].rearrange('(c p) n -> p c n', p=128),
                  writes=[wn + 'u'])
            wbb = wblkb[wi]
            P.op('pool', lambda e, wb=wb, wbb=wbb: e.tensor_copy(out=wbb[:, :, :, :], in_=wb[:, :, :, :]),
                 reads=[wn + 'g', wn + 'u'], writes=[wn + 'b'])
            pg, pgn = getps()
            pu, pun = getps()
            for c in range(8):
                P.op('pe', lambda e, c=c, pg=pg, wb=wbb: e.matmul(pg[:, :TT], lhsT=wb[:, 0, c, :], rhs=xnT[:, c, :TT],
                                                               start=(c == 0), stop=(c == 7)),
                     reads=[wn + 'b', 'xnT'], writes=[pgn])
            for c in range(8):
                P.op('pe', lambda e, c=c, pu=pu, wb=wbb: e.matmul(pu[:, :TT], lhsT=wb[:, 1, c, :], rhs=xnT[:, c, :TT],
                                                               start=(c == 0), stop=(c == 7)),
                     reads=[wn + 'b', 'xnT'], writes=[pun])
            si = nxt('sg', 2)
            P.op('act', lambda e, pg=pg, si=si: e.activation(out=sg[si][:, :TT], in_=pg[:, :TT], func=AF.Silu),
                 reads=[pgn], writes=[f'sg{si}'])
            P.op('dve', lambda e, pu=pu, si=si, j=j: e.tensor_tensor(out=hT[:, j, :TT], in0=sg[si][:, :TT],
                                                                     in1=pu[:, :TT], op=ALU.mult),
                 reads=[pun, f'sg{si}'], writes=[f'hT{j}'])
        for n in range(8):
            wi = nxt('wo', 2)
            wo, won = woblk[wi], f'woblk{wi}'
            P.dma('sp', wo[:], w_out[:, n * 128:(n + 1) * 128].rearrange('(j p) n -> p j n', p=128), writes=[won])
            wob = woblkb[wi]
            P.op('act', lambda e, wo=wo, wob=wob: e.copy(out=wob[:, :, :], in_=wo[:, :, :]), reads=[won], writes=[won + 'b'])
            py, pyn = getps()
            for j in range(22):
                P.op('pe', lambda e, j=j, py=py, wo=wob: e.matmul(py[:, :TT], lhsT=wo[:, j, :], rhs=hT[:, j, :TT],
                                                               start=(j == 0), stop=(j == 21)),
                     reads=[won + 'b', f'hT{j}'], writes=[pyn])
            P.op('dve', lambda e, n=n, py=py: e.scalar_tensor_tensor(out=xT[:, n, :TT], in0=py[:, :TT], scalar=0.5,
                                                                     in1=xT[:, n, :TT], op0=ALU.mult, op1=ALU.add),
                 reads=[pyn, 'xT'], writes=['xT'])

    def headnorm(t, tname, s_list, rows, ncol_heads, gidx, scale):
        nh = ncol_heads
        for s in s_list:
            v = t[:rows, s, 0:nh * 64]
            v3 = v.rearrange('p (h d) -> p h d', h=nh)
            P.op('pool', lambda e, v=v: e.tensor_tensor(out=hsq[:rows, 0:nh * 64], in0=v, in1=v, op=ALU.mult),
                 reads=[tname], writes=['hsq'])
            P.op('dve', lambda e: e.tensor_reduce(out=hss[:rows, 0:nh],
                                                  in_=hsq[:rows, 0:nh * 64].rearrange('p (h d) -> p h d', h=nh),
                                                  axis=AX.X, op=ALU.add),
                 reads=['hsq'], writes=['hss'])
            P.op('act', lambda e: e.activation(out=hrs[:rows, 0:nh], in_=hss[:rows, 0:nh], func=AF.Sqrt,
                                               scale=1.0 / 64, bias=EPS),
                 reads=['hss'], writes=['hrs'])
            P.op('dve', lambda e: e.reciprocal(out=hrs[:rows, 0:nh], in_=hrs[:rows, 0:nh]),
                 reads=['hrs'], writes=['hrs'])
            P.op('dve', lambda e, v3=v3: e.tensor_tensor(out=v3, in0=v3,
                                                         in1=hrs[:rows, 0:nh].unsqueeze(2).to_broadcast([rows, nh, 64]),
                                                         op=ALU.mult),
                 reads=[tname, 'hrs'], writes=[tname])
            P.op('pool', lambda e, v3=v3: e.tensor_tensor(
                out=v3, in0=v3, in1=qkn[:rows, gidx:gidx + 1, :].to_broadcast([rows, nh, 64]), op=ALU.mult),
                reads=[tname, 'qkn'], writes=[tname])

    def token_tile(x_rows, ns, rows, tok0, kind, seq_info):
        TT = ns * rows
        P.dma('sp', xt[:rows, :ns, :], x_rows.rearrange('(s p) d -> p s d', p=rows), writes=['xt'])
        for c in range(8):
            ps, pn = getps()
            for s in range(ns):
                P.op('pe', lambda e, c=c, s=s, ps=ps: e.transpose(out=ps[:, s * rows:(s + 1) * rows],
                                                                  in_=xt[:rows, s, c * 128:(c + 1) * 128],
                                                                  identity=ident[:rows, :rows]),
                     reads=['xt', 'ident'], writes=[pn])
            if c % 2 == 0:
                P.op('dve', lambda e, c=c, ps=ps: e.tensor_copy(out=xT[:, c, :TT], in_=ps[:, :TT]),
                     reads=[pn], writes=['xT'])
            else:
                P.op('act', lambda e, c=c, ps=ps: e.copy(out=xT[:, c, :TT], in_=ps[:, :TT]),
                     reads=[pn], writes=['xT'])
        ffn(TA, w1i, w1o, g1, 'g1', TT)
        P.dma('pool', x1_scr[:, :, tok0:tok0 + TT].rearrange('c p t -> p c t'), xT[:, :, :TT], reads=['xT'],
              writes=['x1_scr'])
        rmsnorm_T(TA, gm, 'gm', TT)
        for (c0, w, tag) in PBLOCKS:
            wi = nxt('wp', 2)
            wp, wpn = wpblk[wi], f'wpblk{wi}'
            P.dma('sp', wp[:, :, :w], wpj[:, c0:c0 + w].rearrange('(c p) n -> p c n', p=128), writes=[wpn])
            wpb = wpblkb[wi]
            if wi == 0:
                P.op('pool', lambda e, wp=wp, wpb=wpb, w=w: e.tensor_copy(out=wpb[:, :, :w], in_=wp[:, :, :w]), reads=[wpn],
                     writes=[wpn + 'b'])
            else:
                P.op('act', lambda e, wp=wp, wpb=wpb, w=w: e.copy(out=wpb[:, :, :w], in_=wp[:, :, :w]), reads=[wpn],
                     writes=[wpn + 'b'])
            pi = nxt('pst', 2)
            st, stn = pst[pi], f'pst{pi}'
            for s in range(ns):
                pp, ppn = getps()
                for c in range(8):
                    P.op('pe', lambda e, c=c, s=s, pp=pp, wp=wpb, w=w: e.matmul(pp[:rows, :w],
                                                                              lhsT=xnT[:, c, s * rows:(s + 1) * rows],
                                                                              rhs=wp[:, c, :w], start=(c == 0), stop=(c == 7)),
                         reads=[wpn + 'b', 'xnT'], writes=[ppn])
                if s % 2 == 0:
                    P.op('act', lambda e, s=s, pp=pp, st=st, w=w: e.copy(out=st[:rows, s, :w], in_=pp[:rows, :w]),
                         reads=[ppn], writes=[stn])
                else:
                    P.op('dve', lambda e, s=s, pp=pp, st=st, w=w: e.tensor_copy(out=st[:rows, s, :w], in_=pp[:rows, :w]),
                         reads=[ppn], writes=[stn])
            if tag == 'q':
                headnorm(st, stn, range(ns), rows, 8, 0, 0.125)
            elif tag == 'kv1':
                headnorm(st, stn, range(ns), rows, 4, 2, 1.0)
            elif tag == 'kv2':
                headnorm(st, stn, range(ns), rows, 4, 3, 1.0)
            P.dma('pool', p_scr[tok0:tok0 + TT, c0:c0 + w].rearrange('(s p) n -> p s n', p=rows),
                  st[:rows, :ns, :w], reads=[stn], writes=['p_scr'])
            if kind == 'prompt':
                seq, t0 = seq_info
                if tag in ('kv0', 'kv1'):
                    co = 0 if tag == 'kv0' else 512
                    r0 = seq * SEQ + t0
                    P.dma('pool', o_kvp[r0:r0 + TT, co:co + 512].rearrange('(s p) n -> p s n', p=rows),
                          st[:rows, :ns, :512], reads=[stn], writes=['o_kvp'])
                if tag == 'kv2' and t0 >= SEQ - 512:
                    r0 = seq * 512 + (t0 - (SEQ - 512))
                    P.dma('pool', o_winp[r0:r0 + TT, :].rearrange('(s p) n -> p s n', p=rows),
                          st[:rows, :ns, :512], reads=[stn], writes=['o_winp'])
                if tag == 'xbc' and t0 + TT == SEQ:
                    cc = c0 - 2048
                    P.dma('pool', o_convp[seq * 3:seq * 3 + 3, cc:cc + 512], st[rows - 3:rows, ns - 1, :512],
                          reads=[stn], writes=['o_convp'])
            else:
                if tag in ('kv0', 'kv1'):
                    co = 0 if tag == 'kv0' else 512
                    P.dma('pool', o_kvs[:, co:co + 512], st[:rows, 0, :512], reads=[stn], writes=['o_kvs'])
                if tag == 'kv2':
                    for b in range(NS_SEQ_PC):
                        P.dma('pool', o_wins[b * 512 + 504:b * 512 + 512, :], st[b * 8:b * 8 + 8, 0, :512],
                              reads=[stn], writes=['o_wins'])
                if tag == 'xbc':
                    cc = c0 - 2048
                    for b in range(NS_SEQ_PC):
                        P.dma('pool', o_convs[b * 3:b * 3 + 3, cc:cc + 512], st[b * 8 + 5:b * 8 + 8, 0, :512],
                              reads=[stn], writes=['o_convs'])

    if do_sample:
        for b in range(NS_SEQ_PC):
            P.dma('pool', o_wins[b * 512:b * 512 + 504, :], cwin[b, 8:512, :], writes=['o_wins'])
        token_tile(xs, 1, TS, TP, 'sample', None)
    for it in range(n_ptiles):
        tok = it * NT
        token_tile(xp[tok:tok + NT, :], 4, 128, tok, 'prompt', (tok // SEQ, tok % SEQ))

    P.barrier()
    esA.close()
    esB = ExitStack()
    P.cur = esB
    cw = P.sb('cw', [128, 4, 3072])
    cbias = P.sb('cbias', [128, 3072])
    dtb = P.sb('dtb', [128, 32])
    Aneg = P.sb('Aneg', [128, 32])
    dsk = P.sb('dsk', [128, 32])
    ssmn = P.sb('ssmn', [128, 2048])
    P.dma('sp', cw[:], convw_d.partition_broadcast(128), writes=['cw'])
    P.dma('sp', cbias[:], convb_d.partition_broadcast(128), writes=['cbias'])
    P.dma('sp', dtb[:], dtb_d.partition_broadcast(128), writes=['dtb'])
    P.dma('sp', Aneg[:], alog_d.partition_broadcast(128), writes=['Aneg'])
    P.dma('sp', dsk[:], dsk_d.partition_broadcast(128), writes=['dsk'])
    P.dma('sp', ssmn[:], ssmn_d.partition_broadcast(128), writes=['ssmn'])
    P.op('act', lambda e: e.activation(out=Aneg[:], in_=Aneg[:], func=AF.Exp), reads=['Aneg'], writes=['Aneg'])
    P.op('dve', lambda e: e.tensor_scalar(out=Aneg[:], in0=Aneg[:], scalar1=-1.0, scalar2=None, op0=ALU.mult),
         reads=['Aneg'], writes=['Aneg'])

    xsh = [P.sb('xsh', [128, 3072]) for _ in range(4)]
    xc = P.sb('xc', [128, 3072])
    zt = P.sb('zt', [128, 2048])
    yt = P.sb('yt', [128, 2048])
    xdt = P.sb('xdt', [128, 2048])
    xdd = P.sb('xdd', [128, 2048])
    hS = P.sb('hTs', [128, 2048])
    BT = P.sb('BT', [128, 4, 128])
    CT = P.sb('CT', [128, 4, 128])
    dtr = P.sb('dtr', [128, 32])
    dtt = P.sb('dtt', [128, 32])
    dA = P.sb('dA', [128, 32])
    acs = P.sb('acs', [128, 32])
    ea = P.sb('ea', [128, 32])
    dte = P.sb('dte', [128, 32])
    cd = P.sb('cd', [128, 32])
    cbTm = P.sb('cbTm', [128, 128])
    rj = [P.sb('rj', [128, 128]) for _ in range(2)]
    arg = [P.sb('arg', [128, 128]) for _ in range(2)]
    MT = [P.sb('MT', [128, 128]) for _ in range(2)]
    t1 = P.sb('t1', [128, 512])
    gss = P.sb('gss', [128, 4])
    grs = P.sb('grs', [128, 4])
    sst = P.sb('sst', [128, 16, 128])
    cntB = {'h': 0}
    pyo_t = P.ps('pyo', [128, 512])
    pyd_t = P.ps('pyd', [128, 512])

    def ssd_chunk(tok, rows, first, sconv, z_rows_tok):
        R = rows
        for k in range(4):
            nm = f'xsh{k}'
            if k == 0:
                P.dma('sp', xsh[0][:R, :], p_scr[tok:tok + R, 2048:5120], reads=['p_scr'], writes=[nm])
            elif first:
                if sconv is None:
                    P.op('pool', lambda e, k=k: e.memset(xsh[k][0:k, :], 0.0), writes=[nm])
                    P.dma('sp', xsh[k][k:R, :], p_scr[tok:tok + R - k, 2048:5120], reads=['p_scr'], writes=[nm + 'b'])
                else:
                    P.dma('sp', xsh[k][0:k, :], sconv[3 - k:3, :], writes=[nm])
                    P.dma('sp', xsh[k][k:R, :], p_scr[tok:tok + R - k, 2048:5120], reads=['p_scr'], writes=[nm + 'b'])
            else:
                P.dma('sp', xsh[k][:R, :], p_scr[tok - k:tok - k + R, 2048:5120], reads=['p_scr'], writes=[nm])
        P.dma('sp', dtr[:R, :], p_scr[tok:tok + R, 5120:5152], reads=['p_scr'], writes=['dtr'])
        P.dma('sp', zt[:R, :], p_scr[tok:tok + R, 0:2048], reads=['p_scr'], writes=['zt'])
        for k in range(4):
            P.op('pool', lambda e, k=k: e.tensor_tensor(out=xsh[k][:R, :], in0=xsh[k][:R, :], in1=cw[:R, 3 - k, :],
                                                        op=ALU.mult),
                 reads=[f'xsh{k}', f'xsh{k}b', 'cw'], writes=[f'xsh{k}'])
        P.op('dve', lambda e: e.tensor_tensor(out=xc[:R, :], in0=xsh[0][:R, :], in1=xsh[1][:R, :], op=ALU.add),
             reads=['xsh0', 'xsh1'], writes=['xc'])
        P.op('dve', lambda e: e.tensor_tensor(out=xc[:R, :], in0=xc[:R, :], in1=xsh[2][:R, :], op=ALU.add),
             reads=['xc', 'xsh2'], writes=['xc'])
        P.op('dve', lambda e: e.tensor_tensor(out=xc[:R, :], in0=xc[:R, :], in1=xsh[3][:R, :], op=ALU.add),
             reads=['xc', 'xsh3'], writes=['xc'])
        P.op('dve', lambda e: e.tensor_tensor(out=xc[:R, :], in0=xc[:R, :], in1=cbias[:R, :], op=ALU.add),
             reads=['xc', 'cbias'], writes=['xc'])
        P.op('act', lambda e: e.activation(out=xc[:R, :], in_=xc[:R, :], func=AF.Silu), reads=['xc'], writes=['xc'])
        P.op('dve', lambda e: e.tensor_tensor(out=dtt[:R, :], in0=dtr[:R, :], in1=dtb[:R, :], op=ALU.add),
             reads=['dtr', 'dtb'], writes=['dtt'])
        P.op('act', lambda e: e.activation(out=dtt[:R, :], in_=dtt[:R, :], func=AF.Exp), reads=['dtt'], writes=['dtt'])
        P.op('act', lambda e: e.activation(out=dtt[:R, :], in_=dtt[:R, :], func=AF.Ln, bias=1.0, scale=1.0),
             reads=['dtt'], writes=['dtt'])
        P.op('dve', lambda e: e.tensor_tensor(out=dA[:R, :], in0=dtt[:R, :], in1=Aneg[:R, :], op=ALU.mult),
             reads=['dtt', 'Aneg'], writes=['dA'])
        pa, pan = getps()
        P.op('pe', lambda e: e.matmul(pa[:R, 0:32], lhsT=tri[:R, :R], rhs=dA[:R, :], start=True, stop=True),
             reads=['tri', 'dA'], writes=[pan])
        pt, ptn = getps()
        P.op('pe', lambda e: e.matmul(pt[:, 0:32], lhsT=ones[:R, :], rhs=dA[:R, :], start=True, stop=True),
             reads=['ones', 'dA'], writes=[ptn])
        P.op('dve', lambda e: e.tensor_copy(out=acs[:R, :], in_=pa[:R, 0:32]), reads=[pan], writes=['acs'])
        P.op('act', lambda e: e.activation(out=ea[:R, :], in_=pa[:R, 0:32], func=AF.Exp), reads=[pan], writes=['ea'])
        P.op('dve', lambda e: e.tensor_tensor(out=dte[:R, :], in0=pt[:R, 0:32], in1=acs[:R, :], op=ALU.subtract),
             reads=[ptn, 'acs'], writes=['dte'])
        P.op('act', lambda e: e.activation(out=dte[:R, :], in_=dte[:R, :], func=AF.Exp), reads=['dte'], writes=['dte'])
        P.op('act', lambda e: e.activation(out=cd[:, :], in_=pt[:, 0:32], func=AF.Exp), reads=[ptn], writes=['cd'])
        P.op('dve', lambda e: e.tensor_tensor(out=xdt[:R, :].rearrange('p (j d) -> p j d', j=32),
                                              in0=xc[:R, 0:2048].rearrange('p (j d) -> p j d', j=32),
                                              in1=dtt[:R, :].unsqueeze(2).to_broadcast([R, 32, 64]), op=ALU.mult),
             reads=['xc', 'dtt'], writes=['xdt'])
        P.op('pool', lambda e: e.tensor_tensor(out=xdd[:R, :].rearrange('p (j d) -> p j d', j=32),
                                               in0=xdt[:R, :].rearrange('p (j d) -> p j d', j=32),
                                               in1=dte[:R, :].unsqueeze(2).to_broadcast([R, 32, 64]), op=ALU.mult),
             reads=['xdt', 'dte'], writes=['xdd'])
        pb_, pbn = getps()
        pc_, pcn = getps()
        for g in range(4):
            P.op('pe', lambda e, g=g: e.transpose(out=pb_[:, g * R:(g + 1) * R],
                                                  in_=xc[:R, 2048 + g * 128:2048 + (g + 1) * 128],
                                                  identity=ident[:R, :R]),
                 reads=['xc', 'ident'], writes=[pbn])
            P.op('pe', lambda e, g=g: e.transpose(out=pc_[:, g * R:(g + 1) * R],
                                                  in_=xc[:R, 2560 + g * 128:2560 + (g + 1) * 128],
                                                  identity=ident[:R, :R]),
                 reads=['xc', 'ident'], writes=[pcn])
        P.op('dve', lambda e: e.tensor_copy(out=BT[:, :, :R], in_=pb_[:, 0:4 * R].rearrange('p (g r) -> p g r', g=4)),
             reads=[pbn], writes=['BT'])
        P.op('act', lambda e: e.copy(out=CT[:, :, :R], in_=pc_[:, 0:4 * R].rearrange('p (g r) -> p g r', g=4)),
             reads=[pcn], writes=['CT'])
        for g in range(4):
            pcb, pcbn = getps()
            P.op('pe', lambda e, g=g, pcb=pcb: e.matmul(pcb[:R, :R], lhsT=BT[:, g, :R], rhs=CT[:, g, :R],
                                                       start=True, stop=True),
                 reads=['BT', 'CT'], writes=[pcbn])
            P.op('dve', lambda e, pcb=pcb: e.tensor_tensor(out=cbTm[:R, :R], in0=pcb[:R, :R], in1=tri[:R, :R],
                                                           op=ALU.mult),
                 reads=[pcbn, 'tri'], writes=['cbTm'])
            pyo, pyon = pyo_t, 'pyo_t'
            P.op('pe', lambda e, g=g, pyo=pyo: e.matmul(pyo[:R, :], lhsT=CT[:, g, :R], rhs=hS[:, g * 512:(g + 1) * 512],
                                                       start=True, stop=True),
                 reads=['CT', f'hS{g}'], writes=[pyon])
            pyd, pydn = pyd_t, 'pyd_t'
            for jj in range(8):
                j = g * 8 + jj
                hi = cntB['h']
                cntB['h'] = (hi + 1) % 2
                P.op('pool', lambda e, j=j, hi=hi: e.tensor_scalar(out=rj[hi][:R, :R], in0=tri[:R, :R],
                                                                   scalar1=dA[:R, j:j + 1], scalar2=None, op0=ALU.mult),
                     reads=['tri', 'dA'], writes=[f'rj{hi}'])
                pbc, pbcn = getps()
                P.op('pe', lambda e, hi=hi, pbc=pbc: e.matmul(pbc[:R, :R], lhsT=ones[:R, :R], rhs=rj[hi][:R, :R],
                                                             start=True, stop=True),
                     reads=['ones', f'rj{hi}'], writes=[pbcn])
                P.op('dve', lambda e, j=j, hi=hi, pbc=pbc: e.tensor_scalar(out=arg[hi][:R, :R], in0=pbc[:R, :R],
                                                                           scalar1=acs[:R, j:j + 1], scalar2=0.0,
                                                                           op0=ALU.subtract, op1=ALU.min),
                     reads=[pbcn, 'acs'], writes=[f'arg{hi}'])
                P.op('act', lambda e, hi=hi: e.activation(out=arg[hi][:R, :R], in_=arg[hi][:R, :R], func=AF.Exp),
                     reads=[f'arg{hi}'], writes=[f'arg{hi}'])
                P.op('pool', lambda e, hi=hi: e.tensor_tensor(out=MT[hi][:R, :R], in0=arg[hi][:R, :R],
                                                             in1=cbTm[:R, :R], op=ALU.mult),
                     reads=[f'arg{hi}', 'cbTm'], writes=[f'MT{hi}'])
                P.op('pe', lambda e, j=j, jj=jj, hi=hi, pyd=pyd: e.matmul(pyd[:R, jj * 64:(jj + 1) * 64],
                                                                         lhsT=MT[hi][:R, :R],
                                                                         rhs=xdt[:R, j * 64:(j + 1) * 64],
                                                                         start=True, stop=True),
                     reads=[f'MT{hi}', 'xdt'], writes=[pydn])
            P.op('dve', lambda e, g=g, pyo=pyo: e.tensor_tensor(
                out=t1[:R, :].rearrange('p (j d) -> p j d', j=8),
                in0=pyo[:R, :].rearrange('p (j d) -> p j d', j=8),
                in1=ea[:R, g * 8:(g + 1) * 8].unsqueeze(2).to_broadcast([R, 8, 64]), op=ALU.mult),
                reads=[pyon, 'ea'], writes=['t1'])
            P.op('dve', lambda e, g=g, pyd=pyd: e.tensor_tensor(out=yt[:R, g * 512:(g + 1) * 512], in0=t1[:R, :],
                                                               in1=pyd[:R, :], op=ALU.add),
                 reads=['t1', pydn], writes=[f'yt{g}'])
            pst_, pstn = getps()
            P.op('pe', lambda e, g=g, pst_=pst_: e.matmul(pst_[:, :], lhsT=xc[:R, 2048 + g * 128:2048 + (g + 1) * 128],
                                                         rhs=xdd[:R, g * 512:(g + 1) * 512], start=True, stop=True),
                 reads=['xc', 'xdd'], writes=[pstn])
            P.op('dve', lambda e, g=g: e.tensor_tensor(
                out=hS[:, g * 512:(g + 1) * 512].rearrange('p (j d) -> p j d', j=8),
                in0=hS[:, g * 512:(g + 1) * 512].rearrange('p (j d) -> p j d', j=8),
                in1=cd[:, g * 8:(g + 1) * 8].unsqueeze(2).to_broadcast([128, 8, 64]), op=ALU.mult),
                reads=[f'hS{g}', 'cd'], writes=[f'hS{g}'])
            P.op('dve', lambda e, g=g, pst_=pst_: e.tensor_tensor(out=hS[:, g * 512:(g + 1) * 512],
                                                                 in0=hS[:, g * 512:(g + 1) * 512], in1=pst_[:, :],
                                                                 op=ALU.add),
                 reads=[f'hS{g}', pstn], writes=[f'hS{g}'])
        ytn = [f'yt{g}' for g in range(4)]
        P.op('pool', lambda e: e.tensor_tensor(out=xdd[:R, :].rearrange('p (j d) -> p j d', j=32),
                                               in0=xc[:R, 0:2048].rearrange('p (j d) -> p j d', j=32),
                                               in1=dsk[:R, :].unsqueeze(2).to_broadcast([R, 32, 64]), op=ALU.mult),
             reads=['xc', 'dsk', 'xdd'], writes=['xdd'])
        P.op('pool', lambda e: e.tensor_tensor(out=yt[:R, :], in0=yt[:R, :], in1=xdd[:R, :], op=ALU.add),
             reads=ytn + ['xdd'], writes=ytn)
        P.op('act', lambda e: e.activation(out=zt[:R, :], in_=zt[:R, :], func=AF.Silu), reads=['zt'], writes=['zt'])
        P.op('dve', lambda e: e.tensor_tensor(out=yt[:R, :], in0=yt[:R, :], in1=zt[:R, :], op=ALU.mult),
             reads=ytn + ['zt'], writes=ytn)
        P.op('pool', lambda e: e.tensor_tensor(out=xdd[:R, :], in0=yt[:R, :], in1=yt[:R, :], op=ALU.mult),
             reads=ytn + ['xdd'], writes=['xdd'])
        P.op('dve', lambda e: e.tensor_reduce(out=gss[:R, :], in_=xdd[:R, :].rearrange('p (g d) -> p g d', g=4),
                                              axis=AX.X, op=ALU.add),
             reads=['xdd'], writes=['gss'])
        P.op('act', lambda e: e.activation(out=grs[:R, :], in_=gss[:R, :], func=AF.Sqrt, scale=1.0 / 512, bias=EPS),
             reads=['gss'], writes=['grs'])
        P.op('dve', lambda e: e.reciprocal(out=grs[:R, :], in_=grs[:R, :]), reads=['grs'], writes=['grs'])
        P.op('dve', lambda e: e.tensor_tensor(out=yt[:R, :].rearrange('p (g d) -> p g d', g=4),
                                              in0=yt[:R, :].rearrange('p (g d) -> p g d', g=4),
                                              in1=grs[:R, :].unsqueeze(2).to_broadcast([R, 4, 512]), op=ALU.mult),
             reads=ytn + ['grs'], writes=ytn)
        P.op('pool', lambda e: e.tensor_tensor(out=yt[:R, :], in0=yt[:R, :], in1=ssmn[:R, :], op=ALU.mult),
             reads=ytn + ['ssmn'], writes=ytn)
        P.dma('pool', ys_scr[tok:tok + R, :], yt[:R, :], reads=ytn, writes=['ys_scr'])

    def ssm_out(dst_rows):
        for a4 in range(4):
            ps, pn = getps()
            for i in range(4):
                a = a4 * 4 + i
                P.op('pe', lambda e, a=a, i=i, ps=ps: e.transpose(out=ps[:, i * 128:(i + 1) * 128],
                                                                  in_=hS[:, a * 128:(a + 1) * 128], identity=ident[:, :]),
                     reads=[f'hS{a // 4}', 'ident'], writes=[pn])
            P.op('act', lambda e, a4=a4, ps=ps: e.copy(out=sst[:, a4 * 4:(a4 + 1) * 4, :],
                                                      in_=ps[:, :].rearrange('p (i n) -> p i n', i=4)),
                 reads=[pn], writes=['sst'])
        P.dma('pool', dst_rows.rearrange('(a p) n -> p a n', p=128), sst[:, :, :], reads=['sst'], writes=['o_ssm'])

    hTn = [f'hS{g}' for g in range(4)]
    if do_sample:
        for b in range(NS_SEQ_PC):
            P.dma('sp', sst[:, :, :], sssm_d[b].rearrange('(a p) n -> p a n', p=128), writes=['sst'])
            for a4 in range(4):
                ps, pn = getps()
                for i in range(4):
                    a = a4 * 4 + i
                    P.op('pe', lambda e, a=a, i=i, ps=ps: e.transpose(out=ps[:, i * 128:(i + 1) * 128], in_=sst[:, a, :],
                                                                      identity=ident[:, :]),
                         reads=['sst', 'ident'], writes=[pn])
                P.op('dve', lambda e, a4=a4, ps=ps: e.tensor_copy(out=hS[:, a4 * 512:(a4 + 1) * 512], in_=ps[:, :]),
                     reads=[pn], writes=[f'hS{a4}'])
            ssd_chunk(TP + b * DEC, DEC, True, sconv_d[b], None)
            ssm_out(o_ssms[b * 2048:(b + 1) * 2048, :])
    for sq_ in range(n_ptiles * 512 // SEQ):
        P.op('dve', lambda e: e.memset(hS[:, :], 0.0), reads=hTn, writes=hTn)
        for c in range(SEQ // 128):
            ssd_chunk(sq_ * SEQ + c * 128, 128, c == 0, None, None)
        ssm_out(o_ssmp[sq_ * 2048:(sq_ + 1) * 2048, :])
    P.barrier()
    esB.close()


    BIG = 30000.0
    PADC = 384
    WH = 2432

    def bucket_thresholds():
        d = np.arange(0, 40000)
        df = np.maximum(d, 1).astype(np.float32)
        large = 16 + (np.log(df / np.float32(16)) / np.float32(np.log(2048 / 16)) * np.float32(16)).astype(np.int32)
        large = np.minimum(large, 31)
        bk = np.where(d < 16, d, large)
        return [int(np.argmax(bk >= b)) for b in range(32)]

    TB = bucket_thresholds()

    def phase_C():
        esC = ExitStack()
        P.cur = esC
        I32 = mybir.dt.int32
        pacc = [P.ps('pacc', [128, 512]) for _ in range(2)]
        Jr = P.sb('Jr', [128, 128])
        Es = P.sb('Es', [32, 16, 128])
        P.dma('sp', Jr[:], J_d, writes=['Jr'])
        P.dma('sp', Es[:], E_d, writes=['Es'])
        Gc = P.sb('Gc', [128, 16, 64])
        dblki = P.sb('dblki', [128, 32], I32)
        dblk = P.sb('dblk', [128, 32])
        P.op('pool', lambda e: e.iota(out=dblki[:, :], pattern=[[-64, 32]], base=0, channel_multiplier=1), writes=['dblki'])
        P.op('dve', lambda e: e.tensor_copy(out=dblk[:, :], in_=dblki[:, :]), reads=['dblki'], writes=['dblk'])
        esC0 = ExitStack()
        P.cur = esC0
        tabT = P.sb('tabT', [16, 32])
        delT = P.sb('delT', [16, 32])
        P.dma('sp', tabT[:], tabT_d, writes=['tabT'])
        P.op('dve', lambda e: e.tensor_tensor(out=delT[:, 1:32], in0=tabT[:, 1:32], in1=tabT[:, 0:31], op=ALU.subtract),
             reads=['tabT'], writes=['delT'])
        ddi = P.sb('ddi', [16, 2560], I32)
        dd = P.sb('dd', [16, 2560])
        vacc = P.sb('vacc', [16, 2560])
        vtmp = P.sb('vtmp', [16, 2560])
        vm = P.sb('vm', [16, 2560])
        vout = P.sb('vout', [16, 2560])
        P.op('pool', lambda e: e.iota(out=ddi[:, :], pattern=[[1, 2560]], base=-511, channel_multiplier=0), writes=['ddi'])
        P.op('dve', lambda e: e.tensor_copy(out=dd[:, :], in_=ddi[:, :]), reads=['ddi'], writes=['dd'])
        P.op('dve', lambda e: e.tensor_scalar(out=vacc[:, :], in0=dd[:, :], scalar1=0.0, scalar2=tabT[:, 0:1],
                                              op0=ALU.mult, op1=ALU.add), reads=['dd', 'tabT'], writes=['vacc'])
        for b in range(1, 32):
            P.op('dve', lambda e, b=b: e.tensor_scalar(out=vtmp[:, :], in0=dd[:, :], scalar1=float(TB[b]),
                                                       scalar2=delT[:, b:b + 1], op0=ALU.is_ge, op1=ALU.mult),
                 reads=['dd', 'delT'], writes=['vtmp'])
            P.op('pool', lambda e: e.tensor_tensor(out=vacc[:, :], in0=vacc[:, :], in1=vtmp[:, :], op=ALU.add),
                 reads=['vacc', 'vtmp'], writes=['vacc'])
        for kind in range(2):
            P.op('dve', lambda e: e.tensor_scalar(out=vm[:, :], in0=dd[:, :], scalar1=0.0, scalar2=None, op0=ALU.is_ge),
                 reads=['dd'], writes=['vm'])
            if kind == 1:
                P.op('dve', lambda e: e.tensor_scalar(out=vtmp[:, :], in0=dd[:, :], scalar1=512.0, scalar2=None,
                                                      op0=ALU.is_le), reads=['dd'], writes=['vtmp'])
                P.op('dve', lambda e: e.tensor_tensor(out=vm[:, :], in0=vm[:, :], in1=vtmp[:, :], op=ALU.mult),
                     reads=['vm', 'vtmp'], writes=['vm'])
            P.op('dve', lambda e: e.tensor_tensor(out=vout[:, :], in0=vacc[:, :], in1=vm[:, :], op=ALU.mult),
                 reads=['vacc', 'vm'], writes=['vout'])
            P.op('dve', lambda e: e.tensor_scalar(out=vm[:, :], in0=vm[:, :], scalar1=-1.0, scalar2=BIG,
                                                  op0=ALU.add, op1=ALU.mult), reads=['vm'], writes=['vm'])
            P.op('dve', lambda e: e.tensor_tensor(out=vout[:, :], in0=vout[:, :], in1=vm[:, :], op=ALU.add),
                 reads=['vout', 'vm'], writes=['vout'])
            P.dma('sp', bias_scr[kind, :, :], vout[:, :], reads=['vout'], writes=['bias_scr'])
        tabB = P.sb('tabB', [128, 32, 16])
        delB = P.sb('delB', [128, 32, 16])
        P.dma('sp', tabB[:], relb_d.partition_broadcast(128), writes=['tabB'])
        P.op('dve', lambda e: e.tensor_tensor(out=delB[:, 1:32, :], in0=tabB[:, 1:32, :], in1=tabB[:, 0:31, :],
                                              op=ALU.subtract), reads=['tabB'], writes=['delB'])
        dci = P.sb('dci', [128, 64], I32)
        dc = P.sb('dc', [128, 64])
        cm = P.sb('cm', [128, 64])
        cmn = P.sb('cmn', [128, 64])
        ctmp = P.sb('ctmp', [128, 64])
        P.op('pool', lambda e: e.iota(out=dci[:, :], pattern=[[-32, 64]], base=1889, channel_multiplier=1), writes=['dci'])
        P.op('dve', lambda e: e.tensor_copy(out=dc[:, :], in_=dci[:, :]), reads=['dci'], writes=['dc'])
        P.op('dve', lambda e: e.tensor_scalar(out=cm[:, :], in0=dc[:, :], scalar1=0.0, scalar2=None, op0=ALU.is_ge),
             reads=['dc'], writes=['cm'])
        P.op('dve', lambda e: e.tensor_scalar(out=cmn[:, :], in0=cm[:, :], scalar1=-1.0, scalar2=BIG, op0=ALU.add,
                                              op1=ALU.mult), reads=['cm'], writes=['cmn'])
        for hq in range(16):
            P.op('dve', lambda e, hq=hq: e.tensor_scalar(out=Gc[:, hq, :], in0=dc[:, :], scalar1=0.0,
                                                         scalar2=tabB[:, 0, hq:hq + 1], op0=ALU.mult, op1=ALU.add),
                 reads=['dc', 'tabB'], writes=[f'Gc{hq}'])
            for b in range(1, 32):
                P.op('pool', lambda e, hq=hq, b=b: e.tensor_scalar(out=ctmp[:, :], in0=dc[:, :], scalar1=float(TB[b]),
                                                                   scalar2=delB[:, b, hq:hq + 1], op0=ALU.is_ge,
                                                                   op1=ALU.mult),
                     reads=['dc', 'delB'], writes=['ctmp'])
                P.op('pool', lambda e, hq=hq: e.tensor_tensor(out=Gc[:, hq, :], in0=Gc[:, hq, :], in1=ctmp[:, :],
                                                             op=ALU.add), reads=['ctmp', f'Gc{hq}'], writes=[f'Gc{hq}'])
            P.op('dve', lambda e, hq=hq: e.tensor_tensor(out=Gc[:, hq, :], in0=Gc[:, hq, :], in1=cm[:, :], op=ALU.mult),
                 reads=['cm', f'Gc{hq}'], writes=[f'Gc{hq}'])
            P.op('dve', lambda e, hq=hq: e.tensor_tensor(out=Gc[:, hq, :], in0=Gc[:, hq, :], in1=cmn[:, :], op=ALU.add),
                 reads=['cmn', f'Gc{hq}'], writes=[f'Gc{hq}'])
        P.barrier()
        esC0.close()
        P.cur = esC
        w1c = P.sb('w1c', [64, 2, 32, 128])
        w2c = P.sb('w2c', [128, 2, 64])
        peT = P.sb('peT', [64, 2, 32])
        cbc = P.sb('cbc', [128, 2])
        for kv in range(2):
            P.dma('sp', w1c[:, kv, :, :], cw1_d[kv].rearrange('j d f -> d j f'), writes=['w1c'])
            P.dma('sp', w2c[:, kv, :], cw2_d[kv], writes=['w2c'])
        P.dma('sp', peT[:], peT_d, writes=['peT'])
        for kv in range(2):
            ps, pn = getps()
            for j in range(32):
                P.op('pe', lambda e, kv=kv, j=j, ps=ps: e.matmul(ps[:, 0:1], lhsT=w1c[:, kv, j, :], rhs=peT[:, kv, j:j + 1],
                                                                start=(j == 0), stop=(j == 31)),
                     reads=['w1c', 'peT'], writes=[pn])
            P.op('dve', lambda e, kv=kv, ps=ps: e.tensor_copy(out=cbc[:, kv:kv + 1], in_=ps[:, 0:1]), reads=[pn],
                 writes=['cbc'])
        H4 = [P.sb('H4', [128, WH]) for _ in range(4)]
        kTs = P.sb('kTs', [64, 2048])
        kTw = P.sb('kTw', [64, 2048])
        qT4 = [P.sb('qT4', [64, 2048]) for _ in range(4)]
        Vs = P.sb('Vs', [128, 16, 65])
        Vw = P.sb('Vw', [128, 16, 65])
        oacc = P.sb('oacc', [128, 16, 256])
        nmT = P.sb('nmT', [32, 2048])
        stk = [P.sb('stk', [128, 16, 64]) for _ in range(3)]
        gts = P.sb('gts', [128, 16, 48])
        hidT = P.sb('hidT', [128, 64])
        kcr = P.sb('kcr', [64, 64])
        vcb = P.sb('vcb', [64, 64])
        kcT = P.sb('kcT', [64, 64])
        ksq = P.sb('ksq', [64, 64])
        kss = P.sb('kss', [64, 2])
        Sc = [P.sb('Sc', [128, 64]) for _ in range(2)]
        ssum = [P.sb('ssum', [128, 1]) for _ in range(2)]
        pTs = [P.sb('pTs', [64, 128]) for _ in range(2)]
        impa = P.sb('impa', [128, 64])
        scr = P.sb('scr', [128, 32])
        scw = P.sb('scw', [128, 32])
        m8 = P.sb('m8', [128, 16])
        selm = P.sb('selm', [128, 32])
        PT = [P.sb('PT', [128, 512]) for _ in range(3)]
        rden = [P.sb('rden', [128, 4]) for _ in range(2)]
        cC = {'stk': 0, 'Sc': 0, 'PT': 0, 'acc': 0}
        P.op('dve', lambda e: e.memset(Vs[:, :, 64:65], 1.0), writes=['Vs'])
        P.op('dve', lambda e: e.memset(Vw[:, :, 64:65], 1.0), writes=['Vw'])
        qkn1 = qkn

        def load_tok(col0, ncols, dst, dstname, tokbase):
            P.dma('sp', dst[:, :, :ncols], p_scr[tokbase:tokbase + SEQ, col0:col0 + ncols].rearrange('(k p) n -> p k n', p=128),
                  reads=['p_scr'], writes=[dstname])

        def transposeT(src, srcname, dstT, dstname, rev, cols=64, c0=0):
            mat, mname = (Jr, 'Jr') if rev else (ident, 'ident')
            for k4 in range(4):
                ps, pn = getps()
                for i in range(4):
                    k = k4 * 4 + i
                    P.op('pe', lambda e, k=k, i=i, ps=ps: e.transpose(out=ps[:cols, i * 128:(i + 1) * 128],
                                                                      in_=src[:, k, c0:c0 + cols], identity=mat[:, :]),
                         reads=[srcname, mname], writes=[pn])
                if k4 % 2 == 0:
                    P.op('dve', lambda e, k4=k4, ps=ps: e.tensor_copy(out=dstT[:cols, k4 * 512:(k4 + 1) * 512],
                                                                      in_=ps[:cols, :]), reads=[pn], writes=[dstname])
                else:
                    P.op('act', lambda e, k4=k4, ps=ps: e.copy(out=dstT[:cols, k4 * 512:(k4 + 1) * 512], in_=ps[:cols, :]),
                         reads=[pn], writes=[dstname])

        def build_V(src, srcname, V, vname):
            for k8 in range(2):
                ps, pn = getps()
                for i in range(8):
                    k = k8 * 8 + i
                    P.op('pe', lambda e, k=k, i=i, ps=ps: e.matmul(ps[:, i * 64:(i + 1) * 64], lhsT=Jr[:, :], rhs=src[:, k, 0:64],
                                                                  start=True, stop=True),
                         reads=[srcname, 'Jr'], writes=[pn])
                P.op('dve', lambda e, k8=k8, ps=ps: e.tensor_copy(out=V[:, k8 * 8:(k8 + 1) * 8, 0:64],
                                                                  in_=ps[:, :].rearrange('p (k d) -> p k d', k=8)),
                     reads=[pn], writes=[vname])

        def compress(kv, srcT, srcname, dst, dstname):
            ps, pn = getps()
            v3 = srcT[:, :].rearrange('d (n j) -> d j n', j=32)
            for j in range(32):
                P.op('pe', lambda e, j=j, ps=ps: e.matmul(ps[:, 0:64], lhsT=w1c[:, kv, j, :], rhs=v3[:, j, :],
                                                         start=(j == 0), stop=(j == 31)),
                     reads=['w1c', srcname], writes=[pn])
            P.op('act', lambda e, ps=ps: e.activation(out=hidT[:, :], in_=ps[:, 0:64], func=AF.Silu, bias=cbc[:, kv:kv + 1]),
                 reads=[pn, 'cbc'], writes=['hidT'])
            ps2, pn2 = getps()
            P.op('pe', lambda e, ps2=ps2: e.matmul(ps2[:64, 0:64], lhsT=hidT[:, :], rhs=w2c[:, kv, :], start=True, stop=True),
                 reads=['hidT', 'w2c'], writes=[pn2])
            P.op('dve', lambda e, ps2=ps2: e.tensor_copy(out=dst[:, :], in_=ps2[:64, 0:64]), reads=[pn2], writes=[dstname])

        def attn_T(h, kind, kT, kTname, V, vname, gcol):
            for g in range(4):
                P.dma('sp', H4[g][:, :], bass.AP(tensor=bias_scr.tensor, offset=bias_scr[kind, h * 4 + g, :].offset,
                                                 ap=[[1, 128], [1, WH]]), reads=['bias_scr'], writes=[f'H4{g}'])
            for qc in range(4):
                kt_lo = 0 if kind == 0 else max(0, 4 * qc - 4)
                kt_hi = 4 * qc + 3
                for g in range(4):
                    ai = cC['acc']
                    cC['acc'] = (ai + 1) % 2
                    acc, accn = pacc[ai], f'pacc{ai}'
                    for kt in range(kt_lo, kt_hi + 1):
                        c0 = qc * 512 - kt * 128 + PADC
                        ps, pn = getps()
                        P.op('pe', lambda e, kt=kt, g=g, qc=qc, ps=ps: e.matmul(
                            ps[:, :], lhsT=kT[:, kt * 128:(kt + 1) * 128], rhs=qT4[g][:, qc * 512:(qc + 1) * 512],
                            start=True, stop=False), reads=[kTname, f'qT{g}'], writes=[pn])
                        P.op('pe', lambda e, g=g, c0=c0, ps=ps: e.matmul(
                            ps[:, :], lhsT=ident[:, :], rhs=H4[g][:, c0:c0 + 512], start=False, stop=(kind == 1)),
                            reads=['ident', f'H4{g}'], writes=[pn])
                        if kind == 0:
                            P.op('pe', lambda e, kt=kt, qc=qc, ps=ps: e.matmul(
                                ps[:, :], lhsT=Es[:, kt, :], rhs=nmT[:, qc * 512:(qc + 1) * 512], start=False, stop=True),
                                reads=['Es', 'nmT'], writes=[pn])
                        pi = cC['PT']
                        cC['PT'] = (pi + 1) % 3
                        P.op('act', lambda e, pi=pi, ps=ps: e.activation(out=PT[pi][:, :], in_=ps[:, :], func=AF.Exp),
                             reads=[pn], writes=[f'PT{pi}'])
                        for sub in range(4):
                            P.op('pe', lambda e, sub=sub, pi=pi, kt=kt, acc=acc, kt_lo=kt_lo, kt_hi=kt_hi: e.matmul(
                                acc[:, sub * 65:(sub + 1) * 65], lhsT=PT[pi][:, sub * 128:(sub + 1) * 128], rhs=V[:, kt, :],
                                start=(kt == kt_lo and sub == 0), stop=(kt == kt_hi), skip_group_check=True),
                                reads=[f'PT{pi}', vname], writes=[accn])
                    ri = ai
                    a3 = acc[:, 0:260].rearrange('p (s c) -> p s c', s=4)
                    P.op('dve', lambda e, a3=a3, ri=ri: e.reciprocal(out=rden[ri][:, :].unsqueeze(2), in_=a3[:, :, 64:65]),
                         reads=[accn], writes=[f'rden{ri}'])
                    P.op('dve', lambda e, ri=ri, qc=qc, g=g: e.tensor_tensor(
                        out=rden[ri][:, :], in0=rden[ri][:, :],
                        in1=gts[:, qc * 4:(qc + 1) * 4, (h * 4 + g) * 3 + gcol], op=ALU.mult),
                        reads=[f'rden{ri}', 'gts'], writes=[f'rden{ri}'])
                    for sub in range(4):
                        qt = qc * 4 + sub
                        P.op('dve', lambda e, sub=sub, qt=qt, g=g, ri=ri, acc=acc: e.scalar_tensor_tensor(
                            out=oacc[:, qt, g * 64:(g + 1) * 64], in0=acc[:, sub * 65:sub * 65 + 64],
                            scalar=rden[ri][:, sub:sub + 1], in1=oacc[:, qt, g * 64:(g + 1) * 64],
                            op0=ALU.mult, op1=ALU.add), reads=[accn, f'rden{ri}', 'oacc'], writes=['oacc'])

        for h in range(4):
            for sq_ in range(n_ptiles * 512 // SEQ):
                tb = sq_ * SEQ
                load_tok(7712, 48, gts, 'gts', tb)
                P.op('act', lambda e: e.activation(out=gts[:, :, :], in_=gts[:, :, :], func=AF.Sigmoid), reads=['gts'],
                     writes=['gts'])
                for g in range(4):
                    load_tok(5152 + h * 256 + g * 64, 64, stk[2], 'stk2', tb)
                    transposeT(stk[2], 'stk2', qT4[g], f'qT{g}', False)
                load_tok(6176 + h * 64, 64, stk[0], 'stk0', tb)
                load_tok(6176 + 256 + h * 64, 64, stk[1], 'stk1', tb)
                transposeT(stk[0], 'stk0', kTs, 'kTs', False)
                transposeT(stk[1], 'stk1', kTw, 'kTw', False)
                compress(0, kTs, 'kTs', kcr, 'kcr')
                compress(1, kTw, 'kTw', vcb, 'vcb')
                P.op('pool', lambda e: e.tensor_tensor(out=ksq[:, :], in0=kcr[:, :], in1=kcr[:, :], op=ALU.mult),
                     reads=['kcr'], writes=['ksq'])
                P.op('dve', lambda e: e.tensor_reduce(out=kss[:, 0:1], in_=ksq[:, :], axis=AX.X, op=ALU.add),
                     reads=['ksq'], writes=['kss'])
                P.op('act', lambda e: e.activation(out=kss[:, 1:2], in_=kss[:, 0:1], func=AF.Sqrt, scale=1.0 / 64, bias=EPS),
                     reads=['kss'], writes=['kss'])
                P.op('dve', lambda e: e.reciprocal(out=kss[:, 1:2], in_=kss[:, 1:2]), reads=['kss'], writes=['kss'])
                P.op('dve', lambda e: e.scalar_tensor_tensor(out=kcr[:, :], in0=kcr[:, :], scalar=kss[:, 1:2],
                                                             in1=qkn1[:64, 1, :], op0=ALU.mult, op1=ALU.mult),
                     reads=['kcr', 'kss', 'qkn'], writes=['kcr'])
                ps, pn = getps()
                P.op('pe', lambda e, ps=ps: e.transpose(out=ps[:64, 0:64], in_=kcr[:, :], identity=ident[:64, :64]),
                     reads=['kcr', 'ident'], writes=[pn])
                P.op('dve', lambda e, ps=ps: e.tensor_copy(out=kcT[:, :], in_=ps[:64, 0:64]), reads=[pn], writes=['kcT'])
                load_tok(6688 + h * 64, 64, stk[0], 'stk0', tb)
                load_tok(6688 + 256 + h * 64, 64, stk[1], 'stk1', tb)
                transposeT(stk[0], 'stk0', kTs, 'kTs', True)
                build_V(stk[1], 'stk1', Vs, 'Vs')
                load_tok(7200 + h * 64, 64, stk[0], 'stk0', tb)
                load_tok(7200 + 256 + h * 64, 64, stk[1], 'stk1', tb)
                transposeT(stk[0], 'stk0', kTw, 'kTw', True)
                build_V(stk[1], 'stk1', Vw, 'Vw')
                for qt in range(16):
                    ncol = 4 * qt + 4
                    P.op('pool', lambda e: e.memset(impa[:, :], 0.0), reads=['impa'], writes=['impa'])
                    for g in range(4):
                        hq = h * 4 + g
                        si = cC['Sc']
                        cC['Sc'] = (si + 1) % 2
                        ps, pn = getps()
                        P.op('pe', lambda e, g=g, qt=qt, ncol=ncol, ps=ps: e.matmul(
                            ps[:, 0:ncol], lhsT=qT4[g][:, qt * 128:(qt + 1) * 128], rhs=kcT[:, 0:ncol], start=True, stop=True),
                            reads=[f'qT{g}', 'kcT'], writes=[pn])
                        P.op('dve', lambda e, si=si, hq=hq, qt=qt, ncol=ncol, ps=ps: e.tensor_tensor(
                            out=Sc[si][:, 0:ncol], in0=ps[:, 0:ncol], in1=Gc[:, hq, 60 - 4 * qt:64], op=ALU.add),
                            reads=[pn, f'Gc{hq}'], writes=[f'Sc{si}'])
                        P.op('act', lambda e, si=si, ncol=ncol: e.activation(out=Sc[si][:, 0:ncol], in_=Sc[si][:, 0:ncol],
                                                                             func=AF.Exp, accum_out=ssum[si][:, 0:1]),
                             reads=[f'Sc{si}'], writes=[f'Sc{si}', f'ssum{si}'])
                        P.op('dve', lambda e, si=si: e.tensor_scalar(out=ssum[si][:, :], in0=ssum[si][:, :], scalar1=1e-30,
                                                                     scalar2=None, op0=ALU.max),
                             reads=[f'ssum{si}'], writes=[f'ssum{si}'])
                        P.op('dve', lambda e, si=si: e.reciprocal(out=ssum[si][:, :], in_=ssum[si][:, :]),
                             reads=[f'ssum{si}'], writes=[f'ssum{si}'])
                        P.op('dve', lambda e, si=si, ncol=ncol: e.tensor_scalar(out=Sc[si][:, 0:ncol], in0=Sc[si][:, 0:ncol],
                                                                                scalar1=ssum[si][:, 0:1], scalar2=None,
                                                                                op0=ALU.mult),
                             reads=[f'Sc{si}', f'ssum{si}'], writes=[f'Sc{si}'])
                        P.op('pool', lambda e, si=si, ncol=ncol: e.tensor_tensor(out=impa[:, 0:ncol], in0=impa[:, 0:ncol],
                                                                                 in1=Sc[si][:, 0:ncol], op=ALU.add),
                             reads=['impa', f'Sc{si}'], writes=['impa'])
                        ps2, pn2 = getps()
                        P.op('pe', lambda e, si=si, ncol=ncol, ps2=ps2: e.transpose(out=ps2[:ncol, 0:128], in_=Sc[si][:, 0:ncol],
                                                                                   identity=ident[:, :]),
                             reads=[f'Sc{si}', 'ident'], writes=[pn2])
                        P.op('act', lambda e, si=si, ncol=ncol, ps2=ps2: e.copy(out=pTs[si][:ncol, :], in_=ps2[:ncol, 0:128]),
                             reads=[pn2], writes=[f'pTs{si}'])
                        ps3, pn3 = getps()
                        P.op('pe', lambda e, si=si, ncol=ncol, ps3=ps3: e.matmul(ps3[:, 0:64], lhsT=pTs[si][:ncol, :],
                                                                                rhs=vcb[:ncol, :], start=True, stop=True),
                             reads=[f'pTs{si}', 'vcb'], writes=[pn3])
                        P.op('dve', lambda e, g=g, qt=qt, hq=hq, ps3=ps3: e.tensor_scalar(
                            out=oacc[:, qt, g * 64:(g + 1) * 64], in0=ps3[:, 0:64], scalar1=gts[:, qt, hq * 3:hq * 3 + 1],
                            scalar2=(None if 'c' in branches else 0.0), op0=ALU.mult,
                            **({} if 'c' in branches else {'op1': ALU.mult})), reads=[pn3, 'gts'], writes=['oacc'])
                    iv = impa[:, :].rearrange('p (b two) -> p b two', two=2)
                    P.op('dve', lambda e, iv=iv: e.tensor_tensor(out=scr[:, :].unsqueeze(2), in0=iv[:, :, 0:1], in1=iv[:, :, 1:2],
                                                                 op=ALU.add), reads=['impa'], writes=['scr'])
                    P.op('dve', lambda e, qt=qt: e.tensor_scalar(out=scw[:, :], in0=dblk[:, :], scalar1=float(-qt * 128),
                                                                 scalar2=None, op0=ALU.is_ge), reads=['dblk'], writes=['scw'])
                    P.op('dve', lambda e: e.scalar_tensor_tensor(out=scr[:, :], in0=scr[:, :], scalar=1.0, in1=scw[:, :],
                                                                 op0=ALU.add, op1=ALU.mult), reads=['scr', 'scw'],
                         writes=['scr'])
                    P.op('dve', lambda e: e.tensor_scalar(out=scr[:, :], in0=scr[:, :], scalar1=-1.0, scalar2=None,
                                                          op0=ALU.add), reads=['scr'], writes=['scr'])
                    lo = max(2 * qt - 1, 0)
                    P.op('dve', lambda e, qt=qt, lo=lo: e.memset(scr[0:64, lo:2 * qt], 1e9) if 2 * qt - 1 >= 0
                         else e.memset(scr[0:64, 0:1], 1e9), reads=['scr'], writes=['scr'])
                    P.op('dve', lambda e, qt=qt: e.memset(scr[0:64, 2 * qt:2 * qt + 1], 2e9), reads=['scr'], writes=['scr'])
                    P.op('dve', lambda e, qt=qt: e.memset(scr[64:128, 2 * qt:2 * qt + 1], 1e9), reads=['scr'], writes=['scr'])
                    P.op('dve', lambda e, qt=qt: e.memset(scr[64:128, 2 * qt + 1:2 * qt + 2], 2e9), reads=['scr'],
                         writes=['scr'])
                    P.op('dve', lambda e: e.memset(scr[:, 0:1], 3e9), reads=['scr'], writes=['scr'])
                    P.op('dve', lambda e: e.max(out=m8[:, 0:8], in_=scr[:, :]), reads=['scr'], writes=['m8'])
                    P.op('dve', lambda e: e.match_replace(out=scw[:, :], in_to_replace=m8[:, 0:8], in_values=scr[:, :],
                                                          imm_value=-2.0), reads=['scr', 'm8'], writes=['scw'])
                    P.op('dve', lambda e: e.max(out=m8[:, 8:16], in_=scw[:, :]), reads=['scw'], writes=['m8'])
                    P.op('dve', lambda e: e.tensor_scalar(out=selm[:, :], in0=scr[:, :], scalar1=m8[:, 15:16], scalar2=None,
                                                          op0=ALU.is_ge), reads=['scr', 'm8'], writes=['selm'])
                    ps4, pn4 = getps()
                    P.op('pe', lambda e, ps4=ps4: e.transpose(out=ps4[:32, 0:128], in_=selm[:, :], identity=ident[:, :]),
                         reads=['selm', 'ident'], writes=[pn4])
                    P.op('dve', lambda e, qt=qt, ps4=ps4: e.tensor_scalar(out=nmT[:, qt * 128:(qt + 1) * 128],
                                                                         in0=ps4[:32, 0:128], scalar1=-1.0, scalar2=BIG,
                                                                         op0=ALU.add, op1=ALU.mult),
                         reads=[pn4], writes=['nmT'])
                if 's' in branches:
                    attn_T(h, 0, kTs, 'kTs', Vs, 'Vs', 1)
                if 'w' in branches:
                    attn_T(h, 1, kTw, 'kTw', Vw, 'Vw', 2)
                P.dma('pool', o_scr[tb:tb + SEQ, h * 256:(h + 1) * 256].rearrange('(k p) n -> p k n', p=128),
                      oacc[:, :, :], reads=['oacc'], writes=['o_scr'])
        P.barrier()
        esC.close()

    if not o_from_host:
        phase_C()

    def phase_C2():
        esS = ExitStack()
        P.cur = esS
        I32 = mybir.dt.int32
        pacc = [P.ps('paccS', [128, 512]) for _ in range(2)]
        E2 = P.sb('E2', [2, 128])
        Sel8 = P.sb('Sel8', [32, 8])
        P.dma('sp', E2[:], e2_d, writes=['E2'])
        P.dma('sp', Sel8[:], sel8_d, writes=['Sel8'])
        tabB = P.sb('tabBs', [128, 32, 16])
        delB = P.sb('delBs', [128, 32, 16])
        P.dma('sp', tabB[:], relb_d.partition_broadcast(128), writes=['tabB'])
        P.op('dve', lambda e: e.tensor_tensor(out=delB[:, 1:32, :], in0=tabB[:, 1:32, :], in1=tabB[:, 0:31, :],
                                              op=ALU.subtract), reads=['tabB'], writes=['delB'])
        tab32 = P.sb('tab32', [32, 4, 32])
        del32 = P.sb('del32', [32, 4, 32])
        for h in range(4):
            for g in range(4):
                P.dma('sp', tab32[g * 8:(g + 1) * 8, h, :], tabT_d[h * 4 + g].partition_broadcast(8), writes=['tab32'])
        P.op('dve', lambda e: e.tensor_tensor(out=del32[:, :, 1:32], in0=tab32[:, :, 1:32], in1=tab32[:, :, 0:31],
                                              op=ALU.subtract), reads=['tab32'], writes=['del32'])
        w1c = P.sb('w1cS', [64, 2, 32, 128])
        w2c = P.sb('w2cS', [128, 2, 64])
        peT = P.sb('peTS', [64, 2, 32])
        cbc = P.sb('cbcS', [128, 2])
        for kv in range(2):
            P.dma('sp', w1c[:, kv, :, :], cw1_d[kv].rearrange('j d f -> d j f'), writes=['w1c'])
            P.dma('sp', w2c[:, kv, :], cw2_d[kv], writes=['w2c'])
        P.dma('sp', peT[:], peT_d, writes=['peT'])
        for kv in range(2):
            ps, pn = getps()
            for j in range(32):
                P.op('pe', lambda e, kv=kv, j=j, ps=ps: e.matmul(ps[:, 0:1], lhsT=w1c[:, kv, j, :], rhs=peT[:, kv, j:j + 1],
                                                                start=(j == 0), stop=(j == 31)),
                     reads=['w1c', 'peT'], writes=[pn])
            P.op('dve', lambda e, kv=kv, ps=ps: e.tensor_copy(out=cbc[:, kv:kv + 1], in_=ps[:, 0:1]), reads=[pn],
                 writes=['cbc'])
        Gs32 = P.sb('Gs32', [32, 4, 512])
        FV = max(0, min(127, (16257 - TB[31]) // 128 + 1))
        NPV = 128 - FV
        Bs = P.sb('Bs', [128, 4, NPV + 1, 32])
        Bw = P.sb('Bw', [128, 4, 4, 32])
        Bn = P.sb('Bn', [8, 4, 32])
        Bnw = P.sb('Bnw', [8, 4, 32])
        es0 = ExitStack()
        P.cur = es0
        d8i = P.sb('d8i', [8, 512], I32)
        d8 = P.sb('d8', [8, 512])
        dq = P.sb('dq', [32, 512])
        tq = P.sb('tq', [32, 512])
        dsi = P.sb('dsi', [128, NPV * 8], I32)
        dsf = P.sb('dsf', [128, NPV * 8])
        tsf = P.sb('tsf', [128, NPV * 8])
        dwi = P.sb('dwi', [128, 32], I32)
        dwf = P.sb('dwf', [128, 32])
        twf = P.sb('twf', [128, 32])
        mwf = P.sb('mwf', [128, 32])
        nwf = P.sb('nwf', [128, 32])
        dni = P.sb('dni', [8, 8], I32)
        dnf = P.sb('dnf', [8, 8])
        tnf = P.sb('tnf', [8, 8])
        mnf = P.sb('mnf', [8, 8])
        nnf = P.sb('nnf', [8, 8])
        P.op('pool', lambda e: e.iota(out=d8i[:, :], pattern=[[-32, 512]], base=16353, channel_multiplier=1), writes=['d8i'])
        P.op('dve', lambda e: e.tensor_copy(out=d8[:, :], in_=d8i[:, :]), reads=['d8i'], writes=['d8'])
        for g in range(4):
            P.dma('sp', dq[g * 8:(g + 1) * 8, :], d8[:, :], reads=['d8'], writes=['dq'])
        P.op('pool', lambda e: e.iota(out=dsi[:, :], pattern=[[-128, NPV], [1, 8]], base=16384 - 128 * FV,
                                      channel_multiplier=-1), writes=['dsi'])
        P.op('dve', lambda e: e.tensor_copy(out=dsf[:, :], in_=dsi[:, :]), reads=['dsi'], writes=['dsf'])
        P.op('pool', lambda e: e.iota(out=dwi[:, :], pattern=[[-128, 4], [1, 8]], base=512, channel_multiplier=-1),
             writes=['dwi'])
        P.op('dve', lambda e: e.tensor_copy(out=dwf[:, :], in_=dwi[:, :]), reads=['dwi'], writes=['dwf'])
        P.op('dve', lambda e: e.tensor_scalar(out=mwf[:, :], in0=dwf[:, :], scalar1=512.0, scalar2=None, op0=ALU.is_le),
             reads=['dwf'], writes=['mwf'])
        P.op('dve', lambda e: e.tensor_scalar(out=nwf[:, :], in0=mwf[:, :], scalar1=-1.0, scalar2=BIG, op0=ALU.add,
                                              op1=ALU.mult), reads=['mwf'], writes=['nwf'])
        P.op('pool', lambda e: e.iota(out=dni[:, :], pattern=[[1, 8]], base=0, channel_multiplier=-1), writes=['dni'])
        P.op('dve', lambda e: e.tensor_copy(out=dnf[:, :], in_=dni[:, :]), reads=['dni'], writes=['dnf'])
        P.op('dve', lambda e: e.tensor_scalar(out=mnf[:, :], in0=dnf[:, :], scalar1=0.0, scalar2=None, op0=ALU.is_ge),
             reads=['dnf'], writes=['mnf'])
        P.op('dve', lambda e: e.tensor_scalar(out=nnf[:, :], in0=mnf[:, :], scalar1=-1.0, scalar2=BIG, op0=ALU.add,
                                              op1=ALU.mult), reads=['mnf'], writes=['nnf'])

        def gen(dst, dname, Dap, dn, tmp, tn, t0, dl, sn, e_tmp='pool', e_add='dve'):
            P.op('dve', lambda e: e.tensor_scalar(out=dst, in0=Dap, scalar1=0.0, scalar2=t0, op0=ALU.mult, op1=ALU.add),
                 reads=[dn, sn], writes=[dname])
            for b in range(1, 32):
                P.op(e_tmp, lambda e, b=b: e.tensor_scalar(out=tmp, in0=Dap, scalar1=float(TB[b]), scalar2=dl(b),
                                                           op0=ALU.is_ge, op1=ALU.mult), reads=[dn, sn], writes=[tn])
                P.op(e_add, lambda e: e.tensor_tensor(out=dst, in0=dst, in1=tmp, op=ALU.add), reads=[tn, dname],
                     writes=[dname])

        for h in range(4):
            gen(Gs32[:, h, :], 'Gs32', dq[:, :], 'dq', tq[:, :], 'tq', tab32[:, h, 0:1],
                lambda b, h=h: del32[:, h, b:b + 1], 'del32', 'pool', 'pool')
            for g in range(4):
                hq = h * 4 + g
                P.op('dve', lambda e, h=h, g=g, hq=hq: e.tensor_copy(out=Bs[:, h, 0, g * 8:(g + 1) * 8],
                                                                     in_=tabB[:, 31, hq:hq + 1].to_broadcast([128, 8])),
                     reads=['tabB'], writes=['Bs0'])
                gen(Bs[:, h, 1:, g * 8:(g + 1) * 8], 'Bs', dsf[:, :].rearrange('p (a r) -> p a r', r=8), 'dsf',
                    tsf[:, :].rearrange('p (a r) -> p a r', r=8), 'tsf', tabB[:, 0, hq:hq + 1],
                    lambda b, hq=hq: delB[:, b, hq:hq + 1], 'delB')
                bwv = Bw[:, h, :, g * 8:(g + 1) * 8]
                gen(bwv, 'Bw', dwf[:, :].rearrange('p (a r) -> p a r', r=8), 'dwf',
                    twf[:, :].rearrange('p (a r) -> p a r', r=8), 'twf', tabB[:, 0, hq:hq + 1],
                    lambda b, hq=hq: delB[:, b, hq:hq + 1], 'delB', 'dve', 'dve')
                P.op('dve', lambda e, bwv=bwv: e.tensor_tensor(out=bwv, in0=bwv, in1=mwf[:, :].rearrange('p (a r) -> p a r', r=8),
                                                             op=ALU.mult), reads=['Bw', 'mwf'], writes=['Bw'])
                P.op('dve', lambda e, bwv=bwv: e.tensor_tensor(out=bwv, in0=bwv, in1=nwf[:, :].rearrange('p (a r) -> p a r', r=8),
                                                             op=ALU.add), reads=['Bw', 'nwf'], writes=['Bw'])
                bnv = Bn[:, h, g * 8:(g + 1) * 8]
                gen(bnv, 'Bn', dnf[:, :], 'dnf', tnf[:, :], 'tnf', tabB[:8, 0, hq:hq + 1],
                    lambda b, hq=hq: delB[:8, b, hq:hq + 1], 'delB', 'dve', 'dve')
                P.op('dve', lambda e, bnv=bnv: e.tensor_tensor(out=bnv, in0=bnv, in1=mnf[:, :], op=ALU.mult),
                     reads=['Bn', 'mnf'], writes=['Bn'])
                P.op('dve', lambda e, bnv=bnv: e.tensor_tensor(out=bnv, in0=bnv, in1=nnf[:, :], op=ALU.add),
                     reads=['Bn', 'nnf'], writes=['Bn'])
        P.barrier()
        es0.close()
        P.cur = esS
        kbuf = P.sb('kbuf', [64, 2, 4, 1024])
        hidT = P.sb('hidTS', [128, 2, 4, 512])
        X = [P.sb('Xpg', [128, 512]) for _ in range(3)]
        kcT = P.sb('kcTS', [64, 4, 512])
        vcs = P.sb('vcsS', [128, 4, 4, 64])
        kcn = P.sb('kcnS', [128, 64])
        ksq = P.sb('ksqS', [128, 64])
        kss = P.sb('kssS', [128, 2])
        qrow = P.sb('qrow', [8, 1024])
        qTall = P.sb('qTall', [64, 4, 32])
        gT32 = P.sb('gT32', [32, 4, 4, 3])
        knrow = P.sb('knrow', [8, 2, 512])
        kTn = P.sb('kTn', [64, 2, 4, 8])
        Vn = P.sb('Vn', [8, 2, 4, 65])
        Scs = P.sb('ScsS', [32, 512])
        ssum = P.sb('ssumS', [32, 1])
        pT = P.sb('pTS', [128, 4, 32])
        scr = P.sb('scrS', [8, 256])
        scw = P.sb('scwS', [8, 256])
        m8 = P.sb('m8S', [8, 16])
        selm = P.sb('selmS', [8, 256])
        NM2 = P.sb('NM2', [2, 4, 128, 8])
        ksT = [P.sb('ksTS', [64, 4, 128]) for _ in range(2)]
        PT = [P.sb('PTS', [128, 128]) for _ in range(2)]
        Vaug = [P.sb('VaugS', [128, 4, 65]) for _ in range(2)]
        PTn = P.sb('PTn', [8, 128])
        ofin = P.sb('ofin', [32, 4, 64])
        rden = P.sb('rdenS', [32, 4])
        Wt = P.sb('Wt', [128, 4, 512])
        cS = {'X': 0, 'k': 0}
        for i in range(2):
            P.op('dve', lambda e, i=i: e.memset(Vaug[i][:, :, 64:65], 1.0), writes=[f'Vaug{i}'])
        P.op('dve', lambda e: e.memset(Vn[:, :, :, 64:65], 1.0), writes=['Vn'])

        ptb = P.sb('ptb', [128, 128], I32)
        ptf = P.sb('ptf', [128, 128])
        pgi = P.sb('pgi', [128, 1], I32)
        pgf = P.sb('pgf', [128, 2])
        idxh = P.sb('idxh', [128, 2, 128], I32)
        P.op('pool', lambda e: e.iota(out=pgi[:, :], pattern=[[0, 1]], base=0, channel_multiplier=2), writes=['pgi'])
        P.op('dve', lambda e: e.tensor_copy(out=pgf[:, 0:1], in_=pgi[:, :]), reads=['pgi'], writes=['pgf'])
        P.op('dve', lambda e: e.tensor_scalar(out=pgf[:, 1:2], in0=pgf[:, 0:1], scalar1=1.0, scalar2=None, op0=ALU.add),
             reads=['pgf'], writes=['pgf'])

        def page_index(b):
            P.dma('sp', ptb[:, :], pt_d[b].partition_broadcast(128), writes=['ptb'])
            P.op('dve', lambda e: e.tensor_copy(out=ptf[:, :], in_=ptb[:, :]), reads=['ptb'], writes=['ptf'])
            for half in range(2):
                P.op('dve', lambda e, half=half: e.tensor_scalar(out=idxh[:, half, :], in0=ptf[:, :], scalar1=256.0,
                                                                 scalar2=pgf[:, half:half + 1], op0=ALU.mult, op1=ALU.add),
                     reads=['ptf', 'pgf'], writes=['idxh'])

        def page_dma(dst, dstname, b, pg, c0):
            half = c0 // 512

            def fn(e):
                return e.indirect_dma_start(out=dst, out_offset=None, in_=cache_d[:, :],
                                            in_offset=bass.IndirectOffsetOnAxis(ap=idxh[:, half, pg:pg + 1], axis=0))
            d = P._deps(['idxh'], [dstname])
            i = P.rr['pool']
            P.rr['pool'] = (i + 1) % P.NDS
            key = f'pool_d{i}'
            if key not in P.sem:
                P._mk(key)
            if P.cnt[key] > d.get(key, 0):
                d[key] = P.cnt[key]
            P._commit('pool', d, fn, key, 16, ['idxh'], [dstname])

        def attn_pass(b, nkt, load_fn, bias_fn, mask, kn_idx, acc, accn, gcol):
            for kt in range(nkt):
                xt_, xn_, kc0, vc0 = load_fn(kt)
                ki = cS['k']
                cS['k'] = (ki + 1) % 2
                ps, pn = getps()
                for h in range(4):
                    P.op('pe', lambda e, h=h, ps=ps, xt_=xt_, kc0=kc0: e.transpose(
                        out=ps[:64, h * 128:(h + 1) * 128], in_=xt_[:, kc0 + h * 64:kc0 + (h + 1) * 64], identity=ident[:, :]),
                        reads=[xn_, 'ident'], writes=[pn])
                P.op('dve', lambda e, ki=ki, ps=ps: e.tensor_copy(out=ksT[ki][:, :, :],
                                                                  in_=ps[:64, :].rearrange('p (h k) -> p h k', h=4)),
                     reads=[pn], writes=[f'ksT{ki}'])
                P.op('pool', lambda e, ki=ki, xt_=xt_, vc0=vc0: e.tensor_copy(
                    out=Vaug[ki][:, :, 0:64], in_=xt_[:, vc0:vc0 + 256].rearrange('p (h d) -> p h d', h=4)),
                    reads=[xn_], writes=[f'Vaug{ki}'])
                ps2, pn2 = getps()
                for h in range(4):
                    P.op('pe', lambda e, h=h, ki=ki, ps2=ps2: e.matmul(ps2[:, h * 32:(h + 1) * 32], lhsT=ksT[ki][:, h, :],
                                                                      rhs=qTall[:, h, :], start=(h == 0), stop=False,
                                                                      skip_group_check=True),
                         reads=[f'ksT{ki}', 'qTall'], writes=[pn2])
                bap, bn_ = bias_fn(kt)
                P.op('pe', lambda e, ps2=ps2, bap=bap: e.matmul(ps2[:, 0:128], lhsT=ident[:, :], rhs=bap, start=False,
                                                               stop=(not mask), skip_group_check=True),
                     reads=['ident', bn_], writes=[pn2])
                if mask:
                    P.op('pe', lambda e, ps2=ps2, kt=kt: e.matmul(
                        ps2[:, 0:128], lhsT=E2[:, :], rhs=NM2[:, :, kt, :].unsqueeze(2).to_broadcast([2, 4, 4, 8]),
                        start=False, stop=True, skip_group_check=True), reads=['E2', 'NM2'], writes=[pn2])
                P.op('act', lambda e, ki=ki, ps2=ps2: e.activation(out=PT[ki][:, :], in_=ps2[:, 0:128], func=AF.Exp),
                     reads=[pn2], writes=[f'PT{ki}'])
                for h in range(4):
                    P.op('pe', lambda e, h=h, ki=ki, kt=kt: e.matmul(acc[:32, h * 65:(h + 1) * 65],
                                                                    lhsT=PT[ki][:, h * 32:(h + 1) * 32], rhs=Vaug[ki][:, h, :],
                                                                    start=(kt == 0 and h == 0), stop=False,
                                                                    skip_group_check=True),
                         reads=[f'PT{ki}', f'Vaug{ki}'], writes=[accn])
            ps3, pn3 = getps()
            for h in range(4):
                P.op('pe', lambda e, h=h, ps3=ps3: e.matmul(ps3[:8, h * 32:(h + 1) * 32], lhsT=kTn[:, kn_idx, h, :],
                                                           rhs=qTall[:, h, :], start=(h == 0), stop=False,
                                                           skip_group_check=True),
                     reads=['kTn', 'qTall'], writes=[pn3])
            P.op('pe', lambda e, ps3=ps3: e.matmul(ps3[:8, 0:128], lhsT=ident[:8, :8], rhs=Bn[:, :, :], start=False,
                                                   stop=True, skip_group_check=True),
                 reads=['ident', 'Bn'], writes=[pn3])
            P.op('act', lambda e, ps3=ps3: e.activation(out=PTn[:, :], in_=ps3[:8, 0:128], func=AF.Exp), reads=[pn3],
                 writes=['PTn'])
            for h in range(4):
                P.op('pe', lambda e, h=h: e.matmul(acc[:32, h * 65:(h + 1) * 65], lhsT=PTn[:, h * 32:(h + 1) * 32],
                                                   rhs=Vn[:, kn_idx, h, :], start=False, stop=(h == 3),
                                                   skip_group_check=True),
                     reads=['PTn', 'Vn'], writes=[accn])
            a3 = acc[:32, 0:260].rearrange('p (h c) -> p h c', h=4)
            P.op('dve', lambda e: e.reciprocal(out=rden[:, :].unsqueeze(2), in_=a3[:, :, 64:65]), reads=[accn],
                 writes=['rden'])
            P.op('dve', lambda e: e.tensor_tensor(out=rden[:, :], in0=rden[:, :], in1=gT32[:, :, 0, gcol], op=ALU.mult),
                 reads=['rden', 'gT32'], writes=['rden'])
            for h in range(4):
                P.op('dve', lambda e, h=h: e.scalar_tensor_tensor(out=ofin[:, h, :], in0=acc[:32, h * 65:h * 65 + 64],
                                                                  scalar=rden[:, h:h + 1], in1=ofin[:, h, :],
                                                                  op0=ALU.mult, op1=ALU.add),
                     reads=[accn, 'rden', 'ofin'], writes=['ofin'])

        for b in range(NS_SEQ_PC):
            tok0 = TP + b * DEC
            page_index(b)
            P.dma('sp', qrow[:, :], p_scr[tok0:tok0 + 8, 5152:6176], reads=['p_scr'], writes=['qrow'])
            ps, pn = getps()
            for hq in range(16):
                P.op('pe', lambda e, hq=hq, ps=ps: e.transpose(out=ps[:64, hq * 8:(hq + 1) * 8],
                                                              in_=qrow[:, hq * 64:(hq + 1) * 64], identity=ident[:8, :8]),
                     reads=['qrow', 'ident'], writes=[pn])
            P.op('dve', lambda e, ps=ps: e.tensor_copy(out=qTall[:, :, :], in_=ps[:64, 0:128].rearrange('p (h q) -> p h q', h=4)),
                 reads=[pn], writes=['qTall'])
            gsrc = p_scr[tok0:tok0 + 8, 7712:7760].rearrange('r (h g c) -> r h g c', h=4, g=4)
            for g in range(4):
                P.dma('sp', gT32[g * 8:(g + 1) * 8, :, 0, :], gsrc[:, :, g, :], reads=['p_scr'], writes=['gT32'])
            P.op('act', lambda e: e.activation(out=gT32[:, :, 0, :], in_=gT32[:, :, 0, :], func=AF.Sigmoid), reads=['gT32'],
                 writes=['gT32'])
            P.dma('sp', knrow[:, 0, :], p_scr[tok0:tok0 + 8, 6688:7200], reads=['p_scr'], writes=['knrow'])
            P.dma('sp', knrow[:, 1, :], p_scr[tok0:tok0 + 8, 7200:7712], reads=['p_scr'], writes=['knrow'])
            ps, pn = getps()
            for kn in range(2):
                for h in range(4):
                    P.op('pe', lambda e, kn=kn, h=h, ps=ps: e.transpose(out=ps[:64, (kn * 4 + h) * 8:(kn * 4 + h + 1) * 8],
                                                                       in_=knrow[:, kn, h * 64:(h + 1) * 64],
                                                                       identity=ident[:8, :8]),
                         reads=['knrow', 'ident'], writes=[pn])
            P.op('dve', lambda e, ps=ps: e.tensor_copy(out=kTn[:, :, :, :],
                                                       in_=ps[:64, 0:64].rearrange('p (k h r) -> p k h r', k=2, h=4)),
                 reads=[pn], writes=['kTn'])
            P.op('pool', lambda e: e.tensor_copy(out=Vn[:, :, :, 0:64],
                                                 in_=knrow[:, :, 256:512].rearrange('p k (h d) -> p k h d', h=4)),
                 reads=['knrow'], writes=['Vn'])
            for bt in range(16):
                for sl in range(8):
                    pg = bt * 8 + sl
                    xi = cS['X']
                    cS['X'] = (xi + 1) % 3
                    page_dma(X[xi][:, :], f'X{xi}', b, pg, 0)
                    for kv in range(2):
                        ps, pn = getps()
                        for h in range(4):
                            P.op('pe', lambda e, kv=kv, h=h, xi=xi, ps=ps: e.transpose(
                                out=ps[:64, h * 128:(h + 1) * 128], in_=X[xi][:, kv * 256 + h * 64:kv * 256 + (h + 1) * 64],
                                identity=ident[:, :]), reads=[f'X{xi}', 'ident'], writes=[pn])
                        eng = 'dve' if kv == 0 else 'act'
                        if kv == 0:
                            P.op('dve', lambda e, kv=kv, sl=sl, ps=ps: e.tensor_copy(
                                out=kbuf[:, kv, :, sl * 128:(sl + 1) * 128], in_=ps[:64, :].rearrange('p (h k) -> p h k', h=4)),
                                reads=[pn], writes=[f'kbuf{kv}'])
                        else:
                            P.op('act', lambda e, kv=kv, sl=sl, ps=ps: e.copy(
                                out=kbuf[:, kv, :, sl * 128:(sl + 1) * 128], in_=ps[:64, :].rearrange('p (h k) -> p h k', h=4)),
                                reads=[pn], writes=[f'kbuf{kv}'])
                for kv in range(2):
                    for h in range(4):
                        ps, pn = getps()
                        v3 = kbuf[:, kv, h, :].rearrange('d (n j) -> d j n', j=32)
                        for j in range(32):
                            P.op('pe', lambda e, kv=kv, j=j, ps=ps, v3=v3: e.matmul(ps[:, 0:32], lhsT=w1c[:, kv, j, :],
                                                                                   rhs=v3[:, j, :], start=(j == 0),
                                                                                   stop=(j == 31)),
                                 reads=['w1c', f'kbuf{kv}'], writes=[pn])
                        P.op('act', lambda e, kv=kv, h=h, bt=bt, ps=ps: e.activation(
                            out=hidT[:, kv, h, bt * 32:(bt + 1) * 32], in_=ps[:, 0:32], func=AF.Silu, bias=cbc[:, kv:kv + 1]),
                            reads=[pn, 'cbc'], writes=['hidT'])
            P.op('dve', lambda e: e.memset(ofin[:, :, :], 0.0), reads=['ofin'], writes=['ofin'])
            for h in range(4):
                for ch in range(4):
                    ps, pn = getps()
                    P.op('pe', lambda e, h=h, ch=ch, ps=ps: e.matmul(ps[:, 0:64], lhsT=hidT[:, 0, h, ch * 128:(ch + 1) * 128],
                                                                    rhs=w2c[:, 0, :], start=True, stop=True),
                         reads=['hidT', 'w2c'], writes=[pn])
                    P.op('dve', lambda e, ps=ps: e.tensor_copy(out=kcn[:, :], in_=ps[:, 0:64]), reads=[pn], writes=['kcn'])
                    P.op('pool', lambda e: e.tensor_tensor(out=ksq[:, :], in0=kcn[:, :], in1=kcn[:, :], op=ALU.mult),
                         reads=['kcn'], writes=['ksq'])
                    P.op('dve', lambda e: e.tensor_reduce(out=kss[:, 0:1], in_=ksq[:, :], axis=AX.X, op=ALU.add),
                         reads=['ksq'], writes=['kss'])
                    P.op('act', lambda e: e.activation(out=kss[:, 1:2], in_=kss[:, 0:1], func=AF.Sqrt, scale=1.0 / 64,
                                                       bias=EPS), reads=['kss'], writes=['kss'])
                    P.op('dve', lambda e: e.reciprocal(out=kss[:, 1:2], in_=kss[:, 1:2]), reads=['kss'], writes=['kss'])
                    P.op('dve', lambda e: e.scalar_tensor_tensor(out=kcn[:, :], in0=kcn[:, :], scalar=kss[:, 1:2],
                                                                 in1=qkn[:, 1, :], op0=ALU.mult, op1=ALU.mult),
                         reads=['kcn', 'kss', 'qkn'], writes=['kcn'])
                    ps2, pn2 = getps()
                    P.op('pe', lambda e, ps2=ps2: e.transpose(out=ps2[:64, 0:128], in_=kcn[:, :], identity=ident[:, :]),
                         reads=['kcn', 'ident'], writes=[pn2])
                    P.op('dve', lambda e, h=h, ch=ch, ps2=ps2: e.tensor_copy(out=kcT[:, h, ch * 128:(ch + 1) * 128],
                                                                            in_=ps2[:64, 0:128]), reads=[pn2],
                         writes=['kcT'])
                    ps3, pn3 = getps()
                    P.op('pe', lambda e, h=h, ch=ch, ps3=ps3: e.matmul(ps3[:, 0:64], lhsT=hidT[:, 1, h, ch * 128:(ch + 1) * 128],
                                                                      rhs=w2c[:, 1, :], start=True, stop=True),
                         reads=['hidT', 'w2c'], writes=[pn3])
                    P.op('act', lambda e, h=h, ch=ch, ps3=ps3: e.copy(out=vcs[:, h, ch, :], in_=ps3[:, 0:64]), reads=[pn3],
                         writes=['vcs'])
                ps, pn = getps()
                P.op('pe', lambda e, h=h, ps=ps: e.matmul(ps[:32, :], lhsT=qTall[:, h, :], rhs=kcT[:, h, :], start=True,
                                                         stop=True), reads=['qTall', 'kcT'], writes=[pn])
                P.op('dve', lambda e, h=h, ps=ps: e.tensor_tensor(out=Scs[:, :], in0=ps[:32, :], in1=Gs32[:, h, :], op=ALU.add),
                     reads=[pn, 'Gs32'], writes=['Scs'])
                P.op('act', lambda e: e.activation(out=Scs[:, :], in_=Scs[:, :], func=AF.Exp, accum_out=ssum[:, 0:1]),
                     reads=['Scs'], writes=['Scs', 'ssum'])
                P.op('dve', lambda e: e.reciprocal(out=ssum[:, :], in_=ssum[:, :]), reads=['ssum'], writes=['ssum'])
                P.op('dve', lambda e: e.tensor_scalar(out=Scs[:, :], in0=Scs[:, :], scalar1=ssum[:, 0:1], scalar2=None,
                                                      op0=ALU.mult), reads=['Scs', 'ssum'], writes=['Scs'])
                ps, pn = getps()
                P.op('pe', lambda e, ps=ps: e.matmul(ps[:8, :], lhsT=Sel8[:, :], rhs=Scs[:, :], start=True, stop=True),
                     reads=['Sel8', 'Scs'], writes=[pn])
                P.op('dve', lambda e, ps=ps: e.tensor_copy(out=scw[:, :], in_=ps[:8, 0:256]), reads=[pn], writes=['scw'])
                iv = ps[:8, :].rearrange('p (b two) -> p b two', two=2)
                P.op('dve', lambda e, iv=iv: e.tensor_copy(out=scw[:, :].unsqueeze(2), in_=iv[:, :, 0:1]), reads=[pn],
                     writes=['scw'])
                P.op('dve', lambda e, iv=iv: e.tensor_tensor(out=scr[:, :].unsqueeze(2), in0=scw[:, :].unsqueeze(2),
                                                             in1=iv[:, :, 1:2], op=ALU.add), reads=[pn, 'scw'],
                     writes=['scr'])
                P.op('dve', lambda e: e.memset(scr[:, 255:256], 2e9), reads=['scr'], writes=['scr'])
                P.op('dve', lambda e: e.memset(scr[:, 0:1], 3e9), reads=['scr'], writes=['scr'])
                P.op('dve', lambda e: e.max(out=m8[:, 0:8], in_=scr[:, :]), reads=['scr'], writes=['m8'])
                P.op('dve', lambda e: e.match_replace(out=scw[:, :], in_to_replace=m8[:, 0:8], in_values=scr[:, :],
                                                      imm_value=-2.0), reads=['scr', 'm8'], writes=['scw'])
                P.op('dve', lambda e: e.max(out=m8[:, 8:16], in_=scw[:, :]), reads=['scw'], writes=['m8'])
                P.op('dve', lambda e: e.tensor_scalar(out=selm[:, :], in0=scr[:, :], scalar1=m8[:, 14:15], scalar2=None,
                                                      op0=ALU.is_ge), reads=['scr', 'm8'], writes=['selm'])
                P.op('dve', lambda e: e.tensor_scalar(out=selm[:, :], in0=selm[:, :], scalar1=-1.0, scalar2=BIG, op0=ALU.add,
                                                      op1=ALU.mult), reads=['selm'], writes=['selm'])
                P.dma('sp', nm_scr[b, h, :, :], selm[:, :], reads=['selm'], writes=['nm_scr'])
                for r_ in range(8):
                    P.dma('sp', NM2[:, h, :, r_], nm_scr[b, h, r_, :].rearrange('(pg par) -> par pg', par=2),
                          reads=['nm_scr'], writes=['NM2'], allow_slow_non_contiguous=True)
                ps, pn = getps()
                for ch in range(4):
                    P.op('pe', lambda e, ch=ch, ps=ps: e.transpose(out=ps[:, ch * 32:(ch + 1) * 32],
                                                                  in_=Scs[:, ch * 128:(ch + 1) * 128], identity=ident[:32, :32]),
                         reads=['Scs', 'ident'], writes=[pn])
                P.op('act', lambda e, ps=ps: e.copy(out=pT[:, :, :], in_=ps[:, 0:128].rearrange('p (c q) -> p c q', c=4)),
                     reads=[pn], writes=['pT'])
                ps2, pn2 = getps()
                for ch in range(4):
                    P.op('pe', lambda e, h=h, ch=ch, ps2=ps2: e.matmul(ps2[:32, 0:64], lhsT=pT[:, ch, :], rhs=vcs[:, h, ch, :],
                                                                      start=(ch == 0), stop=(ch == 3)),
                         reads=['pT', 'vcs'], writes=[pn2])
                if 'c' in sbranches:
                    P.op('dve', lambda e, h=h, ps2=ps2: e.tensor_scalar(out=ofin[:, h, :], in0=ps2[:32, 0:64],
                                                                       scalar1=gT32[:, h, 0, 0:1], scalar2=None, op0=ALU.mult),
                         reads=[pn2, 'gT32'], writes=['ofin'])
            if 's' in sbranches:
                def load_sel(kt, b=b):
                    xi = cS['X']
                    cS['X'] = (xi + 1) % 3
                    page_dma(X[xi][:, :], f'X{xi}', b, kt, 512)
                    return X[xi], f'X{xi}', 0, 256
                attn_pass(b, 128, load_sel, lambda kt: (Bs[:, :, (0 if kt < FV else kt - FV + 1), :], 'Bs'), True, 0,
                          pacc[0], 'paccS0', 1)
            if 'w' in sbranches:
                P.dma('sp', Wt[:, :, :], cwin[b].rearrange('(k p) c -> p k c', p=128), writes=['Wt'])
                attn_pass(b, 4, lambda kt: (Wt[:, kt, :], 'Wt', 0, 256), lambda kt: (Bw[:, :, kt, :], 'Bw'), False, 1,
                          pacc[1], 'paccS1', 2)
            osrc = o_scr[tok0:tok0 + 8, :].rearrange('r (h g d) -> r h g d', h=4, g=4)
            for g in range(4):
                P.dma('pool', osrc[:, :, g, :], ofin[g * 8:(g + 1) * 8, :, :], reads=['ofin'], writes=['o_scr'])
        P.barrier()
        esS.close()

    if not o_from_host and do_sample:
        phase_C2()

    def phase_D():
        esD = ExitStack()
        P.cur = esD
        g2 = P.sb('g2', [128, 8])
        P.dma('sp', g2[:], g2_d, writes=['g2'])
        big = P.sb('bigD', [128, 22, 512], BF16)
        oT = P.sb('oTD', [128, 8, 512], BF16)
        mixT = P.sb('mixTD', [128, 8, 512], BF16)
        xT_ = P.sb('xTD', [128, 8, 512])
        xnT_ = P.sb('xnTD', [128, 8, 512], BF16)
        rstd_ = P.sb('rstdD', [128, 512])
        sg_ = [P.sb('sgD', [128, 512]) for _ in range(2)]
        wblk_ = [P.sb('wblkD', [128, 2, 8, 128]) for _ in range(NWB)]
        wblkb_ = [P.sb('wblkbD', [128, 2, 8, 128], BF16) for _ in range(NWB)]
        woblk_ = [P.sb('woblkD', [128, 22, 128]) for _ in range(2)]
        woblkb_ = [P.sb('woblkbD', [128, 22, 128], BF16) for _ in range(2)]
        stg = [P.sb('stgD', [128, 4, 128]) for _ in range(3)]
        wsq = [P.sb('wsqD', [128, 16, 128]) for _ in range(2)]
        wsqb = [P.sb('wsqbD', [128, 16, 128], BF16) for _ in range(2)]
        gA = P.sb('gAD', [128, 512])
        gB = P.sb('gBD', [128, 512])
        ya = P.sb('yaD', [128, 512])
        yout = P.sb('youtD', [128, 4, D])
        TD = dict(xT=xT_, xnT=xnT_, hT=big, sg=sg_, wblk=wblk_, woblk=woblk_, rstd=rstd_, wblkb=wblkb_, woblkb=woblkb_,
                  sq=yout[:, :, :].rearrange('p s (c t) -> p (s c) t', t=512), sqn='yout')
        cD = {'stg': 0, 'wsq': 0}

        def load_T(src_cols, rows, ns, dst, dstname, act_func=None):
            TT = ns * rows
            i = cD['stg']
            cD['stg'] = (i + 1) % 3
            st, stn = stg[i], f'stgD{i}'
            P.dma('sp', st[:rows, :ns, :], src_cols.rearrange('(s p) n -> p s n', p=rows), writes=[stn])
            ps, pn = getps()
            for s_ in range(ns):
                P.op('pe', lambda e, s_=s_, ps=ps, st=st: e.transpose(out=ps[:, s_ * rows:(s_ + 1) * rows],
                                                                      in_=st[:rows, s_, :], identity=ident[:rows, :rows]),
                     reads=[stn, 'ident'], writes=[pn])
            if act_func is None:
                P.op('dve', lambda e, ps=ps: e.tensor_copy(out=dst[:, :TT], in_=ps[:, :TT]), reads=[pn], writes=[dstname])
            else:
                P.op('act', lambda e, ps=ps: e.activation(out=dst[:, :TT], in_=ps[:, :TT], func=act_func),
                     reads=[pn], writes=[dstname])

        def tile_D(tok0, rows, ns, y_dst):
            TT = ns * rows
            for k in range(16):
                load_T(ys_scr[tok0:tok0 + TT, k * 128:(k + 1) * 128], rows, ns, big[:, k, :], f'hT{k}')
            for k in range(8):
                load_T(o_scr[tok0:tok0 + TT, k * 128:(k + 1) * 128], rows, ns, oT[:, k, :], f'oT{k}')
            P.dma('sp', xT_[:, :, :TT], x1_scr[:, :, tok0:tok0 + TT].rearrange('c p t -> p c t'), reads=['x1_scr'],
                  writes=['xT'])
            for n in range(8):
                i = cD['wsq']
                cD['wsq'] = (i + 1) % 2
                w_, wn_ = wsq[i], f'wsqD{i}'
                P.dma('sp', w_[:, :, :], wbs_d[:, n * 128:(n + 1) * 128].rearrange('(k p) n -> p k n', p=128),
                      writes=[wn_])
                wb_ = wsqb[i]
                P.op('pool', lambda e, w_=w_, wb_=wb_: e.tensor_copy(out=wb_[:, :, :], in_=w_[:, :, :]), reads=[wn_],
                     writes=[wn_ + 'b'])
                pa_, pan_ = getps()
                for k in range(16):
                    P.op('pe', lambda e, k=k, pa_=pa_, w_=wb_: e.matmul(pa_[:, :TT], lhsT=w_[:, k, :], rhs=big[:, k, :TT],
                                                                     start=(k == 0), stop=(k == 15)),
                         reads=[wn_ + 'b', f'hT{k}'], writes=[pan_])
                load_T(p_scr[tok0:tok0 + TT, 7760 + n * 128:7760 + (n + 1) * 128], rows, ns, gA, 'gA', AF.Sigmoid)
                load_T(p_scr[tok0:tok0 + TT, 8784 + n * 128:8784 + (n + 1) * 128], rows, ns, gB, 'gB', AF.Sigmoid)
                P.op('dve', lambda e, pa_=pa_: e.tensor_tensor(out=ya[:, :TT], in0=pa_[:, :TT], in1=gA[:, :TT], op=ALU.mult),
                     reads=[pan_, 'gA'], writes=['ya'])
                i = cD['wsq']
                cD['wsq'] = (i + 1) % 2
                w2_, wn2_ = wsq[i], f'wsqD{i}'
                P.dma('sp', w2_[:, 0:8, :], wba_d[:, n * 128:(n + 1) * 128].rearrange('(k p) n -> p k n', p=128),
                      writes=[wn2_])
                wb2_ = wsqb[i]
                P.op('pool', lambda e, w2_=w2_, wb2_=wb2_: e.tensor_copy(out=wb2_[:, 0:8, :], in_=w2_[:, 0:8, :]), reads=[wn2_],
                     writes=[wn2_ + 'b'])
                pb2, pbn2 = getps()
                for k in range(8):
                    P.op('pe', lambda e, k=k, pb2=pb2, w2_=wb2_: e.matmul(pb2[:, :TT], lhsT=w2_[:, k, :], rhs=oT[:, k, :TT],
                                                                       start=(k == 0), stop=(k == 7)),
                         reads=[wn2_ + 'b', f'oT{k}'], writes=[pbn2])
                P.op('dve', lambda e, pb2=pb2: e.tensor_tensor(out=gB[:, :TT], in0=pb2[:, :TT], in1=gB[:, :TT], op=ALU.mult),
                     reads=[pbn2, 'gB'], writes=['gB'])
                P.op('pool', lambda e, n=n: e.tensor_tensor(out=mixT[:, n, :TT], in0=ya[:, :TT], in1=gB[:, :TT], op=ALU.add),
                     reads=['ya', 'gB'], writes=[f'mixT{n}'])
            for n in range(8):
                i = cD['wsq']
                cD['wsq'] = (i + 1) % 2
                w_, wn_ = wsq[i], f'wsqD{i}'
                P.dma('sp', w_[:, 0:8, :], wout_d[:, n * 128:(n + 1) * 128].rearrange('(k p) n -> p k n', p=128),
                      writes=[wn_])
                wb_ = wsqb[i]
                P.op('pool', lambda e, w_=w_, wb_=wb_: e.tensor_copy(out=wb_[:, 0:8, :], in_=w_[:, 0:8, :]), reads=[wn_],
                     writes=[wn_ + 'b'])
                pm, pmn = getps()
                for k in range(8):
                    P.op('pe', lambda e, k=k, pm=pm, w_=wb_: e.matmul(pm[:, :TT], lhsT=w_[:, k, :], rhs=mixT[:, k, :TT],
                                                                   start=(k == 0), stop=(k == 7)),
                         reads=[wn_ + 'b', f'mixT{k}'], writes=[pmn])
                P.op('dve', lambda e, n=n, pm=pm: e.tensor_tensor(out=xT_[:, n, :TT], in0=xT_[:, n, :TT], in1=pm[:, :TT],
                                                                 op=ALU.add),
                     reads=[pmn, 'xT'], writes=['xT'])
            ffn(TD, w2i, w2o, g2, 'g2', TT)
            for s_ in range(ns):
                for hlf in range(2):
                    ps, pn = getps()
                    for c4 in range(4):
                        c = hlf * 4 + c4
                        P.op('pe', lambda e, c=c, c4=c4, s_=s_, ps=ps: e.transpose(
                            out=ps[:rows, c4 * 128:(c4 + 1) * 128], in_=xT_[:, c, s_ * rows:(s_ + 1) * rows],
                            identity=ident[:, :]),
                            reads=['xT', 'ident'], writes=[pn])
                    if hlf == 0:
                        P.op('act', lambda e, s_=s_, ps=ps: e.copy(out=yout[:rows, s_, 0:512], in_=ps[:rows, :]),
                             reads=[pn], writes=['yout'])
                    else:
                        P.op('dve', lambda e, s_=s_, ps=ps: e.tensor_copy(out=yout[:rows, s_, 512:1024], in_=ps[:rows, :]),
                             reads=[pn], writes=['yout'])
            P.dma('pool', y_dst.rearrange('(s p) d -> p s d', p=rows), yout[:rows, :ns, :], reads=['yout'], writes=['o_y'])

        if do_sample:
            tile_D(TP, TS, 1, o_ys)
        for it in range(n_ptiles):
            tile_D(it * 512, 128, 4, o_yp[it * 512:(it + 1) * 512, :])
        P.barrier()
        esD.close()

    phase_D()

    P.finish()
    P.emit()
    es.close()
    return nc


def _gl(g):
    return np.ascontiguousarray(np.asarray(g, np.float32).reshape(8, 128).T)


def _e2():
    E = np.zeros((2, 128), np.float32)
    E[0, 0:64] = 1.0
    E[1, 64:128] = 1.0
    return E


def _esel():
    E = np.zeros((32, 16, 128), np.float32)
    for kt in range(16):
        E[2 * kt, kt, 64:128] = 1.0
        E[2 * kt + 1, kt, 0:64] = 1.0
    return E


def make_in_maps(inp, cores):
    ident = np.eye(128, dtype=np.float32)
    qkn = np.ascontiguousarray(np.broadcast_to(inp['qk_norm'][0].reshape(1, 256), (128, 256))).astype(np.float32)
    maps = []
    for j in cores:
        maps.append({
            'xp': np.ascontiguousarray(inp['x_prompt'][2 * j:2 * j + 2].reshape(TP, D)),
            'xs': np.ascontiguousarray(inp['x_sample'][4 * j:4 * j + 4].reshape(TS, D)),
            'ident': ident,
            'g_ffn1': _gl(inp['ffn1_norm'][0]),
            'g_mix': _gl(inp['mix_norm'][0]),
            'qkn': qkn,
            'ffn1_w_in': np.ascontiguousarray(inp['ffn1_w_in'][0]),
            'ffn1_w_out': np.ascontiguousarray(inp['ffn1_w_out'][0]),
            'w_in_proj': np.ascontiguousarray(inp['w_in_proj'][0]),
            'cache_win': np.ascontiguousarray(inp['cache_win'][0, 4 * j:4 * j + 4].reshape(4, 512, 512)),
            'tri': np.triu(np.ones((128, 128), np.float32)),
            'conv_w': np.ascontiguousarray(inp['conv_w'][0]),
            'conv_b': np.ascontiguousarray(inp['conv_b'][0]),
            'dt_bias': np.ascontiguousarray(inp['dt_bias'][0]),
            'a_log': np.ascontiguousarray(inp['a_log'][0]),
            'd_skip': np.ascontiguousarray(inp['d_skip'][0]),
            'ssm_norm': np.ascontiguousarray(inp['ssm_norm'][0]),
            'state_ssm': np.ascontiguousarray(inp['state_ssm'][0, 4 * j:4 * j + 4].reshape(4, 2048, 128)),
            'state_conv': np.ascontiguousarray(inp['state_conv'][0, 4 * j:4 * j + 4]),
            'w_branch_ssm': np.ascontiguousarray(inp['w_branch_ssm'][0]),
            'tabT': np.ascontiguousarray(inp['rel_bias'].T),
            'cache_kv': inp['cache_kv'].reshape(-1, 512),
            'page_table': np.ascontiguousarray(inp['page_table'][4 * j:4 * j + 4]).astype(np.int32),
            'Sel8': np.ascontiguousarray(np.tile(np.eye(8, dtype=np.float32), (4, 1))),
            'E2': _e2(),
            'rel_bias': np.ascontiguousarray(inp['rel_bias']),
            'Jrev': np.ascontiguousarray(np.eye(128, dtype=np.float32)[::-1]),
            'Esel': _esel(),
            'cmp_peT': np.ascontiguousarray(np.transpose(inp['cmp_pe'][0], (2, 0, 1))),
            'cmp_w1': np.ascontiguousarray(inp['cmp_w1'][0]),
            'cmp_w2': np.ascontiguousarray(inp['cmp_w2'][0]),
            'w_branch_attn': np.ascontiguousarray(inp['w_branch_attn'][0]),
            'w_out': np.ascontiguousarray(inp['w_out'][0]),
            'g_ffn2': _gl(inp['ffn2_norm'][0]),
            'ffn2_w_in': np.ascontiguousarray(inp['ffn2_w_in'][0]),
            'ffn2_w_out': np.ascontiguousarray(inp['ffn2_w_out'][0]),
        })
    return maps


def assemble(results, n):
    f32 = np.float32
    y_p = np.zeros((16, SEQ, D), f32)
    y_s = np.zeros((32, DEC, D), f32)
    kv_p = np.zeros((1, 16, SEQ, 4, 4, 64), f32)
    win_p = np.zeros((1, 16, 512, 2, 4, 64), f32)
    ssm_p = np.zeros((1, 16, 32, 64, 128), f32)
    conv_p = np.zeros((1, 16, 3, 3072), f32)
    kv_s = np.zeros((1, 32, DEC, 4, 4, 64), f32)
    win_s = np.zeros((1, 32, 512, 2, 4, 64), f32)
    ssm_s = np.zeros((1, 32, 32, 64, 128), f32)
    conv_s = np.zeros((1, 32, 3, 3072), f32)
    for j in range(n):
        r = results[j]
        kv_p[0, 2 * j:2 * j + 2] = r['o_kv_p'].reshape(2, SEQ, 4, 4, 64)
        win_p[0, 2 * j:2 * j + 2] = r['o_win_p'].reshape(2, 512, 2, 4, 64)
        conv_p[0, 2 * j:2 * j + 2] = r['o_conv_p'].reshape(2, 3, 3072)
        kv_s[0, 4 * j:4 * j + 4] = r['o_kv_s'].reshape(4, DEC, 4, 4, 64)
        win_s[0, 4 * j:4 * j + 4] = r['o_win_s'].reshape(4, 512, 2, 4, 64)
        conv_s[0, 4 * j:4 * j + 4] = r['o_conv_s'].reshape(4, 3, 3072)
        ssm_p[0, 2 * j:2 * j + 2] = r['o_ssm_p'].reshape(2, 32, 64, 128)
        y_p[2 * j:2 * j + 2] = r['o_y_p'].reshape(2, SEQ, D)
        y_s[4 * j:4 * j + 4] = r['o_y_s'].reshape(4, DEC, D)
        ssm_s[0, 4 * j:4 * j + 4] = r['o_ssm_s'].reshape(4, 32, 64, 128)
    return (y_p, y_s, kv_p, win_p, ssm_p, conv_p, kv_s, win_s, ssm_s, conv_s)


def kernel(**inp):
    inp = {k: np.asarray(v) for k, v in inp.items()}
    nc = build()
    maps = make_in_maps(inp, range(N_CORES))
    res = run_bass_kernel_spmd(nc, maps, core_ids=list(range(N_CORES)))
    return assemble(res.results, N_CORES)
```
